# Optimizing a Trainium2 kernel written in Bass

```python
import jax
import jax.numpy as jnp
from jax import lax
import numpy as np

D_MODEL = 2048
BATCH = 4
SEQ = 2048
DEPTH = 1
DEC_BATCH = 128
DEC_SEQ = 1
PAST_LEN = 16384
PAGE_SIZE = 128

MIX_WIDTH = D_MODEL
POOL_WIDTH = MIX_WIDTH // 2
RWKV_WIDTH = MIX_WIDTH - POOL_WIDTH
POOL_WINDOWS = (2, 4, 8, 16)
N_POOL_GROUPS = len(POOL_WINDOWS)
POOL_GROUP = POOL_WIDTH // N_POOL_GROUPS
POOL_HIST = max(POOL_WINDOWS) - 1
HEAD_DIM = 64
N_HEADS = RWKV_WIDTH // HEAD_DIM
DECAY_RANK = 64
AAA_RANK = 64
GATE_RANK = 160
SHIFT_WIDTH = 3 * RWKV_WIDTH + DECAY_RANK + AAA_RANK + GATE_RANK
PROJ_WIDTH = POOL_WIDTH + SHIFT_WIDTH
D_FF = ((8 * D_MODEL + 3 * 256 - 1) // (3 * 256)) * 256
RMS_EPS = 1e-6
GN_EPS = 64e-5
NORM_EPS_SQ = 1e-24

kernel_name = "hymba_pool_rwkv7_step"


def rmsnorm(x, g):
    x32 = x.astype(jnp.float32)
    y = x32 * lax.rsqrt(jnp.mean(x32 * x32, axis=-1, keepdims=True) + RMS_EPS)
    return (y * g.astype(jnp.float32)).astype(x.dtype)


def pool_mixer(u, prev, start, w_pool, pool_scale):
    B, T, _ = u.shape
    full = jnp.concatenate([prev.astype(u.dtype), u], axis=1)
    cs = jnp.pad(jnp.cumsum(full.astype(jnp.float32), axis=1), ((0, 0), (1, 0), (0, 0)))
    pos = (start + jnp.arange(T)).astype(jnp.float32)
    end = POOL_HIST + 1
    means = []
    for g, w in enumerate(POOL_WINDOWS):
        c = slice(g * POOL_GROUP, (g + 1) * POOL_GROUP)
        s = cs[:, end:end + T, c] - cs[:, end - w:end - w + T, c]
        cnt = jnp.minimum(pos + 1.0, float(w))
        means.append(s / cnt[None, :, None])
    pooled = jnp.concatenate(means, axis=-1) - u.astype(jnp.float32)
    pooled = pooled.reshape(B, T, N_POOL_GROUPS, POOL_GROUP)
    out = jnp.einsum('btgc,gcd->btgd', pooled, w_pool.astype(jnp.float32))
    out = out.reshape(B, T, POOL_WIDTH) * pool_scale.astype(jnp.float32)
    return out, full[:, -POOL_HIST:]


def wkv_scan(state0, r, decay, k, v, aa, bb):
    def step(S, inp):
        r_t, w_t, k_t, v_t, a_t, b_t = inp
        Sa = jnp.einsum('bhvk,bhk->bhv', S, a_t)
        S = S * w_t[:, :, None, :] + Sa[..., None] * b_t[:, :, None, :] + v_t[..., None] * k_t[:, :, None, :]
        y = jnp.einsum('bhvk,bhk->bhv', S, r_t)
        return S, y
    xs = (jnp.moveaxis(r, 1, 0), jnp.moveaxis(decay, 1, 0), jnp.moveaxis(k, 1, 0),
          jnp.moveaxis(v, 1, 0), jnp.moveaxis(aa, 1, 0), jnp.moveaxis(bb, 1, 0))
    S, ys = lax.scan(step, state0, xs)
    return jnp.moveaxis(ys, 0, 1), S


def rwkv7_mixer(p, shift_prev, wkv_prev, mu_shift, w0, w2, a0, a2, g2, k_k, k_a, r_k, gn_w, gn_b):
    B, T, _ = p.shape
    C = RWKV_WIDTH
    f32 = jnp.float32
    p32 = p.astype(f32)
    p_prev = jnp.concatenate([shift_prev.astype(f32)[:, None, :], p32[:, :-1]], axis=1)
    pm = p32 + (p_prev - p32) * mu_shift.astype(f32)
    r = pm[..., :C]
    k = pm[..., C:2 * C]
    v = pm[..., 2 * C:3 * C]
    o = 3 * C
    wd = pm[..., o:o + DECAY_RANK]
    o = o + DECAY_RANK
    ad = pm[..., o:o + AAA_RANK]
    o = o + AAA_RANK
    gd = pm[..., o:o + GATE_RANK]
    w = -jax.nn.softplus(-(w0.astype(f32) + jnp.tanh(wd) @ w2.astype(f32))) - 0.5
    decay = jnp.exp(-jnp.exp(w))
    a = jax.nn.sigmoid(a0.astype(f32) + ad @ a2.astype(f32))
    gate = jax.nn.sigmoid(gd) @ g2.astype(f32)
    kk = (k * k_k.astype(f32)).reshape(B, T, N_HEADS, HEAD_DIM)
    kk = kk * lax.rsqrt(jnp.maximum(jnp.sum(kk * kk, axis=-1, keepdims=True), NORM_EPS_SQ))
    k = k * (1.0 + (a - 1.0) * k_a.astype(f32))
    rh = r.reshape(B, T, N_HEADS, HEAD_DIM)
    kh = k.reshape(B, T, N_HEADS, HEAD_DIM)
    vh = v.reshape(B, T, N_HEADS, HEAD_DIM)
    ah = a.reshape(B, T, N_HEADS, HEAD_DIM)
    dh = decay.reshape(B, T, N_HEADS, HEAD_DIM)
    y, wkv_new = wkv_scan(wkv_prev.astype(f32), rh, dh, kh, vh, -kk, kk * ah)
    mu = jnp.mean(y, axis=-1, keepdims=True)
    var = jnp.mean(jnp.square(y - mu), axis=-1, keepdims=True)
    yn = ((y - mu) * lax.rsqrt(var + GN_EPS)).reshape(B, T, C) * gn_w.astype(f32) + gn_b.astype(f32)
    bonus = jnp.sum(rh * kh * r_k.astype(f32), axis=-1, keepdims=True) * vh
    out = (yn + bonus.reshape(B, T, C)) * gate
    return out, p[:, -1], wkv_new


def mixer_block(x, pool_prev, shift_prev, wkv_prev, start, norm_mix, w_in, w_pool, pool_scale,
                mu_shift, w0, w2, a0, a2, g2, k_k, k_a, r_k, gn_w, gn_b, w_out):
    h = rmsnorm(x, norm_mix)
    proj = jnp.einsum('btd,de->bte', h, w_in)
    pool_out, pool_new = pool_mixer(proj[..., :POOL_WIDTH], pool_prev, start, w_pool, pool_scale)
    rwkv_out, shift_new, wkv_new = rwkv7_mixer(proj[..., POOL_WIDTH:], shift_prev, wkv_prev, mu_shift,
                                               w0, w2, a0, a2, g2, k_k, k_a, r_k, gn_w, gn_b)
    mix = jnp.concatenate([pool_out, rwkv_out], axis=-1).astype(x.dtype)
    x = x + jnp.einsum('btc,cd->btd', mix, w_out)
    return x, pool_new, shift_new, wkv_new.astype(wkv_prev.dtype)


def swiglu_ffn(x, norm_ffn, w_gate, w_up, w_down):
    h = rmsnorm(x, norm_ffn)
    u = jax.nn.silu(h @ w_gate) * (h @ w_up)
    return x + u @ w_down


def setup_inputs(seed: int = 0) -> dict:
    key = jax.random.key(seed)
    ks = list(jax.random.split(key, 32))

    def nrm(i, shape, scale):
        return scale * jax.random.normal(ks[i], shape, jnp.float32)

    C = RWKV_WIDTH
    n = jnp.arange(C, dtype=jnp.float32) / (C - 1)
    w0_base = -6.0 + 5.0 * n ** 0.85 + 0.5
    return {
        "x_prompt": nrm(0, (BATCH, SEQ, D_MODEL), 1.0),
        "x_sample": nrm(1, (DEC_BATCH, DEC_SEQ, D_MODEL), 1.0),
        "state_pool": nrm(2, (DEPTH, DEC_BATCH, POOL_HIST, POOL_WIDTH), 1.0),
        "state_shift": nrm(3, (DEPTH, DEC_BATCH, SHIFT_WIDTH), 1.0),
        "state_wkv": nrm(4, (DEPTH, DEC_BATCH, N_HEADS, HEAD_DIM, HEAD_DIM), 0.5),
        "norm_mix": 1.0 + nrm(5, (DEPTH, D_MODEL), 0.02),
        "w_in": nrm(6, (DEPTH, D_MODEL, PROJ_WIDTH), D_MODEL ** -0.5),
        "w_pool": nrm(7, (DEPTH, N_POOL_GROUPS, POOL_GROUP, POOL_GROUP), POOL_GROUP ** -0.5),
        "pool_scale": 1.0 + nrm(8, (DEPTH, POOL_WIDTH), 0.1),
        "mu_shift": jax.random.uniform(ks[9], (DEPTH, SHIFT_WIDTH), jnp.float32),
        "w0": w0_base[None, :] + nrm(10, (DEPTH, C), 0.1),
        "w2": nrm(11, (DEPTH, DECAY_RANK, C), 0.5 * DECAY_RANK ** -0.5),
        "a0": nrm(12, (DEPTH, C), 0.1),
        "a2": nrm(13, (DEPTH, AAA_RANK, C), AAA_RANK ** -0.5),
        "g2": nrm(14, (DEPTH, GATE_RANK, C), GATE_RANK ** -0.5),
        "k_k": 0.85 + nrm(15, (DEPTH, C), 0.02),
        "k_a": 1.0 + nrm(16, (DEPTH, C), 0.02),
        "r_k": nrm(17, (DEPTH, N_HEADS, HEAD_DIM), 0.1),
        "gn_w": 1.0 + nrm(18, (DEPTH, C), 0.02),
        "gn_b": nrm(19, (DEPTH, C), 0.02),
        "w_out": nrm(20, (DEPTH, MIX_WIDTH, D_MODEL), MIX_WIDTH ** -0.5),
        "norm_ffn": 1.0 + nrm(21, (DEPTH, D_MODEL), 0.02),
        "w_gate": nrm(22, (DEPTH, D_MODEL, D_FF), D_MODEL ** -0.5),
        "w_up": nrm(23, (DEPTH, D_MODEL, D_FF), D_MODEL ** -0.5),
        "w_down": nrm(24, (DEPTH, D_FF, D_MODEL), D_FF ** -0.5),
        "norm_final": 1.0 + nrm(25, (D_MODEL,), 0.02),
    }


def reference(x_prompt, x_sample, state_pool, state_shift, state_wkv, norm_mix, w_in, w_pool,
              pool_scale, mu_shift, w0, w2, a0, a2, g2, k_k, k_a, r_k, gn_w, gn_b, w_out,
              norm_ffn, w_gate, w_up, w_down, norm_final):
    def run_group(x, pool_st, shift_st, wkv_st, start):
        pools, shifts, wkvs = [], [], []
        for l in range(DEPTH):
            x, p_new, s_new, w_new = mixer_block(
                x, pool_st[l], shift_st[l], wkv_st[l], start, norm_mix[l], w_in[l], w_pool[l],
                pool_scale[l], mu_shift[l], w0[l], w2[l], a0[l], a2[l], g2[l], k_k[l], k_a[l],
                r_k[l], gn_w[l], gn_b[l], w_out[l])
            x = swiglu_ffn(x, norm_ffn[l], w_gate[l], w_up[l], w_down[l])
            pools.append(p_new)
            shifts.append(s_new)
            wkvs.append(w_new)
        return rmsnorm(x, norm_final), jnp.stack(pools), jnp.stack(shifts), jnp.stack(wkvs)

    B = x_prompt.shape[0]
    pool0 = jnp.zeros((DEPTH, B, POOL_HIST, POOL_WIDTH), x_prompt.dtype)
    shift0 = jnp.zeros((DEPTH, B, SHIFT_WIDTH), x_prompt.dtype)
    wkv0 = jnp.zeros((DEPTH, B, N_HEADS, HEAD_DIM, HEAD_DIM), state_wkv.dtype)
    y_prompt, new_pool_prompt, new_shift_prompt, new_wkv_prompt = run_group(x_prompt, pool0, shift0, wkv0, 0)
    y_sample, new_pool_sample, new_shift_sample, new_wkv_sample = run_group(
        x_sample, state_pool, state_shift, state_wkv, PAST_LEN)
    return (y_prompt, y_sample, new_pool_prompt, new_shift_prompt, new_wkv_prompt,
            new_pool_sample, new_shift_sample, new_wkv_sample)
```

```python
import numpy as np
from contextlib import ExitStack
import concourse.bass as bass
import concourse.mybir as mybir
from concourse.bass_utils import run_bass_kernel_spmd

F32 = mybir.dt.float32
BF16 = mybir.dt.bfloat16
AF = mybir.ActivationFunctionType
ALU = mybir.AluOpType
AX = mybir.AxisListType

D = 2048
NCORE = 8
TOWN = 1024
NS = 16
PW = 1024
CW = 1024
SHW = 3360
PROJ = 4384
DFF = 5632
NFT = 44
CH = 128
DEC_K = 0.6065306597126334


class Sched:
    ENGS = ("sync", "act", "dve", "pool", "pe")

    def __init__(self, nc, sems, dma_sems):
        self.nc = nc
        self.sem = dict(zip(self.ENGS, sems))
        self.cnt = {e: 0 for e in self.ENGS}
        self.known = {e: {} for e in self.ENGS}
        self.prog = {e: [] for e in self.ENGS}
        self.lastw = {}
        self.readers = {}
        self.dma_sems = list(dma_sems)
        self.dma_next = 0
        self.dma_cnt = [0] * len(self.dma_sems)
        self.dma_pending = [None] * len(self.dma_sems)

    def _need(self, eng, dep):
        if dep is None:
            return
        if dep[0] == "e":
            _, f, n = dep
            if f == eng and eng in ("pe", "sync"):
                return
            key = ("e", f)
            sem = self.sem[f]
        else:
            _, i, n = dep
            key = ("d", i)
            sem = self.dma_sems[i]
        if self.known[eng].get(key, 0) >= n:
            return
        self.known[eng][key] = n
        self.prog[eng].append(lambda e, sem=sem, n=n: e.wait_ge(sem, n))

    def _deps(self, eng, reads, writes):
        for k in reads:
            self._need(eng, self.lastw.get(k))
        for k in writes:
            self._need(eng, self.lastw.get(k))
            for d in self.readers.get(k, ()):
                self._need(eng, d)

    def _commit(self, dep, reads, writes):
        for k in reads:
            self.readers.setdefault(k, []).append(dep)
        for k in writes:
            self.lastw[k] = dep
            self.readers[k] = []

    def op(self, eng, fn, reads=(), writes=(), inc=True):
        self._deps(eng, reads, writes)
        if inc:
            self.cnt[eng] += 1
            n = self.cnt[eng]
            sem = self.sem[eng]
            self.prog[eng].append(lambda e, fn=fn, sem=sem: fn(e).then_inc(sem, 1))
            dep = ("e", eng, n)
        else:
            self.prog[eng].append(lambda e, fn=fn: fn(e))
            dep = ("e", eng, self.cnt[eng] + 1)
        self._commit(dep, reads, writes)

    def dma(self, fn, reads=(), writes=(), eng="sync"):
        i = self.dma_next
        self.dma_next = (self.dma_next + 1) % len(self.dma_sems)
        if self.dma_pending[i] is not None:
            self._need(eng, self.dma_pending[i])
        self._deps(eng, reads, writes)
        self.dma_cnt[i] += 16
        v = self.dma_cnt[i]
        sem = self.dma_sems[i]
        self.prog[eng].append(lambda e, fn=fn, sem=sem: fn(e).then_inc(sem, 16))
        dep = ("d", i, v)
        self.dma_pending[i] = dep
        self._commit(dep, reads, writes)

    def barrier(self):
        snap = dict(self.cnt)
        pend = [d for d in self.dma_pending if d is not None]
        for e in self.ENGS:
            for d in pend:
                self._need(e, d)
            for f in self.ENGS:
                if f != e and snap[f] > 0:
                    self._need(e, ("e", f, snap[f]))

    def finish(self, eng="sync"):
        for d in self.dma_pending:
            self._need(eng, d)
        for f in self.ENGS:
            if f != eng and self.cnt[f] > 0:
                self._need(eng, ("e", f, self.cnt[f]))

    def emit(self, block):
        progs = self.prog

        @block.sync
        def _(e):
            for f in progs["sync"]:
                f(e)

        @block.scalar
        def _(e):
            for f in progs["act"]:
                f(e)

        @block.vector
        def _(e):
            for f in progs["dve"]:
                f(e)

        @block.gpsimd
        def _(e):
            for f in progs["pool"]:
                f(e)

        @block.tensor
        def _(e):
            for f in progs["pe"]:
                f(e)


def _nm(ap):
    return ap.tensor.name


class B:
    def __init__(self, S):
        self.S = S
        self.rr = 0

    def _k(self, aps):
        return [_nm(a) for a in aps if hasattr(a, "tensor")]

    def tt(self, eng, out, a, b, op):
        self.S.op(eng, lambda e: e.tensor_tensor(out=out, in0=a, in1=b, op=op), self._k([a, b]), self._k([out]))

    def ts(self, eng, out, a, s1, op0, s2=None, op1=None):
        if op1 is None:
            self.S.op(eng, lambda e: e.tensor_scalar(out=out, in0=a, scalar1=s1, scalar2=None, op0=op0),
                      self._k([a, s1]), self._k([out]))
        else:
            self.S.op(eng, lambda e: e.tensor_scalar(out=out, in0=a, scalar1=s1, scalar2=s2, op0=op0, op1=op1),
                      self._k([a, s1, s2]), self._k([out]))

    def stt(self, out, a, s, b, op0, op1, accum=None):
        if accum is None:
            self.S.op("dve", lambda e: e.scalar_tensor_tensor(out=out, in0=a, scalar=s, in1=b, op0=op0, op1=op1),
                      self._k([a, s, b]), self._k([out]))
        else:
            self.S.op("dve", lambda e: e.scalar_tensor_tensor(out=out, in0=a, scalar=s, in1=b, op0=op0, op1=op1,
                                                              accum_out=accum),
                      self._k([a, s, b]), self._k([out, accum]))

    def act(self, out, a, func, scale=None, bias=None):
        kw = {}
        if scale is not None:
            kw["scale"] = scale
        if bias is not None:
            kw["bias"] = bias
        self.S.op("act", lambda e: e.activation(out=out, in_=a, func=func, **kw),
                  self._k([a, scale, bias]), self._k([out]))

    def copy(self, eng, out, a):
        if eng == "act":
            self.act(out, a, AF.Copy)
        else:
            self.S.op(eng, lambda e: e.tensor_copy(out=out, in_=a), self._k([a]), self._k([out]))

    def rsqrt(self, out, a):
        self.act(out, a, AF.Sqrt)
        self.S.op("dve", lambda e: e.reciprocal(out=out, in_=out), self._k([out]), self._k([out]))

    def evac(self, out, a):
        self.rr ^= 1
        self.copy("act" if self.rr else "dve", out, a)

    def red(self, eng, out, a, op=ALU.add):
        self.S.op(eng, lambda e: e.tensor_reduce(out=out, in_=a, axis=AX.X, op=op), self._k([a]), self._k([out]))

    def memset(self, eng, ap, v):
        self.S.op(eng, lambda e: e.memset(ap, v), [], self._k([ap]))

    def mm(self, out, lhsT, rhs, start=True, stop=True, inc=None, tp=None):
        if inc is None:
            inc = stop
        kw = {}
        if tp is not None:
            kw["tile_position"] = tp
        self.S.op("pe", lambda e: e.matmul(out, lhsT=lhsT, rhs=rhs, start=start, stop=stop, **kw),
                  self._k([lhsT, rhs]), self._k([out]), inc=inc)

    def tr(self, out, a, ident, inc=True):
        self.S.op("pe", lambda e: e.transpose(out, a, ident), self._k([a, ident]), self._k([out]), inc=inc)

    def dma(self, out, a, eng="sync", slow=False):
        kw = {"allow_slow_non_contiguous": True} if slow else {}
        self.S.dma(lambda e: e.dma_start(out=out, in_=a, **kw), self._k([a]), self._k([out]), eng=eng)


def build_program(stop=None):
    nc = bass.Bass("TRN2", target_bir_lowering=False)

    def din(name, shape):
        return nc.dram_tensor(name, list(shape), F32, kind="ExternalInput").ap()

    def dout(name, shape):
        return nc.dram_tensor(name, list(shape), F32, kind="ExternalOutput").ap()

    xprev = din("xprev", [TOWN, D])
    xown = din("xown", [TOWN, D])
    xs = din("xs", [NS, D])
    st_pool = din("st_pool", [NS * 15, PW])
    st_pool3 = st_pool.rearrange("(b r) c -> b r c", r=15)
    st_shift = din("st_shift", [NS, SHW])
    st_wkv = din("st_wkv", [NS, 16, 64, 64])
    pos = din("pos", [1, TOWN])
    norm_mix = din("norm_mix", [D])
    w_in = din("w_in", [D, PROJ])
    w_pool = din("w_pool", [4, 256, 256])
    pool_scale = din("pool_scale", [PW])
    mu_shift = din("mu_shift", [SHW])
    w0 = din("w0", [CW])
    w2 = din("w2", [64, CW])
    a0 = din("a0", [CW])
    a2 = din("a2", [64, CW])
    g2 = din("g2", [160, CW])
    k_k = din("k_k", [CW])
    k_a = din("k_a", [CW])
    r_k = din("r_k", [CW])
    gn_w = din("gn_w", [CW])
    gn_b = din("gn_b", [CW])
    w_out = din("w_out", [D, D])
    norm_ffn = din("norm_ffn", [D])
    w_gate = din("w_gate", [D, DFF])
    w_up = din("w_up", [D, DFF])
    w_down = din("w_down", [DFF, D])
    norm_final = din("norm_final", [1, D])

    y_own = dout("y_own", [TOWN, D])
    y_s = dout("y_s", [NS, D])
    pool_last = dout("pool_last", [15, PW])
    shift_last = dout("shift_last", [1, SHW])
    wkv_fin = dout("wkv_fin", [16, 64, 64])
    npool_s = dout("npool_s", [NS, 15, PW])
    nshift_s = dout("nshift_s", [NS, SHW])
    nwkv_s = dout("nwkv_s", [NS, 16, 64, 64])

    es = ExitStack()
    with es:
        def sb(name, shape, dt=F32):
            return es.enter_context(nc.sbuf_tensor(name, list(shape), dt))

        sems = [es.enter_context(nc.semaphore(f"se{i}")) for i in range(5)]
        dsems = [es.enter_context(nc.semaphore(f"sd{i}")) for i in range(24)]
        S = Sched(nc, sems, dsems)
        b = B(S)

        dbg_n = [0]

        def dump(name, ap, shape):
            o = nc.dram_tensor("dbg_" + name, list(shape), F32, kind="ExternalOutput").ap()
            b.dma(o, ap)

        def done():
            S.finish()
            with nc.Block() as block:
                S.emit(block)
            return nc
        pbank = [es.enter_context(nc.psum_tensor(f"pb{i}", [128, 512], F32)) for i in range(8)]

        def pbf(i):
            return pbank[i][:].bitcast(BF16)

        ident = sb("ident", [128, 128])
        identb = sb("identb", [128, 128], BF16)
        blk = sb("blk", [128, 128])
        mask4 = sb("mask4", [128, 512])
        maskl = sb("maskl", [128, 256])
        b.memset("pool", ident[:], 1.0)
        S.op("pool", lambda e: e.affine_select(out=ident[:], in_=ident[:], pattern=[[-1, 128]], compare_op=ALU.is_equal,
                                               fill=0.0, base=0, channel_multiplier=1), ["ident"], ["ident"])
        b.copy("dve", identb[:], ident[:])
        b.memset("pool", blk[:], 0.0)
        b.memset("pool", blk[0:64, 0:64], 1.0)
        b.memset("pool", blk[64:128, 64:128], 1.0)
        b.memset("pool", mask4[:], 1.0)
        for q in range(4):
            cmp = ALU.is_gt if q % 2 == 0 else ALU.is_ge
            S.op("pool", lambda e, q=q, cmp=cmp: e.affine_select(
                out=mask4[:, q * 128:(q + 1) * 128], in_=mask4[:, q * 128:(q + 1) * 128], pattern=[[1, 128]],
                compare_op=cmp, fill=0.0, base=0, channel_multiplier=-1), ["mask4"], ["mask4"])
        b.memset("pool", maskl[:], 1.0)
        for q in range(2):
            S.op("pool", lambda e, q=q: e.affine_select(
                out=maskl[:, q * 128:(q + 1) * 128], in_=maskl[:, q * 128:(q + 1) * 128], pattern=[[-1, 128]],
                compare_op=ALU.is_gt, fill=0.0, base=0, channel_multiplier=1), ["maskl"], ["maskl"])

        def colparam(name, src, n):
            nt = (n + 127) // 128
            t = sb(name, [128, nt])
            nfull = n // 128
            if nfull:
                b.dma(t[:, 0:nfull], src[0:nfull * 128].rearrange("(t p) -> p t", p=128), slow=True)
            rem = n - nfull * 128
            if rem:
                b.dma(t[0:rem, nfull:nfull + 1], src[nfull * 128:n].rearrange("(t p) -> p t", p=rem), slow=True)
            return t

        g_mix = colparam("g_mix", norm_mix, D)
        g_ffn = colparam("g_ffn", norm_ffn, D)
        pscale = colparam("pscale", pool_scale, PW)
        mu_c = colparam("mu_c", mu_shift, SHW)
        w0_c = colparam("w0_c", w0, CW)
        a0_c = colparam("a0_c", a0, CW)
        kk_c = colparam("kk_c", k_k, CW)
        ka_c = colparam("ka_c", k_a, CW)
        rk_c = colparam("rk_c", r_k, CW)
        gnw_c = colparam("gnw_c", gn_w, CW)
        gnb_c = colparam("gnb_c", gn_b, CW)
        omm_c = sb("omm_c", [128, 27])
        nka_c = sb("nka_c", [128, 8])
        b.memset("dve", omm_c[:], 0.0)
        b.ts("dve", omm_c[:, 0:26], mu_c[:, 0:26], -1.0, ALU.mult, 1.0, ALU.add)
        b.ts("dve", omm_c[0:32, 26:27], mu_c[0:32, 26:27], -1.0, ALU.mult, 1.0, ALU.add)
        b.ts("dve", nka_c[:], ka_c[:], -1.0, ALU.mult)
        w2b = sb("w2b", [128, CW], BF16)
        g2b = sb("g2b", [128, CW], BF16)
        g2b2 = sb("g2b2", [32, CW], BF16)
        b.dma(w2b[0:64, :], w2[:, :], eng="pool")
        b.dma(w2b[64:128, :], a2[:, :], eng="pool")
        b.dma(g2b[:], g2[0:128, :], eng="pool")
        b.dma(g2b2[:], g2[128:160, :], eng="pool")
        wpb = sb("wpb", [128, 8, 256], BF16)
        b.dma(wpb[:], w_pool.rearrange("g (cc p) d -> p (g cc) d", p=128), eng="pool")

        eps_rms = 1e-6

        mixT = [sb(f"AT{i}", [128, 16, 512], BF16) for i in range(2)]
        mixTs = sb("ATs", [128, 16, NS], BF16)
        STt = [sb(f"ST{hp}", [128, 64]) for hp in range(8)]
        STb = [sb(f"STb{hp}", [128, 128], BF16) for hp in range(8)]
        carry = sb("carry", [128, 27])
        PL = sb("PL", [128, 27, 17])
        UL = sb("UL", [128, 8, 31])
        halo = sb("halo", [128, 8, 16])
        for hp in range(8):
            b.memset("pool", STt[hp][:], 0.0)
            b.memset("pool", STb[hp][:], 0.0)
        b.memset("pool", carry[:], 0.0)

        def norm_to_T_g(nbufs, src_ap, nrow, g_c, dst, dst_off, slot, src_is_sbuf=False):
            xin, xsq, xbf, ssq, rstd = nbufs
            if src_is_sbuf:
                xi = src_ap
            else:
                xi = xin[slot][0:nrow, :]
                b.dma(xi, src_ap)
            b.stt(xbf[slot][0:nrow, :], xi, 1.0, xi, ALU.mult, ALU.mult, accum=ssq[0:nrow, slot:slot + 1])
            b.ts("dve", rstd[0:nrow, slot:slot + 1], ssq[0:nrow, slot:slot + 1], 1.0 / D, ALU.mult, eps_rms, ALU.add)
            b.rsqrt(rstd[0:nrow, slot:slot + 1], rstd[0:nrow, slot:slot + 1])
            b.act(xbf[slot][0:nrow, :], xi, AF.Copy, scale=rstd[0:nrow, slot:slot + 1])
            for q in range(4):
                pb = pbf(q % 2)
                for j in range(4):
                    dk = q * 4 + j
                    b.tr(pb[:, j * 128:j * 128 + nrow], xbf[slot][0:nrow, dk * 128:(dk + 1) * 128],
                         identb[0:nrow, 0:nrow], inc=(j == 3))
                for j in range(4):
                    dk = q * 4 + j
                    if j % 2 == 0:
                        b.act(dst[:, dk, dst_off:dst_off + nrow], pb[:, j * 128:j * 128 + nrow], AF.Copy,
                              scale=g_c[:, dk:dk + 1])
                    else:
                        b.ts("dve", dst[:, dk, dst_off:dst_off + nrow], pb[:, j * 128:j * 128 + nrow],
                             g_c[:, dk:dk + 1], ALU.mult)

        if stop == "A":
            return done()
        mixer_scope = ExitStack()
        with mixer_scope:
            def sbm(name, shape, dt=F32):
                return mixer_scope.enter_context(nc.sbuf_tensor(name, list(shape), dt))

            hT = [sbm(f"hT{i}", [128, 16, 512], BF16) for i in range(2)]
            hTs = sbm("hTs", [128, 16, NS], BF16)
            xin = [sbm("xin0", [128, D]), sbm("xin1", [128, D])]
            xsq = None
            xbf = [sbm("xbf0", [128, D], BF16)] * 2
            ssq = sbm("ssq", [128, 2])
            rstd = sbm("rstd", [128, 2])
            nbufs = (xin, xsq, xbf, ssq, rstd)

            def norm_to_T(*a, **k):
                return norm_to_T_g(nbufs, *a, **k)

            def _unused(src_ap, nrow, g_c, dst, dst_off, slot, src_eng="sync", src_is_sbuf=False):
                if src_is_sbuf:
                    xi = src_ap
                else:
                    xi = xin[slot][0:nrow, :]
                    b.dma(xi, src_ap)
                b.tt("dve", xsq[0:nrow, :], xi, xi, ALU.mult)
                b.red("dve", ssq[0:nrow, slot:slot + 1], xsq[0:nrow, :])
                b.ts("dve", rstd[0:nrow, slot:slot + 1], ssq[0:nrow, slot:slot + 1], 1.0 / D, ALU.mult, eps_rms, ALU.add)
                b.rsqrt(rstd[0:nrow, slot:slot + 1], rstd[0:nrow, slot:slot + 1])
                b.act(xbf[slot][0:nrow, :], xi, AF.Copy, scale=rstd[0:nrow, slot:slot + 1])
                for q in range(4):
                    pb = pbf(q % 2)
                    for j in range(4):
                        dk = q * 4 + j
                        b.tr(pb[:, j * 128:j * 128 + nrow], xbf[slot][0:nrow, dk * 128:(dk + 1) * 128],
                             identb[0:nrow, 0:nrow], inc=(j == 3))
                    for j in range(4):
                        dk = q * 4 + j
                        if j % 2 == 0:
                            b.act(dst[:, dk, dst_off:dst_off + nrow], pb[:, j * 128:j * 128 + nrow], AF.Copy,
                                  scale=g_c[:, dk:dk + 1])
                        else:
                            b.ts("dve", dst[:, dk, dst_off:dst_off + nrow], pb[:, j * 128:j * 128 + nrow],
                                 g_c[:, dk:dk + 1], ALU.mult)

            wbuf = [sbm(f"wbuf{i}", [128, 16, 128], BF16) for i in range(4)]
            w_in_v = w_in.rearrange("(kc p) n -> p kc n", p=128)
            wctr = [0]

            def load_wcol(c0, ncol):
                t = wbuf[wctr[0] % 4]
                wctr[0] += 1
                b.dma(t[:, :, 0:ncol], w_in_v[:, :, c0:c0 + ncol], eng="pool")
                return t

            def proj(wt, ncol, rhs_tile, n, pout, off=0):
                for dk in range(16):
                    b.mm(pout[0:ncol, 0:n], wt[:, dk, 0:ncol], rhs_tile[:, dk, off:off + n], start=(dk == 0), stop=(dk == 15))

            XA = {nm: sbm(f"XA_{nm}", [128, 128]) for nm in ("r", "k", "v", "a", "b", "w")}
            bonus_s = sbm("bonus_s", [128, 8, NS])
            gate_s = sbm("gate_s", [128, 8, NS])
            SH = sbm("SH", [128, 27, NS])
            Ysf = sbm("Ysf", [128, 128])
            hpstack = mixer_scope.enter_context(ExitStack())

            def sbw(name, shape, dt=F32):
                return hpstack.enter_context(nc.sbuf_tensor(name, list(shape), dt))

            NT = 512
            Pb = [sbw(f"Pb{i}", [128, NT + 1]) for i in range(3)]
            dtmp = sbw("dtmp", [128, NT])
            Rt = sbw("Rt", [128, NT])
            K0 = sbw("K0", [128, NT])
            Vt = sbw("Vt", [128, NT])
            Kt = sbw("Kt", [128, NT])
            sig = sbw("sig", [128, NT])
            alr = sbw("alr", [128, NT])
            kkn = sbw("kkn", [128, NT])
            t1 = sbw("t1", [128, NT])
            t2 = sbw("t2", [128, NT])
            csig = sbw("csig", [128, NT])
            onesf = sbw("onesf", [128, NT])
            b.memset("pool", onesf[:], 1.0)
            gb = sbw("gb", [128, 4])
            gend = sbw("gend", [128, 4])
            ECt = sbw("ECt", [128, 4])
            ex = sbw("ex", [128, NT])
            rtb = sbw("rtb", [128, NT], BF16)
            atb = sbw("atb", [128, NT], BF16)
            ktb = sbw("ktb", [128, NT], BF16)
            btb = sbw("btb", [128, NT], BF16)
            Khb = sbw("Khb", [128, NT], BF16)
            Bhb = sbw("Bhb", [128, NT], BF16)
            Vb = sbw("Vb", [128, NT], BF16)
            bonus = sbw("bonus", [128, NT])
            gate = sbw("gate", [128, NT])
            Yt = dtmp
            wab = [sbw(f"wab{i}", [128, NT if i < 2 else NS], BF16) for i in range(3)]
            sgb = [sbw(f"sgb{i}", [128, NT if i < 2 else NS], BF16) for i in range(3)]
            sgb2 = [sbw(f"sgb2{i}", [32, NT if i < 2 else NS], BF16) for i in range(3)]
            AM = [sbw(f"AM{h}", [128, 512], BF16) for h in range(2)]
            LV = [sbw(f"LV{i}", [128, 2, 384], BF16) for i in range(2)]
            TM = sbw("TM", [128, 2, 4, 128], BF16)
            WT = sbw("WT", [128, 128], BF16)
            Ub = sbw("Ub", [128, 2, 128], BF16)
            b.memset("pool", TM[:].rearrange("p h q d -> p (h q d)"), 0.0)
            b.memset("pool", Ub[:].rearrange("p h d -> p (h d)"), 0.0)
            AM_b = [sbw(f"AMb{h}", [128, 512], BF16) for h in range(2)]
            LV_b = [sbw(f"LVb{i}", [128, 2, 384], BF16) for i in range(2)]
            TM_b = sbw("TMb", [128, 2, 4, 128], BF16)
            WT_b = sbw("WTb", [128, 128], BF16)
            Ub_b = sbw("Ubb", [128, 2, 128], BF16)
            b.memset("pool", TM_b[:].rearrange("p h q d -> p (h q d)"), 0.0)
            b.memset("pool", Ub_b[:].rearrange("p h d -> p (h d)"), 0.0)
            slots = [dict(AM=AM, LV=LV, TM=TM, WT=WT, Ub=Ub, pX=pbank[5], pY2=pbank[6]),
                     dict(AM=AM_b, LV=LV_b, TM=TM_b, WT=WT_b, Ub=Ub_b, pX=pbank[3], pY2=pbank[4])]

            def shift_mix(pb_t, ncol, n, ctile, out_ap):
                b.act(dtmp[0:ncol, 0:n], pb_t[0:ncol, 0:n], AF.Copy, scale=mu_c[0:ncol, ctile:ctile + 1])
                b.stt(out_ap, pb_t[0:ncol, 1:n + 1], omm_c[0:ncol, ctile:ctile + 1], dtmp[0:ncol, 0:n], ALU.mult, ALU.add)

            def run_pass(is_own):
                tiles = [("p", 0, 512), ("p", 1, 512)]
                if is_own:
                    tiles.append(("s", 0, NS))
                src = xown if is_own else xprev
                for tt_i in range(2):
                    for blk_i in range(4):
                        r0 = tt_i * 512 + blk_i * 128
                        norm_to_T(src[r0:r0 + 128, :], 128, g_mix, hT[tt_i], blk_i * 128, blk_i % 2)
                if is_own:
                    norm_to_T(xs[:, :], NS, g_mix, hTs, 0, 0)

                if stop == "P0":
                    return True

                def rhs_of(tl):
                    return hTs if tl[0] == "s" else hT[tl[1]]

                for li, (ctile, ncol) in enumerate(((24, 128), (25, 128), (26, 32))):
                    wt = load_wcol(PW + ctile * 128, ncol)
                    for ti, tl in enumerate(tiles):
                        n = tl[2]
                        pp = pbank[2 + (ti % 2)]
                        proj(wt, ncol, rhs_of(tl), n, pp)
                        mu_col = mu_c[0:ncol, ctile:ctile + 1]
                        if tl[0] == "p":
                            pbt = Pb[li % 3]
                            b.copy("dve", pbt[0:ncol, 0:1], carry[0:ncol, ctile:ctile + 1])
                            b.evac(pbt[0:ncol, 1:n + 1], pp[0:ncol, 0:n])
                            b.copy("dve", carry[0:ncol, ctile:ctile + 1], pbt[0:ncol, n:n + 1])
                            if is_own and tl[1] == 1:
                                b.copy("dve", PL[0:ncol, ctile, 16:17], pbt[0:ncol, n:n + 1])
                            shift_mix(pbt, ncol, n, ctile, t1[0:ncol, 0:n])
                        else:
                            b.evac(PL[0:ncol, ctile, 0:NS], pp[0:ncol, 0:n])
                            b.act(dtmp[0:ncol, 0:n], SH[0:ncol, ctile, :], AF.Copy, scale=mu_col)
                            b.stt(t1[0:ncol, 0:n], PL[0:ncol, ctile, 0:NS], omm_c[0:ncol, ctile:ctile + 1], dtmp[0:ncol, 0:n],
                                  ALU.mult, ALU.add)
                        if li == 0:
                            b.act(wab[ti][0:64, 0:n], t1[0:64, 0:n], AF.Tanh)
                            b.copy("act", wab[ti][64:128, 0:n], t1[64:128, 0:n])
                        elif li == 1:
                            b.act(sgb[ti][:, 0:n], t1[:, 0:n], AF.Sigmoid)
                        else:
                            b.act(sgb2[ti][:, 0:n], t1[0:32, 0:n], AF.Sigmoid)

                if stop == "P1":
                    return True
                wts_of = {}

                def item(hp, ti, tl):
                    n = tl[2]
                    rhs_t = rhs_of(tl)
                    outs = [Rt, K0, Vt]
                    cs_ = slice(hp * 128, (hp + 1) * 128)
                    if ti == 0:
                        wts_of[hp] = [load_wcol(PW + q * CW + hp * 128, 128) for q in range(3)]
                    wts = wts_of[hp]
                    for q in range(3):
                        proj(wts[q], 128, rhs_t, n, pbank[2 + q])
                    yield
                    for q in range(3):
                        ctile = q * 8 + hp
                        pp = pbank[2 + q]
                        if tl[0] == "p":
                            pbt = Pb[q]
                            b.copy("dve", pbt[:, 0:1], carry[:, ctile:ctile + 1])
                            b.evac(pbt[:, 1:n + 1], pp[:, 0:n])
                            b.copy("dve", carry[:, ctile:ctile + 1], pbt[:, n:n + 1])
                            if is_own and tl[1] == 1:
                                b.copy("dve", PL[:, ctile, 16:17], pbt[:, n:n + 1])
                        else:
                            b.evac(PL[:, ctile, 0:NS], pp[:, 0:n])
                    yield
                    for q in range(3):
                        ctile = q * 8 + hp
                        if tl[0] == "p":
                            shift_mix(Pb[q], 128, n, ctile, outs[q][:, 0:n])
                        else:
                            b.act(dtmp[:, 0:n], SH[:, ctile, :], AF.Copy, scale=mu_c[:, ctile:ctile + 1])
                            b.stt(outs[q][:, 0:n], PL[:, ctile, 0:NS], omm_c[:, ctile:ctile + 1], dtmp[:, 0:n],
                                  ALU.mult, ALU.add)
                    pw, pa, pg = pbank[5], pbank[6], pbank[7]
                    b.mm(pw[:, 0:n], w2b[0:64, cs_], wab[ti][0:64, 0:n])
                    b.mm(pa[:, 0:n], w2b[64:128, cs_], wab[ti][64:128, 0:n])
                    b.mm(pg[:, 0:n], g2b[:, cs_], sgb[ti][:, 0:n], start=True, stop=False)
                    b.mm(pg[:, 0:n], g2b2[:, cs_], sgb2[ti][:, 0:n], start=False, stop=True)
                    b.act(sig[:, 0:n], pw[:, 0:n], AF.Sigmoid, bias=w0_c[:, hp:hp + 1])
                    b.act(alr[:, 0:n], pa[:, 0:n], AF.Sigmoid, bias=a0_c[:, hp:hp + 1])
                    b.copy("act", gate[:, 0:n], pg[:, 0:n])
                    b.act(kkn[:, 0:n], K0[:, 0:n], AF.Copy, scale=kk_c[:, hp:hp + 1])
                    b.act(t1[:, 0:n], K0[:, 0:n], AF.Square, scale=kk_c[:, hp:hp + 1])
                    b.mm(pw[:, 0:n], blk[:], t1[:, 0:n])
                    b.ts("dve", t2[:, 0:n], pw[:, 0:n], 1e-24, ALU.max)
                    b.rsqrt(t2[:, 0:n], t2[:, 0:n])
                    b.tt("dve", kkn[:, 0:n], kkn[:, 0:n], t2[:, 0:n], ALU.mult)
                    b.act(t1[:, 0:n], alr[:, 0:n], AF.Identity, scale=ka_c[:, hp:hp + 1], bias=nka_c[:, hp:hp + 1])
                    b.stt(Kt[:, 0:n], t1[:, 0:n], 1.0, K0[:, 0:n], ALU.add, ALU.mult)
                    b.stt(t1[:, 0:n], Rt[:, 0:n], rk_c[:, hp:hp + 1], Kt[:, 0:n], ALU.mult, ALU.mult)
                    b.mm(pa[:, 0:n], blk[:], t1[:, 0:n])
                    yield
                    b.tt("dve", bonus[:, 0:n], pa[:, 0:n], Vt[:, 0:n], ALU.mult)
                    b.tt("dve", t2[:, 0:n], kkn[:, 0:n], alr[:, 0:n], ALU.mult)
                    if tl[0] == "s":
                        def stash(nm, ap_):
                            b.copy("dve", XA[nm][:].rearrange("p (b h) -> p b h", h=8)[:, :, hp], ap_)
                        b.act(t1[:, 0:n], sig[:, 0:n], AF.Exp, scale=-DEC_K)
                        stash("w", t1[:, 0:n])
                        stash("r", Rt[:, 0:n])
                        stash("k", Kt[:, 0:n])
                        stash("v", Vt[:, 0:n])
                        b.ts("dve", t1[:, 0:n], kkn[:, 0:n], -1.0, ALU.mult)
                        stash("a", t1[:, 0:n])
                        stash("b", t2[:, 0:n])
                        b.copy("dve", bonus_s[:, hp, :], bonus[:, 0:n])
                        b.copy("dve", gate_s[:, hp, :], gate[:, 0:n])
                        return
                    nch = n // CH
                    S.op("dve", lambda e, n=n: e.tensor_tensor_scan(out=csig[:, 0:n], data0=onesf[:, 0:n], data1=sig[:, 0:n],
                                                                    initial=0.0, op0=ALU.mult, op1=ALU.add),
                         ["onesf", "sig"], ["csig"])
                    b.memset("dve", gb[:, 0:1], 0.0)
                    c3 = csig[:, 0:n].rearrange("p (c t) -> p c t", t=CH)
                    if nch > 1:
                        b.copy("dve", gb[:, 1:nch], c3[:, 0:nch - 1, CH - 1])
                    b.tt("dve", c3, c3, gb[:, 0:nch].unsqueeze(2).to_broadcast([128, nch, CH]), ALU.subtract)
                    b.copy("dve", gend[:, 0:nch], c3[:, :, CH - 1])
                    b.act(ECt[:, 0:nch], gend[:, 0:nch], AF.Exp, scale=-DEC_K)
                    b.act(ex[:, 0:n], csig[:, 0:n], AF.Exp, scale=-DEC_K)
                    b.tt("dve", rtb[:, 0:n], Rt[:, 0:n], ex[:, 0:n], ALU.mult)
                    b.act(ex[:, 0:n], csig[:, 0:n], AF.Exp, scale=DEC_K)
                    b.tt("dve", ktb[:, 0:n], Kt[:, 0:n], ex[:, 0:n], ALU.mult)
                    b.tt("dve", btb[:, 0:n], t2[:, 0:n], ex[:, 0:n], ALU.mult)
                    b.tt("dve", t1[:, 0:n], csig[:, 0:n], sig[:, 0:n], ALU.subtract)
                    b.act(ex[:, 0:n], t1[:, 0:n], AF.Exp, scale=-DEC_K)
                    b.stt(atb[:, 0:n], kkn[:, 0:n], -1.0, ex[:, 0:n], ALU.mult, ALU.mult)
                    t13 = t1[:, 0:n].rearrange("p (c t) -> p c t", t=CH)
                    b.tt("dve", t13, gend[:, 0:nch].unsqueeze(2).to_broadcast([128, nch, CH]), c3, ALU.subtract)
                    b.act(ex[:, 0:n], t1[:, 0:n], AF.Exp, scale=-DEC_K)
                    b.tt("dve", Khb[:, 0:n], Kt[:, 0:n], ex[:, 0:n], ALU.mult)
                    b.tt("dve", Bhb[:, 0:n], t2[:, 0:n], ex[:, 0:n], ALU.mult)
                    b.copy("act", Vb[:, 0:n], Vt[:, 0:n])
                    yield

                    def stageA(sl, c):
                        cs = slice(c * CH, (c + 1) * CH)
                        AMs, LVs, TMs, WTs = sl["AM"], sl["LV"], sl["TM"], sl["WT"]
                        pX, pY2 = sl["pX"], sl["pY2"]
                        pA = [pbank[0], pbank[1]]
                        pZb = pbf(7)
                        for h in range(2):
                            ph = slice(64 * h, 64 * h + 64)
                            b.mm(pA[h][:, 0:128], btb[ph, cs], atb[ph, cs])
                            b.mm(pA[h][:, 128:256], btb[ph, cs], rtb[ph, cs])
                            b.mm(pA[h][:, 256:384], ktb[ph, cs], atb[ph, cs])
                            b.mm(pA[h][:, 384:512], ktb[ph, cs], rtb[ph, cs])
                            b.mm(pX[:, h * 128:(h + 1) * 128], atb[ph, cs], btb[ph, cs])
                            for q, srcT in enumerate((atb, Vb, Khb, Bhb)):
                                b.tr(pZb[:, h * 256 + q * 64:h * 256 + (q + 1) * 64], srcT[ph, cs], identb[ph, ph],
                                     inc=(q == 3))
                        for h in range(2):
                            b.tt("dve", AMs[h][:], pA[h][:], mask4[:], ALU.mult)
                        b.tt("dve", LVs[0][:, :, 128:256], pX[:, 0:256].rearrange("p (h j) -> p h j", h=2),
                             maskl[:].rearrange("p (h j) -> p h j", h=2), ALU.mult)
                        for h in range(2):
                            b.copy("act", TMs[:, h, :, 64 * h:64 * h + 64],
                                   pZb[:, h * 256:(h + 1) * 256].rearrange("p (q d) -> p q d", q=4))
                        yield
                        for h in range(2):
                            b.copy("act", LVs[0][:, h, 256:384], AMs[h][:, 0:128])
                            b.copy("dve", LVs[0][:, h, 64 * h:64 * h + 64], TMs[:, h, 0, 64 * h:64 * h + 64])
                            b.mm(pY2[:, h * 64:(h + 1) * 64], AMs[h][:, 256:384], TMs[:, h, 1, 64 * h:64 * h + 64])
                        yield
                        for h in range(2):
                            o = 64 * (1 - h)
                            b.copy("dve", LVs[0][:, h, o:o + 64], pY2[:, h * 64:(h + 1) * 64])
                        yield
                        cur = 0
                        for lvl in range(7):
                            nxt = 1 - cur
                            for h in range(2):
                                b.mm(pX[:, h * 128:(h + 1) * 128], LVs[cur][:, h, 256:384], LVs[cur][:, h, 0:128])
                            if lvl < 6:
                                for h in range(2):
                                    b.mm(pY2[:, h * 256:h * 256 + 128], LVs[cur][:, h, 256:384], LVs[cur][:, h, 128:256])
                                    b.mm(pY2[:, h * 256 + 128:h * 256 + 256], LVs[cur][:, h, 128:256], LVs[cur][:, h, 256:384])
                            yield
                            b.tt("dve", LVs[nxt][:, :, 0:128], pX[:, 0:256].rearrange("p (h j) -> p h j", h=2),
                                 LVs[cur][:, :, 0:128], ALU.add)
                            if lvl < 6:
                                b.copy("act", LVs[nxt][:, :, 128:384], pY2[:, 0:512].rearrange("p (h j) -> p h j", h=2))
                            yield
                            cur = nxt
                        sl["Zf"] = LVs[cur]
                        for h in range(2):
                            b.tr(pZb[:, h * 128:(h + 1) * 128], LVs[cur][:, h, 0:128], identb[:])
                        for h in range(2):
                            ph = slice(64 * h, 64 * h + 64)
                            b.copy("act", WTs[ph, :], pZb[ph, h * 128:(h + 1) * 128])
                        yield

                    def stageB(sl, c):
                        cs = slice(c * CH, (c + 1) * CH)
                        AMs, TMs, WTs, Ubs, Zf = sl["AM"], sl["TM"], sl["WT"], sl["Ub"], sl["Zf"]
                        pY2 = sl["pY2"]
                        pA = [pbank[0], pbank[1]]
                        for h in range(2):
                            ph = slice(64 * h, 64 * h + 64)
                            b.mm(pA[h][:, 0:64], WTs[ph, :], STb[hp][ph, 64 * h:64 * h + 64])
                        for h in range(2):
                            o = 64 * (1 - h)
                            b.tt("dve", Ubs[:, h, 64 * h:64 * h + 64], pA[h][:, 0:64], Zf[:, h, o:o + 64], ALU.add)
                        yield
                        if is_own:
                            b.mm(pY2[:, 0:128], STb[hp][:], rtb[:, cs], start=True, stop=False)
                            for h in range(2):
                                b.mm(pY2[:, 0:128], Ubs[:, h, :], AMs[h][:, 128:256], start=False, stop=False)
                                b.mm(pY2[:, 0:128], TMs[:, h, 1, :], AMs[h][:, 384:512], start=False, stop=(h == 1))
                        for h in range(2):
                            hs = slice(64 * h, 64 * h + 64)
                            b.mm(pY2[:, 128:192], TMs[:, h, 3, :], Ubs[:, h, hs], start=(h == 0), stop=False)
                            b.mm(pY2[:, 128:192], TMs[:, h, 2, :], TMs[:, h, 1, hs], start=False, stop=(h == 1))
                        yield
                        if is_own:
                            b.copy("act", Yt[:, cs], pY2[:, 0:128])
                        b.ts("dve", STt[hp][:], STt[hp][:], ECt[:, c:c + 1], ALU.mult)
                        b.copy("act", ex[:, 0:64], pY2[:, 128:192])
                        b.tt("dve", STt[hp][:], ex[:, 0:64], STt[hp][:], ALU.add)
                        b.copy("act", STb[hp][0:64, 0:64], STt[hp][0:64, :])
                        b.copy("act", STb[hp][64:128, 64:128], STt[hp][64:128, :])

                    def chunk_gen(c):
                        sl = slots[c % 2]
                        for _ in stageA(sl, c):
                            yield "A"
                        yield "B?"
                        for _ in stageB(sl, c):
                            yield "B"

                    cur_g = chunk_gen(0)
                    nxt_g = chunk_gen(1) if nch > 1 else None
                    nxt_c = 1
                    nxt_waiting = False
                    while cur_g is not None:
                        try:
                            next(cur_g)
                            cur_alive = True
                        except StopIteration:
                            cur_alive = False
                        if not cur_alive:
                            cur_g, nxt_waiting = nxt_g, False
                            nxt_c += 1
                            nxt_g = chunk_gen(nxt_c) if (cur_g is not None and nxt_c < nch) else None
                            continue
                        if nxt_g is not None and not nxt_waiting:
                            if next(nxt_g) == "B?":
                                nxt_waiting = True
                    if is_own:
                        post(hp, Yt[:, 0:n], bonus[:, 0:n], gate[:, 0:n], mixT[tl[1]][:, 8 + hp, 0:n], n)

                its = [(hp, ti, tl) for hp in range(8) for ti, tl in enumerate(tiles)]
                gens = [item(*it) for it in its]

                def step(g_):
                    try:
                        next(g_)
                    except StopIteration:
                        pass

                step(gens[0])
                step(gens[0])
                for i in range(len(its)):
                    step(gens[i])
                    if i + 1 < len(its):
                        step(gens[i + 1])
                    step(gens[i])
                    if i + 1 < len(its):
                        step(gens[i + 1])
                    for _ in gens[i]:
                        pass

            def post(hp, Y, bon, gt, out_ap, n):
                pm_, pv_ = pbank[3], pbank[4]
                b.mm(pm_[:, 0:n], blk[:], Y)
                b.ts("dve", t1[:, 0:n], pm_[:, 0:n], -1.0 / 64.0, ALU.mult)
                b.tt("dve", t1[:, 0:n], t1[:, 0:n], Y, ALU.add)
                b.tt("dve", t2[:, 0:n], t1[:, 0:n], t1[:, 0:n], ALU.mult)
                b.mm(pv_[:, 0:n], blk[:], t2[:, 0:n])
                b.ts("dve", t2[:, 0:n], pv_[:, 0:n], 1.0 / 64.0, ALU.mult, 64e-5, ALU.add)
                b.rsqrt(t2[:, 0:n], t2[:, 0:n])
                b.tt("dve", t1[:, 0:n], t1[:, 0:n], t2[:, 0:n], ALU.mult)
                b.ts("dve", t1[:, 0:n], t1[:, 0:n], gnw_c[:, hp:hp + 1], ALU.mult, gnb_c[:, hp:hp + 1], ALU.add)
                b.tt("dve", t1[:, 0:n], t1[:, 0:n], bon, ALU.add)
                b.tt("dve", out_ap, t1[:, 0:n], gt, ALU.mult)

            with ExitStack() as tsc:
                shtm = tsc.enter_context(nc.sbuf_tensor("shtm", [NS, SHW], F32))
                b.dma(shtm[:], st_shift[:, :])
                for ct in range(27):
                    ncl = min(128, SHW - ct * 128)
                    pp = pbank[2 + ct % 2]
                    b.tr(pp[0:ncl, 0:NS], shtm[:, ct * 128:ct * 128 + ncl], ident[0:NS, 0:NS])
                    b.evac(SH[0:ncl, ct, :], pp[0:ncl, 0:NS])
                S.barrier()

            if stop == "B":
                return done()
            if run_pass(False):
                return done()
            if stop == "C":
                return done()
            for ct in range(8):
                wt = load_wcol(ct * 128, 128)
                pp = pbank[2 + ct % 2]
                proj(wt, 128, hT[1], 16, pp, off=496)
                b.evac(halo[:, ct, :], pp[:, 0:16])
            if stop == "D":
                return done()
            if run_pass(True):
                return done()
            if stop == "E":
                dump("PL", PL[:].rearrange("p c t -> p (c t)"), [128, 27 * 17])
                dump("ST0", STt[0][:], [128, 64])
                dump("ST3", STt[3][:], [128, 64])
                cpm = sbw("cpm", [128, 1024])
                b.copy("dve", cpm[:, 0:512], mixT[0][:, 8, :])
                b.copy("dve", cpm[:, 512:1024], mixT[0][:, 11, :])
                dump("mix", cpm[:], [128, 1024])
                return done()
            wfin = sbw("wfin", [64, 16, 64])
            for hp in range(8):
                pp = pbank[2 + hp % 2]
                b.tr(pp[0:64, 0:128], STt[hp][:], ident[:])
                b.evac(wfin[:, 2 * hp:2 * hp + 2, :].rearrange("p h k -> p (h k)"), pp[0:64, 0:128])
            b.dma(wkv_fin.rearrange("h v k -> v h k"), wfin[:])

            if stop == "F":
                dump("PL", PL[:].rearrange("p c t -> p (c t)"), [128, 27 * 17])
                return done()
            post_tmp = (t1, t2)
            with ExitStack() as ssc:
                def sbs(name, shape, dt=F32):
                    return ssc.enter_context(nc.sbuf_tensor(name, list(shape), dt))
                vecs = {}
                for nm in ("r", "k", "v", "a", "b", "w"):
                    pp = pbank[2 + (len(vecs) % 2)]
                    b.tr(pp[:, 0:128], XA[nm][:], ident[:])
                    vt = sbs(f"sv_{nm}", [128, 128])
                    b.evac(vt[:], pp[:, 0:128])
                    vecs[nm] = vt
                Sa = sbs("Sa", [128, 128])
                ysv = sbs("ysv", [128, 128])
                wkv_in = st_wkv.rearrange("b (hh hl) v k -> (b hh) hl (v k)", hl=2)
                wkv_o = nwkv_s.rearrange("b (hh hl) v k -> (b hh) hl (v k)", hl=2)
                VQ = 8
                Sst = sbs("Sst", [128, VQ * 64])
                Tst = sbs("Tst", [128, VQ * 64])
                S3 = Sst[:].rearrange("p (v k) -> p v k", v=VQ)
                T3 = Tst[:].rearrange("p (v k) -> p v k", v=VQ)
                for hl in range(2):
                    for vq in range(64 // VQ):
                        v0 = hl * 64 + vq * VQ

                        def kbc(vt):
                            return vt[:, hl * 64:(hl + 1) * 64].unsqueeze(1).to_broadcast([128, VQ, 64])

                        def vbc(vt):
                            return vt[:, v0:v0 + VQ].unsqueeze(2).to_broadcast([128, VQ, 64])
                        b.dma(Sst[:], wkv_in[:, hl, vq * VQ * 64:(vq + 1) * VQ * 64])
                        b.tt("dve", T3, S3, kbc(vecs["a"]), ALU.mult)
                        b.red("dve", Sa[:, v0:v0 + VQ], T3)
                        b.tt("dve", S3, S3, kbc(vecs["w"]), ALU.mult)
                        b.tt("dve", T3, vbc(Sa), kbc(vecs["b"]), ALU.mult)
                        b.tt("dve", S3, S3, T3, ALU.add)
                        b.tt("dve", T3, vbc(vecs["v"]), kbc(vecs["k"]), ALU.mult)
                        b.tt("dve", S3, S3, T3, ALU.add)
                        b.dma(wkv_o[:, hl, vq * VQ * 64:(vq + 1) * VQ * 64], Sst[:])
                        b.tt("dve", T3, S3, kbc(vecs["r"]), ALU.mult)
                        b.red("dve", ysv[:, v0:v0 + VQ], T3)
                pp = pbank[2]
                b.tr(pp[:, 0:128], ysv[:], ident[:])
                b.evac(Ysf[:], pp[:, 0:128])
                Yc = sbs("Yc", [128, NS])
                for hp in range(8):
                    b.copy("dve", Yc[:], Ysf[:].rearrange("p (b h) -> p b h", h=8)[:, :, hp])
                    post(hp, Yc[:], bonus_s[:, hp, :], gate_s[:, hp, :], mixTs[:, 8 + hp, :], NS)
                S.barrier()
            hpstack.close()
            if stop == "G":
                dump("PL", PL[:].rearrange("p c t -> p (c t)"), [128, 27 * 17])
                return done()

            with ExitStack() as psc:
                def sbp(name, shape, dt=F32):
                    return psc.enter_context(nc.sbuf_tensor(name, list(shape), dt))
                L = 16 + TOWN
                posb = sbp("posb", [128, TOWN])
                b.dma(posb[:], pos.to_broadcast([128, TOWN]))
                inv = sbp("inv", [128, 4, TOWN])
                for g, W in enumerate((2, 4, 8, 16)):
                    b.ts("dve", inv[:, g, :], posb[:], 1.0, ALU.add, float(W), ALU.min)
                    S.op("dve", lambda e, g=g: e.reciprocal(out=inv[:, g, :], in_=inv[:, g, :]), ["inv"], ["inv"])
                Ubuf = sbp("Ubuf", [128, L])
                s_a = sbp("s_a", [128, L])
                s_b = sbp("s_b", [128, L])
                b.memset("pool", s_a[:], 0.0)
                b.memset("pool", s_b[:], 0.0)
                s_tmp = sbp("s_tmp", [128, TOWN])
                pooledT = [sbp(f"pooled{i}", [128, TOWN], BF16) for i in range(2)]
                pooledS = [sbp(f"pooledS{i}", [128, NS], BF16) for i in range(2)]
                stp = [sbp(f"stp{i}", [120, PW]) for i in range(2)]
                b.dma(stp[0][:], st_pool[0:120, :])
                b.dma(stp[1][:], st_pool[120:240, :])
                UbS = sbp("UbS", [128, NS, 16])
                swS = sbp("swS", [128, NS])
                for ct in range(8):
                    g = ct // 2
                    W = 2 ** (g + 1)
                    wt = load_wcol(ct * 128, 128)
                    b.copy("dve", Ubuf[:, 0:16], halo[:, ct, :])
                    for ti in range(2):
                        pp = pbank[2 + ti]
                        proj(wt, 128, hT[ti], 512, pp)
                        b.evac(Ubuf[:, 16 + ti * 512:16 + (ti + 1) * 512], pp[:, 0:512])
                    pp = pbank[4]
                    proj(wt, 128, hTs, NS, pp)
                    b.evac(UL[:, ct, 0:NS], pp[:, 0:NS])
                    b.copy("dve", UL[:, ct, 16:31], Ubuf[:, L - 15:L])
                    cur, sh, bi = Ubuf, 1, 0
                    bufs = [s_a, s_b]
                    while sh < W:
                        nxt = bufs[bi]
                        bi ^= 1
                        b.tt("dve", nxt[:, sh:L], cur[:, sh:L], cur[:, 0:L - sh], ALU.add)
                        cur = nxt
                        sh *= 2
                    b.tt("dve", s_tmp[:], cur[:, 16:L], inv[:, g, :], ALU.mult)
                    b.tt("dve", pooledT[ct % 2][:], s_tmp[:], Ubuf[:, 16:L], ALU.subtract)
                    for hf in range(2):
                        pp = pbank[5 + hf]
                        b.tr(pp[:, 0:120], stp[hf][:, ct * 128:(ct + 1) * 128], ident[0:120, 0:120])
                        b.evac(UbS[:, 8 * hf:8 * hf + 8, 0:15], pp[:, 0:120].rearrange("p (b r) -> p b r", r=15))
                    b.copy("dve", UbS[:, :, 15], UL[:, ct, 0:NS])
                    b.red("dve", swS[:], UbS[:, :, 16 - W:16])
                    b.stt(pooledS[ct % 2][:], swS[:], 1.0 / W, UL[:, ct, 0:NS], ALU.mult, ALU.subtract)
                    if ct % 2 == 1:
                        for dt_ in range(2):
                            mt = 2 * g + dt_
                            for ti in range(3):
                                n = 512 if ti < 2 else NS
                                pp = pbank[2 + ti]
                                for cc in range(2):
                                    rhs = pooledT[cc][:, ti * 512:(ti + 1) * 512] if ti < 2 else pooledS[cc][:]
                                    b.mm(pp[:, 0:n], wpb[:, 2 * g + cc, dt_ * 128:(dt_ + 1) * 128], rhs,
                                         start=(cc == 0), stop=(cc == 1))
                                dst = mixT[ti][:, mt, :] if ti < 2 else mixTs[:, mt, :]
                                b.act(dst, pp[:, 0:n], AF.Copy, scale=pscale[:, mt:mt + 1])
                if stop == "DBG":
                    dump("PL", PL[:].rearrange("p c t -> p (c t)"), [128, 27 * 17])
                    dump("SH", SH[:].rearrange("p c t -> p (c t)"), [128, 27 * NS])
                pltm = sbp("pltm", [17, SHW])
                for ct in range(27):
                    ncl = min(128, SHW - ct * 128)
                    pp = pbank[5 + ct % 2]
                    b.tr(pp[0:17, 0:ncl], PL[0:ncl, ct, :], ident[0:ncl, 0:ncl])
                    b.evac(pltm[:, ct * 128:ct * 128 + ncl], pp[0:17, 0:ncl])
                b.dma(nshift_s[:, :], pltm[0:16, :])
                b.dma(shift_last[:, :], pltm[16:17, :])
                ultm = sbp("ultm", [31, PW])
                for ct in range(8):
                    pp = pbank[5 + ct % 2]
                    b.tr(pp[0:31, 0:128], UL[:, ct, :], ident[:])
                    b.evac(ultm[:, ct * 128:(ct + 1) * 128], pp[0:31, 0:128])
                b.dma(npool_s[:, 14, :], ultm[0:16, :])
                b.dma(pool_last[:, :], ultm[16:31, :])
                b.dma(npool_s[:, 0:14, :], st_pool3[:, 1:15, :])
                S.barrier()

        if stop == "H":
            return done()
        x2 = [sb(f"x2_{i}", [128, D]) for i in range(9)]
        tokM = [128] * 8 + [NS]

        def at_slice(tk, cc):
            if tk < 8:
                return mixT[tk // 4][:, cc, (tk % 4) * 128:(tk % 4 + 1) * 128]
            return mixTs[:, cc, :]

        with ExitStack() as sc2:
            wob = sc2.enter_context(nc.sbuf_tensor("wob", [128, 16, D], BF16))
            xr = sc2.enter_context(nc.sbuf_tensor("xr", [128, D], F32))
            for cc in range(16):
                b.dma(wob[:, cc, :], w_out[cc * 128:(cc + 1) * 128, :], eng="pool")
            for tk in range(9):
                M = tokM[tk]
                b.dma(xr[0:M, :], xown[tk * 128:(tk + 1) * 128, :] if tk < 8 else xs[:, :])
                for db in range(4):
                    pp = pbank[db]
                    for cc in range(16):
                        b.mm(pp[0:M, :], at_slice(tk, cc), wob[:, cc, db * 512:(db + 1) * 512],
                             start=(cc == 0), stop=(cc == 15))
                    b.tt("dve", x2[tk][0:M, db * 512:(db + 1) * 512], pp[0:M, :], xr[0:M, db * 512:(db + 1) * 512], ALU.add)
            S.barrier()

        if stop == "I":
            return done()
        with ExitStack() as sc3:
            xsq3 = None
            xbf3 = sc3.enter_context(nc.sbuf_tensor("xbf3", [128, D], BF16))
            ssq3 = sc3.enter_context(nc.sbuf_tensor("ssq3", [128, 2], F32))
            rstd3 = sc3.enter_context(nc.sbuf_tensor("rstd3", [128, 2], F32))
            nb3 = (None, xsq3, [xbf3, xbf3], ssq3, rstd3)
            for tk in range(9):
                M = tokM[tk]
                if tk < 8:
                    norm_to_T_g(nb3, x2[tk][0:M, :], M, g_ffn, mixT[tk // 4], (tk % 4) * 128, 0, src_is_sbuf=True)
                else:
                    norm_to_T_g(nb3, x2[tk][0:M, :], M, g_ffn, mixTs, 0, 0, src_is_sbuf=True)
            S.barrier()

        if stop == "J":
            return done()
        with ExitStack() as sc4:
            def sb4(name, shape, dt=F32):
                return sc4.enter_context(nc.sbuf_tensor(name, list(shape), dt))
            GF = 11
            uT = sb4("uT", [128, GF, TOWN + NS], BF16)
            wg = [sb4(f"wg{i}", [128, 16, 128], BF16) for i in range(3)]
            wu = [sb4(f"wu{i}", [128, 16, 128], BF16) for i in range(3)]
            wdn = [sb4(f"wdn{i}", [128, GF, 512], BF16) for i in range(2)]
            sgt = sb4("sgt", [128, 512])
            wg_v = w_gate.rearrange("(kc p) n -> p kc n", p=128)
            wu_v = w_up.rearrange("(kc p) n -> p kc n", p=128)
            wd_v = w_down.rearrange("(ft p) d -> p ft d", p=128)
            ftiles = [(mixT[0], 512, 0), (mixT[1], 512, 512), (mixTs, NS, 1024)]
            pctr = 0
            dctr = 0
            for grp in range(NFT // GF):
                for fl in range(GF):
                    ft = grp * GF + fl
                    wgt, wut = wg[ft % 3], wu[ft % 3]
                    b.dma(wgt[:], wg_v[:, :, ft * 128:(ft + 1) * 128], eng="pool")
                    b.dma(wut[:], wu_v[:, :, ft * 128:(ft + 1) * 128], eng="pool")
                    for (rt_, n, off) in ftiles:
                        pg = pbank[(2 * pctr) % 8]
                        pu = pbank[(2 * pctr + 1) % 8]
                        pctr += 1
                        for dk in range(16):
                            b.mm(pg[:, 0:n], wgt[:, dk, :], rt_[:, dk, 0:n], start=(dk == 0), stop=(dk == 15))
                        for dk in range(16):
                            b.mm(pu[:, 0:n], wut[:, dk, :], rt_[:, dk, 0:n], start=(dk == 0), stop=(dk == 15))
                        b.act(sgt[:, 0:n], pg[:, 0:n], AF.Silu)
                        b.tt("dve", uT[:, fl, off:off + n], sgt[:, 0:n], pu[:, 0:n], ALU.mult)
                for db in range(4):
                    wdt = wdn[dctr % 2]
                    dctr += 1
                    b.dma(wdt[:], wd_v[:, grp * GF:(grp + 1) * GF, db * 512:(db + 1) * 512], eng="pool")
                    for tk in range(9):
                        M = tokM[tk]
                        pp = pbank[pctr % 8]
                        pctr += 1
                        for fl in range(GF):
                            b.mm(pp[0:M, :], uT[:, fl, tk * 128:tk * 128 + M], wdt[:, fl, :], start=(fl == 0), stop=(fl == GF - 1))
                        b.tt("dve", x2[tk][0:M, db * 512:(db + 1) * 512], pp[0:M, :], x2[tk][0:M, db * 512:(db + 1) * 512], ALU.add)
            S.barrier()

        if stop == "K":
            return done()
        with ExitStack() as sc5:
            gfin = sc5.enter_context(nc.sbuf_tensor("gfin", [128, D], F32))
            xsq5 = sc5.enter_context(nc.sbuf_tensor("xsq5", [128, D], F32))
            ssq5 = sc5.enter_context(nc.sbuf_tensor("ssq5", [128, 1], F32))
            rstd5 = sc5.enter_context(nc.sbuf_tensor("rstd5", [128, 1], F32))
            b.dma(gfin[:], norm_final.to_broadcast([128, D]))
            for tk in range(9):
                M = tokM[tk]
                xi = x2[tk][0:M, :]
                b.tt("dve", xsq5[0:M, :], xi, xi, ALU.mult)
                b.red("dve", ssq5[0:M, :], xsq5[0:M, :])
                b.ts("dve", rstd5[0:M, :], ssq5[0:M, :], 1.0 / D, ALU.mult, eps_rms, ALU.add)
                b.rsqrt(rstd5[0:M, :], rstd5[0:M, :])
                b.ts("dve", xi, xi, rstd5[0:M, 0:1], ALU.mult)
                b.tt("dve", xi, xi, gfin[0:M, :], ALU.mult)
                b.dma(y_own[tk * 128:(tk + 1) * 128, :] if tk < 8 else y_s[:, :], xi)

        return done()


_NC_CACHE = {}


def kernel(**inp):
    f = lambda k: np.ascontiguousarray(np.asarray(inp[k], dtype=np.float32))
    xp = f("x_prompt")
    xsmp = f("x_sample")
    sp, ss, sw = f("state_pool"), f("state_shift"), f("state_wkv")
    shared = {
        "norm_mix": f("norm_mix").reshape(D), "w_in": f("w_in").reshape(D, PROJ),
        "w_pool": f("w_pool").reshape(4, 256, 256), "pool_scale": f("pool_scale").reshape(PW),
        "mu_shift": f("mu_shift").reshape(SHW), "w0": f("w0").reshape(CW), "w2": f("w2").reshape(64, CW),
        "a0": f("a0").reshape(CW), "a2": f("a2").reshape(64, CW), "g2": f("g2").reshape(160, CW),
        "k_k": f("k_k").reshape(CW), "k_a": f("k_a").reshape(CW), "r_k": f("r_k").reshape(CW),
        "gn_w": f("gn_w").reshape(CW), "gn_b": f("gn_b").reshape(CW), "w_out": f("w_out").reshape(D, D),
        "norm_ffn": f("norm_ffn").reshape(D), "w_gate": f("w_gate").reshape(D, DFF),
        "w_up": f("w_up").reshape(D, DFF), "w_down": f("w_down").reshape(DFF, D),
        "norm_final": f("norm_final").reshape(1, D),
    }
    in_maps = []
    for c in range(NCORE):
        bq, half = c // 2, c % 2
        m = dict(shared)
        m["xown"] = np.ascontiguousarray(xp[bq, half * TOWN:(half + 1) * TOWN])
        m["xprev"] = np.ascontiguousarray(xp[bq, 0:TOWN]) if half == 1 else np.zeros((TOWN, D), np.float32)
        m["xs"] = np.ascontiguousarray(xsmp[c * NS:(c + 1) * NS, 0])
        m["st_pool"] = np.ascontiguousarray(sp[0, c * NS:(c + 1) * NS]).reshape(NS * 15, PW)
        m["st_shift"] = np.ascontiguousarray(ss[0, c * NS:(c + 1) * NS])
        m["st_wkv"] = np.ascontiguousarray(sw[0, c * NS:(c + 1) * NS])
        m["pos"] = (half * TOWN + np.arange(TOWN, dtype=np.float32)).reshape(1, TOWN)
        in_maps.append(m)
    if "nc" not in _NC_CACHE:
        _NC_CACHE["nc"] = build_program()
    res = run_bass_kernel_spmd(_NC_CACHE["nc"], in_maps, core_ids=list(range(NCORE)))
    R = res.results
    y_prompt = np.zeros((4, 2048, D), np.float32)
    for c in range(NCORE):
        y_prompt[c // 2, (c % 2) * TOWN:(c % 2 + 1) * TOWN] = R[c]["y_own"]
    y_sample = np.concatenate([R[c]["y_s"] for c in range(NCORE)], 0).reshape(128, 1, D)
    npp = np.stack([R[2 * q + 1]["pool_last"] for q in range(4)], 0)[None]
    nsp = np.stack([R[2 * q + 1]["shift_last"].reshape(SHW) for q in range(4)], 0)[None]
    nwp = np.stack([R[2 * q + 1]["wkv_fin"] for q in range(4)], 0)[None]
    nps = np.concatenate([R[c]["npool_s"] for c in range(NCORE)], 0)[None]
    nss = np.concatenate([R[c]["nshift_s"] for c in range(NCORE)], 0)[None]
    nws = np.concatenate([R[c]["nwkv_s"] for c in range(NCORE)], 0)[None]
    return (y_prompt, y_sample, npp.astype(np.float32), nsp.astype(np.float32), nwp.astype(np.float32),
            nps.astype(np.float32), nss.astype(np.float32), nws.astype(np.float32))
```

```python
import numpy as np
from contextlib import ExitStack
import concourse.bass as bass
import concourse.mybir as mybir
from concourse.bass_utils import run_bass_kernel_spmd

F32 = mybir.dt.float32
BF16 = mybir.dt.bfloat16
AF = mybir.ActivationFunctionType
ALU = mybir.AluOpType
AX = mybir.AxisListType

D = 2048
NCORE = 8
TOWN = 1024
NS = 16
PW = 1024
CW = 1024
SHW = 3360
PROJ = 4384
DFF = 5632
NFT = 44
CH = 128
DEC_K = 0.6065306597126334


class Sched:
    ENGS = ("sync", "act", "dve", "pool", "pe")

    def __init__(self, nc, sems, dma_sems):
        self.nc = nc
        self.sem = dict(zip(self.ENGS, sems))
        self.cnt = {e: 0 for e in self.ENGS}
        self.known = {e: {} for e in self.ENGS}
        self.prog = {e: [] for e in self.ENGS}
        self.lastw = {}
        self.readers = {}
        self.dma_sems = list(dma_sems)
        self.dma_next = 0
        self.dma_cnt = [0] * len(self.dma_sems)
        self.dma_pending = [None] * len(self.dma_sems)

    def _need(self, eng, dep):
        if dep is None:
            return
        if dep[0] == "e":
            _, f, n = dep
            if f == eng and eng in ("pe", "sync"):
                return
            key = ("e", f)
            sem = self.sem[f]
        else:
            _, i, n = dep
            key = ("d", i)
            sem = self.dma_sems[i]
        if self.known[eng].get(key, 0) >= n:
            return
        self.known[eng][key] = n
        self.prog[eng].append(lambda e, sem=sem, n=n: e.wait_ge(sem, n))

    def _deps(self, eng, reads, writes):
        for k in reads:
            self._need(eng, self.lastw.get(k))
        for k in writes:
            self._need(eng, self.lastw.get(k))
            for d in self.readers.get(k, ()):
                self._need(eng, d)

    def _commit(self, dep, reads, writes):
        for k in reads:
            self.readers.setdefault(k, []).append(dep)
        for k in writes:
            self.lastw[k] = dep
            self.readers[k] = []

    def op(self, eng, fn, reads=(), writes=(), inc=True):
        self._deps(eng, reads, writes)
        if inc:
            self.cnt[eng] += 1
            n = self.cnt[eng]
            sem = self.sem[eng]
            self.prog[eng].append(lambda e, fn=fn, sem=sem: fn(e).then_inc(sem, 1))
            dep = ("e", eng, n)
        else:
            self.prog[eng].append(lambda e, fn=fn: fn(e))
            dep = ("e", eng, self.cnt[eng] + 1)
        self._commit(dep, reads, writes)

    def dma(self, fn, reads=(), writes=(), eng="sync"):
        i = self.dma_next
        self.dma_next = (self.dma_next + 1) % len(self.dma_sems)
        if self.dma_pending[i] is not None:
            self._need(eng, self.dma_pending[i])
        self._deps(eng, reads, writes)
        self.dma_cnt[i] += 16
        v = self.dma_cnt[i]
        sem = self.dma_sems[i]
        self.prog[eng].append(lambda e, fn=fn, sem=sem: fn(e).then_inc(sem, 16))
        dep = ("d", i, v)
        self.dma_pending[i] = dep
        self._commit(dep, reads, writes)

    def barrier(self):
        snap = dict(self.cnt)
        pend = [d for d in self.dma_pending if d is not None]
        for e in self.ENGS:
            for d in pend:
                self._need(e, d)
            for f in self.ENGS:
                if f != e and snap[f] > 0:
                    self._need(e, ("e", f, snap[f]))

    def finish(self, eng="sync"):
        for d in self.dma_pending:
            self._need(eng, d)
        for f in self.ENGS:
            if f != eng and self.cnt[f] > 0:
                self._need(eng, ("e", f, self.cnt[f]))

    def emit(self, block):
        progs = self.prog

        @block.sync
        def _(e):
            for f in progs["sync"]:
                f(e)

        @block.scalar
        def _(e):
            for f in progs["act"]:
                f(e)

        @block.vector
        def _(e):
            for f in progs["dve"]:
                f(e)

        @block.gpsimd
        def _(e):
            for f in progs["pool"]:
                f(e)

        @block.tensor
        def _(e):
            for f in progs["pe"]:
                f(e)


def _nm(ap):
    return ap.tensor.name


class B:
    def __init__(self, S):
        self.S = S
        self.rr = 0

    def _k(self, aps):
        return [_nm(a) for a in aps if hasattr(a, "tensor")]

    def tt(self, eng, out, a, b, op):
        self.S.op(eng, lambda e: e.tensor_tensor(out=out, in0=a, in1=b, op=op), self._k([a, b]), self._k([out]))

    def ts(self, eng, out, a, s1, op0, s2=None, op1=None):
        if op1 is None:
            self.S.op(eng, lambda e: e.tensor_scalar(out=out, in0=a, scalar1=s1, scalar2=None, op0=op0),
                      self._k([a, s1]), self._k([out]))
        else:
            self.S.op(eng, lambda e: e.tensor_scalar(out=out, in0=a, scalar1=s1, scalar2=s2, op0=op0, op1=op1),
                      self._k([a, s1, s2]), self._k([out]))

    def stt(self, out, a, s, b, op0, op1, accum=None):
        if accum is None:
            self.S.op("dve", lambda e: e.scalar_tensor_tensor(out=out, in0=a, scalar=s, in1=b, op0=op0, op1=op1),
                      self._k([a, s, b]), self._k([out]))
        else:
            self.S.op("dve", lambda e: e.scalar_tensor_tensor(out=out, in0=a, scalar=s, in1=b, op0=op0, op1=op1,
                                                              accum_out=accum),
                      self._k([a, s, b]), self._k([out, accum]))

    def act(self, out, a, func, scale=None, bias=None):
        kw = {}
        if scale is not None:
            kw["scale"] = scale
        if bias is not None:
            kw["bias"] = bias
        self.S.op("act", lambda e: e.activation(out=out, in_=a, func=func, **kw),
                  self._k([a, scale, bias]), self._k([out]))

    def copy(self, eng, out, a):
        if eng == "act":
            self.act(out, a, AF.Copy)
        else:
            self.S.op(eng, lambda e: e.tensor_copy(out=out, in_=a), self._k([a]), self._k([out]))

    def rsqrt(self, out, a):
        self.act(out, a, AF.Sqrt)
        self.S.op("dve", lambda e: e.reciprocal(out=out, in_=out), self._k([out]), self._k([out]))

    def evac(self, out, a):
        self.rr ^= 1
        self.copy("act" if self.rr else "dve", out, a)

    def red(self, eng, out, a, op=ALU.add):
        self.S.op(eng, lambda e: e.tensor_reduce(out=out, in_=a, axis=AX.X, op=op), self._k([a]), self._k([out]))

    def memset(self, eng, ap, v):
        self.S.op(eng, lambda e: e.memset(ap, v), [], self._k([ap]))

    def mm(self, out, lhsT, rhs, start=True, stop=True, inc=None, tp=None):
        if inc is None:
            inc = stop
        kw = {}
        if tp is not None:
            kw["tile_position"] = tp
        self.S.op("pe", lambda e: e.matmul(out, lhsT=lhsT, rhs=rhs, start=start, stop=stop, **kw),
                  self._k([lhsT, rhs]), self._k([out]), inc=inc)

    def tr(self, out, a, ident, inc=True):
        self.S.op("pe", lambda e: e.transpose(out, a, ident), self._k([a, ident]), self._k([out]), inc=inc)

    def dma(self, out, a, eng="sync", slow=False):
        kw = {"allow_slow_non_contiguous": True} if slow else {}
        self.S.dma(lambda e: e.dma_start(out=out, in_=a, **kw), self._k([a]), self._k([out]), eng=eng)


def build_program(stop=None):
    nc = bass.Bass("TRN2", target_bir_lowering=False)

    def din(name, shape):
        return nc.dram_tensor(name, list(shape), F32, kind="ExternalInput").ap()

    def dout(name, shape):
        return nc.dram_tensor(name, list(shape), F32, kind="ExternalOutput").ap()

    xprev = din("xprev", [TOWN, D])
    xown = din("xown", [TOWN, D])
    xs = din("xs", [NS, D])
    st_pool = din("st_pool", [NS * 15, PW])
    st_pool3 = st_pool.rearrange("(b r) c -> b r c", r=15)
    st_shift = din("st_shift", [NS, SHW])
    st_wkv = din("st_wkv", [NS, 16, 64, 64])
    pos = din("pos", [1, TOWN])
    norm_mix = din("norm_mix", [D])
    w_in = din("w_in", [D, PROJ])
    w_pool = din("w_pool", [4, 256, 256])
    pool_scale = din("pool_scale", [PW])
    mu_shift = din("mu_shift", [SHW])
    w0 = din("w0", [CW])
    w2 = din("w2", [64, CW])
    a0 = din("a0", [CW])
    a2 = din("a2", [64, CW])
    g2 = din("g2", [160, CW])
    k_k = din("k_k", [CW])
    k_a = din("k_a", [CW])
    r_k = din("r_k", [CW])
    gn_w = din("gn_w", [CW])
    gn_b = din("gn_b", [CW])
    w_out = din("w_out", [D, D])
    norm_ffn = din("norm_ffn", [D])
    w_gate = din("w_gate", [D, DFF])
    w_up = din("w_up", [D, DFF])
    w_down = din("w_down", [DFF, D])
    norm_final = din("norm_final", [1, D])

    y_own = dout("y_own", [TOWN, D])
    y_s = dout("y_s", [NS, D])
    pool_last = dout("pool_last", [15, PW])
    shift_last = dout("shift_last", [1, SHW])
    wkv_fin = dout("wkv_fin", [16, 64, 64])
    npool_s = dout("npool_s", [NS, 15, PW])
    nshift_s = dout("nshift_s", [NS, SHW])
    nwkv_s = dout("nwkv_s", [NS, 16, 64, 64])

    es = ExitStack()
    with es:
        def sb(name, shape, dt=F32):
            return es.enter_context(nc.sbuf_tensor(name, list(shape), dt))

        sems = [es.enter_context(nc.semaphore(f"se{i}")) for i in range(5)]
        dsems = [es.enter_context(nc.semaphore(f"sd{i}")) for i in range(24)]
        S = Sched(nc, sems, dsems)
        b = B(S)

        dbg_n = [0]

        def dump(name, ap, shape):
            o = nc.dram_tensor("dbg_" + name, list(shape), F32, kind="ExternalOutput").ap()
            b.dma(o, ap)

        def done():
            S.finish()
            with nc.Block() as block:
                S.emit(block)
            return nc
        pbank = [es.enter_context(nc.psum_tensor(f"pb{i}", [128, 512], F32)) for i in range(8)]

        def pbf(i):
            return pbank[i][:].bitcast(BF16)

        ident = sb("ident", [128, 128])
        identb = sb("identb", [128, 128], BF16)
        blk = sb("blk", [128, 128])
        mask4 = sb("mask4", [128, 512])
        maskl = sb("maskl", [128, 256])
        b.memset("pool", ident[:], 1.0)
        S.op("pool", lambda e: e.affine_select(out=ident[:], in_=ident[:], pattern=[[-1, 128]], compare_op=ALU.is_equal,
                                               fill=0.0, base=0, channel_multiplier=1), ["ident"], ["ident"])
        b.copy("dve", identb[:], ident[:])
        b.memset("pool", blk[:], 0.0)
        b.memset("pool", blk[0:64, 0:64], 1.0)
        b.memset("pool", blk[64:128, 64:128], 1.0)
        b.memset("pool", mask4[:], 1.0)
        for q in range(4):
            cmp = ALU.is_gt if q % 2 == 0 else ALU.is_ge
            S.op("pool", lambda e, q=q, cmp=cmp: e.affine_select(
                out=mask4[:, q * 128:(q + 1) * 128], in_=mask4[:, q * 128:(q + 1) * 128], pattern=[[1, 128]],
                compare_op=cmp, fill=0.0, base=0, channel_multiplier=-1), ["mask4"], ["mask4"])
        b.memset("pool", maskl[:], 1.0)
        for q in range(2):
            S.op("pool", lambda e, q=q: e.affine_select(
                out=maskl[:, q * 128:(q + 1) * 128], in_=maskl[:, q * 128:(q + 1) * 128], pattern=[[-1, 128]],
                compare_op=ALU.is_gt, fill=0.0, base=0, channel_multiplier=1), ["maskl"], ["maskl"])

        def colparam(name, src, n):
            nt = (n + 127) // 128
            t = sb(name, [128, nt])
            nfull = n // 128
            if nfull:
                b.dma(t[:, 0:nfull], src[0:nfull * 128].rearrange("(t p) -> p t", p=128), slow=True)
            rem = n - nfull * 128
            if rem:
                b.dma(t[0:rem, nfull:nfull + 1], src[nfull * 128:n].rearrange("(t p) -> p t", p=rem), slow=True)
            return t

        g_mix = colparam("g_mix", norm_mix, D)
        g_ffn = colparam("g_ffn", norm_ffn, D)
        pscale = colparam("pscale", pool_scale, PW)
        mu_c = colparam("mu_c", mu_shift, SHW)
        w0_c = colparam("w0_c", w0, CW)
        a0_c = colparam("a0_c", a0, CW)
        kk_c = colparam("kk_c", k_k, CW)
        ka_c = colparam("ka_c", k_a, CW)
        rk_c = colparam("rk_c", r_k, CW)
        gnw_c = colparam("gnw_c", gn_w, CW)
        gnb_c = colparam("gnb_c", gn_b, CW)
        w2b = sb("w2b", [128, CW], BF16)
        g2b = sb("g2b", [128, CW], BF16)
        g2b2 = sb("g2b2", [32, CW], BF16)
        b.dma(w2b[0:64, :], w2[:, :], eng="pool")
        b.dma(w2b[64:128, :], a2[:, :], eng="pool")
        b.dma(g2b[:], g2[0:128, :], eng="pool")
        b.dma(g2b2[:], g2[128:160, :], eng="pool")
        wpb = sb("wpb", [128, 8, 256], BF16)
        b.dma(wpb[:], w_pool.rearrange("g (cc p) d -> p (g cc) d", p=128), eng="pool")

        eps_rms = 1e-6

        mixT = [sb(f"AT{i}", [128, 16, 512], BF16) for i in range(2)]
        mixTs = sb("ATs", [128, 16, NS], BF16)
        STt = [sb(f"ST{hp}", [128, 64]) for hp in range(8)]
        STb = [sb(f"STb{hp}", [128, 128], BF16) for hp in range(8)]
        carry = sb("carry", [128, 27])
        PL = sb("PL", [128, 27, 17])
        UL = sb("UL", [128, 8, 31])
        halo = sb("halo", [128, 8, 16])
        for hp in range(8):
            b.memset("pool", STt[hp][:], 0.0)
            b.memset("pool", STb[hp][:], 0.0)
        b.memset("pool", carry[:], 0.0)

        def norm_to_T_g(nbufs, src_ap, nrow, g_c, dst, dst_off, slot, src_is_sbuf=False):
            xin, xsq, xbf, ssq, rstd = nbufs
            if src_is_sbuf:
                xi = src_ap
            else:
                xi = xin[slot][0:nrow, :]
                b.dma(xi, src_ap)
            b.stt(xbf[slot][0:nrow, :], xi, 1.0, xi, ALU.mult, ALU.mult, accum=ssq[0:nrow, slot:slot + 1])
            b.ts("dve", rstd[0:nrow, slot:slot + 1], ssq[0:nrow, slot:slot + 1], 1.0 / D, ALU.mult, eps_rms, ALU.add)
            b.rsqrt(rstd[0:nrow, slot:slot + 1], rstd[0:nrow, slot:slot + 1])
            b.act(xbf[slot][0:nrow, :], xi, AF.Copy, scale=rstd[0:nrow, slot:slot + 1])
            for q in range(4):
                pb = pbf(q % 2)
                for j in range(4):
                    dk = q * 4 + j
                    b.tr(pb[:, j * 128:j * 128 + nrow], xbf[slot][0:nrow, dk * 128:(dk + 1) * 128],
                         identb[0:nrow, 0:nrow], inc=(j == 3))
                for j in range(4):
                    dk = q * 4 + j
                    if j % 2 == 0:
                        b.act(dst[:, dk, dst_off:dst_off + nrow], pb[:, j * 128:j * 128 + nrow], AF.Copy,
                              scale=g_c[:, dk:dk + 1])
                    else:
                        b.ts("dve", dst[:, dk, dst_off:dst_off + nrow], pb[:, j * 128:j * 128 + nrow],
                             g_c[:, dk:dk + 1], ALU.mult)

        if stop == "A":
            return done()
        mixer_scope = ExitStack()
        with mixer_scope:
            def sbm(name, shape, dt=F32):
                return mixer_scope.enter_context(nc.sbuf_tensor(name, list(shape), dt))

            hT = [sbm(f"hT{i}", [128, 16, 512], BF16) for i in range(2)]
            hTs = sbm("hTs", [128, 16, NS], BF16)
            xin = [sbm("xin0", [128, D]), sbm("xin1", [128, D])]
            xsq = None
            xbf = [sbm("xbf0", [128, D], BF16)] * 2
            ssq = sbm("ssq", [128, 2])
            rstd = sbm("rstd", [128, 2])
            nbufs = (xin, xsq, xbf, ssq, rstd)

            def norm_to_T(*a, **k):
                return norm_to_T_g(nbufs, *a, **k)

            def _unused(src_ap, nrow, g_c, dst, dst_off, slot, src_eng="sync", src_is_sbuf=False):
                if src_is_sbuf:
                    xi = src_ap
                else:
                    xi = xin[slot][0:nrow, :]
                    b.dma(xi, src_ap)
                b.tt("dve", xsq[0:nrow, :], xi, xi, ALU.mult)
                b.red("dve", ssq[0:nrow, slot:slot + 1], xsq[0:nrow, :])
                b.ts("dve", rstd[0:nrow, slot:slot + 1], ssq[0:nrow, slot:slot + 1], 1.0 / D, ALU.mult, eps_rms, ALU.add)
                b.rsqrt(rstd[0:nrow, slot:slot + 1], rstd[0:nrow, slot:slot + 1])
                b.act(xbf[slot][0:nrow, :], xi, AF.Copy, scale=rstd[0:nrow, slot:slot + 1])
                for q in range(4):
                    pb = pbf(q % 2)
                    for j in range(4):
                        dk = q * 4 + j
                        b.tr(pb[:, j * 128:j * 128 + nrow], xbf[slot][0:nrow, dk * 128:(dk + 1) * 128],
                             identb[0:nrow, 0:nrow], inc=(j == 3))
                    for j in range(4):
                        dk = q * 4 + j
                        if j % 2 == 0:
                            b.act(dst[:, dk, dst_off:dst_off + nrow], pb[:, j * 128:j * 128 + nrow], AF.Copy,
                                  scale=g_c[:, dk:dk + 1])
                        else:
                            b.ts("dve", dst[:, dk, dst_off:dst_off + nrow], pb[:, j * 128:j * 128 + nrow],
                                 g_c[:, dk:dk + 1], ALU.mult)

            wbuf = [sbm(f"wbuf{i}", [128, 16, 128], BF16) for i in range(4)]
            w_in_v = w_in.rearrange("(kc p) n -> p kc n", p=128)
            wctr = [0]

            def load_wcol(c0, ncol):
                t = wbuf[wctr[0] % 4]
                wctr[0] += 1
                b.dma(t[:, :, 0:ncol], w_in_v[:, :, c0:c0 + ncol], eng="pool")
                return t

            def proj(wt, ncol, rhs_tile, n, pout, off=0):
                for dk in range(16):
                    b.mm(pout[0:ncol, 0:n], wt[:, dk, 0:ncol], rhs_tile[:, dk, off:off + n], start=(dk == 0), stop=(dk == 15))

            XA = {nm: sbm(f"XA_{nm}", [128, 128]) for nm in ("r", "k", "v", "a", "b", "w")}
            bonus_s = sbm("bonus_s", [128, 8, NS])
            gate_s = sbm("gate_s", [128, 8, NS])
            SH = sbm("SH", [128, 27, NS])
            Ysf = sbm("Ysf", [128, 128])
            hpstack = mixer_scope.enter_context(ExitStack())

            def sbw(name, shape, dt=F32):
                return hpstack.enter_context(nc.sbuf_tensor(name, list(shape), dt))

            NT = 512
            Pb = [sbw(f"Pb{i}", [128, NT + 1]) for i in range(3)]
            dtmp = sbw("dtmp", [128, NT])
            Rt = sbw("Rt", [128, NT])
            K0 = sbw("K0", [128, NT])
            Vt = sbw("Vt", [128, NT])
            Kt = sbw("Kt", [128, NT])
            sig = sbw("sig", [128, NT])
            alr = sbw("alr", [128, NT])
            kkn = sbw("kkn", [128, NT])
            t1 = sbw("t1", [128, NT])
            t2 = sbw("t2", [128, NT])
            csig = sbw("csig", [128, NT])
            onesf = sbw("onesf", [128, NT])
            b.memset("pool", onesf[:], 1.0)
            gb = sbw("gb", [128, 4])
            gend = sbw("gend", [128, 4])
            ECt = sbw("ECt", [128, 4])
            ex = sbw("ex", [128, NT])
            rtb = sbw("rtb", [128, NT], BF16)
            atb = sbw("atb", [128, NT], BF16)
            ktb = sbw("ktb", [128, NT], BF16)
            btb = sbw("btb", [128, NT], BF16)
            Khb = sbw("Khb", [128, NT], BF16)
            Bhb = sbw("Bhb", [128, NT], BF16)
            Vb = sbw("Vb", [128, NT], BF16)
            bonus = sbw("bonus", [128, NT])
            gate = sbw("gate", [128, NT])
            Yt = dtmp
            wab = [sbw(f"wab{i}", [128, NT if i < 2 else NS], BF16) for i in range(3)]
            sgb = [sbw(f"sgb{i}", [128, NT if i < 2 else NS], BF16) for i in range(3)]
            sgb2 = [sbw(f"sgb2{i}", [32, NT if i < 2 else NS], BF16) for i in range(3)]
            AM = [sbw(f"AM{h}", [128, 512], BF16) for h in range(2)]
            LV = [sbw(f"LV{i}", [128, 2, 384], BF16) for i in range(2)]
            TM = sbw("TM", [128, 2, 4, 128], BF16)
            WT = sbw("WT", [128, 128], BF16)
            Ub = sbw("Ub", [128, 2, 128], BF16)
            b.memset("pool", TM[:].rearrange("p h q d -> p (h q d)"), 0.0)
            b.memset("pool", Ub[:].rearrange("p h d -> p (h d)"), 0.0)
            AM_b = [sbw(f"AMb{h}", [128, 512], BF16) for h in range(2)]
            LV_b = [sbw(f"LVb{i}", [128, 2, 384], BF16) for i in range(2)]
            TM_b = sbw("TMb", [128, 2, 4, 128], BF16)
            WT_b = sbw("WTb", [128, 128], BF16)
            Ub_b = sbw("Ubb", [128, 2, 128], BF16)
            b.memset("pool", TM_b[:].rearrange("p h q d -> p (h q d)"), 0.0)
            b.memset("pool", Ub_b[:].rearrange("p h d -> p (h d)"), 0.0)
            slots = [dict(AM=AM, LV=LV, TM=TM, WT=WT, Ub=Ub, pX=pbank[5], pY2=pbank[6]),
                     dict(AM=AM_b, LV=LV_b, TM=TM_b, WT=WT_b, Ub=Ub_b, pX=pbank[3], pY2=pbank[4])]

            def shift_mix(pb_t, ncol, n, mu_col, out_ap):
                b.tt("dve", dtmp[0:ncol, 0:n], pb_t[0:ncol, 0:n], pb_t[0:ncol, 1:n + 1], ALU.subtract)
                b.stt(out_ap, dtmp[0:ncol, 0:n], mu_col, pb_t[0:ncol, 1:n + 1], ALU.mult, ALU.add)

            def run_pass(is_own):
                tiles = [("p", 0, 512), ("p", 1, 512)]
                if is_own:
                    tiles.append(("s", 0, NS))
                src = xown if is_own else xprev
                for tt_i in range(2):
                    for blk_i in range(4):
                        r0 = tt_i * 512 + blk_i * 128
                        norm_to_T(src[r0:r0 + 128, :], 128, g_mix, hT[tt_i], blk_i * 128, blk_i % 2)
                if is_own:
                    norm_to_T(xs[:, :], NS, g_mix, hTs, 0, 0)

                if stop == "P0":
                    return True

                def rhs_of(tl):
                    return hTs if tl[0] == "s" else hT[tl[1]]

                for li, (ctile, ncol) in enumerate(((24, 128), (25, 128), (26, 32))):
                    wt = load_wcol(PW + ctile * 128, ncol)
                    for ti, tl in enumerate(tiles):
                        n = tl[2]
                        pp = pbank[2 + (ti % 2)]
                        proj(wt, ncol, rhs_of(tl), n, pp)
                        mu_col = mu_c[0:ncol, ctile:ctile + 1]
                        if tl[0] == "p":
                            pbt = Pb[li % 3]
                            b.copy("dve", pbt[0:ncol, 0:1], carry[0:ncol, ctile:ctile + 1])
                            b.evac(pbt[0:ncol, 1:n + 1], pp[0:ncol, 0:n])
                            b.copy("dve", carry[0:ncol, ctile:ctile + 1], pbt[0:ncol, n:n + 1])
                            if is_own and tl[1] == 1:
                                b.copy("dve", PL[0:ncol, ctile, 16:17], pbt[0:ncol, n:n + 1])
                            shift_mix(pbt, ncol, n, mu_col, t1[0:ncol, 0:n])
                        else:
                            b.evac(PL[0:ncol, ctile, 0:NS], pp[0:ncol, 0:n])
                            b.tt("dve", dtmp[0:ncol, 0:n], SH[0:ncol, ctile, :], PL[0:ncol, ctile, 0:NS], ALU.subtract)
                            b.stt(t1[0:ncol, 0:n], dtmp[0:ncol, 0:n], mu_col, PL[0:ncol, ctile, 0:NS], ALU.mult, ALU.add)
                        if li == 0:
                            b.act(wab[ti][0:64, 0:n], t1[0:64, 0:n], AF.Tanh)
                            b.copy("act", wab[ti][64:128, 0:n], t1[64:128, 0:n])
                        elif li == 1:
                            b.act(sgb[ti][:, 0:n], t1[:, 0:n], AF.Sigmoid)
                        else:
                            b.act(sgb2[ti][:, 0:n], t1[0:32, 0:n], AF.Sigmoid)

                if stop == "P1":
                    return True
                wts_of = {}

                def item(hp, ti, tl):
                    n = tl[2]
                    rhs_t = rhs_of(tl)
                    outs = [Rt, K0, Vt]
                    cs_ = slice(hp * 128, (hp + 1) * 128)
                    if ti == 0:
                        wts_of[hp] = [load_wcol(PW + q * CW + hp * 128, 128) for q in range(3)]
                    wts = wts_of[hp]
                    for q in range(3):
                        proj(wts[q], 128, rhs_t, n, pbank[2 + q])
                    yield
                    for q in range(3):
                        ctile = q * 8 + hp
                        pp = pbank[2 + q]
                        if tl[0] == "p":
                            pbt = Pb[q]
                            b.copy("dve", pbt[:, 0:1], carry[:, ctile:ctile + 1])
                            b.evac(pbt[:, 1:n + 1], pp[:, 0:n])
                            b.copy("dve", carry[:, ctile:ctile + 1], pbt[:, n:n + 1])
                            if is_own and tl[1] == 1:
                                b.copy("dve", PL[:, ctile, 16:17], pbt[:, n:n + 1])
                        else:
                            b.evac(PL[:, ctile, 0:NS], pp[:, 0:n])
                    yield
                    for q in range(3):
                        ctile = q * 8 + hp
                        if tl[0] == "p":
                            shift_mix(Pb[q], 128, n, mu_c[:, ctile:ctile + 1], outs[q][:, 0:n])
                        else:
                            b.tt("dve", dtmp[:, 0:n], SH[:, ctile, :], PL[:, ctile, 0:NS], ALU.subtract)
                            b.stt(outs[q][:, 0:n], dtmp[:, 0:n], mu_c[:, ctile:ctile + 1], PL[:, ctile, 0:NS],
                                  ALU.mult, ALU.add)
                    pw, pa, pg = pbank[5], pbank[6], pbank[7]
                    b.mm(pw[:, 0:n], w2b[0:64, cs_], wab[ti][0:64, 0:n])
                    b.mm(pa[:, 0:n], w2b[64:128, cs_], wab[ti][64:128, 0:n])
                    b.mm(pg[:, 0:n], g2b[:, cs_], sgb[ti][:, 0:n], start=True, stop=False)
                    b.mm(pg[:, 0:n], g2b2[:, cs_], sgb2[ti][:, 0:n], start=False, stop=True)
                    b.act(sig[:, 0:n], pw[:, 0:n], AF.Sigmoid, bias=w0_c[:, hp:hp + 1])
                    b.act(alr[:, 0:n], pa[:, 0:n], AF.Sigmoid, bias=a0_c[:, hp:hp + 1])
                    b.copy("act", gate[:, 0:n], pg[:, 0:n])
                    b.ts("dve", kkn[:, 0:n], K0[:, 0:n], kk_c[:, hp:hp + 1], ALU.mult)
                    b.tt("dve", t1[:, 0:n], kkn[:, 0:n], kkn[:, 0:n], ALU.mult)
                    b.mm(pw[:, 0:n], blk[:], t1[:, 0:n])
                    b.ts("dve", t2[:, 0:n], pw[:, 0:n], 1e-24, ALU.max)
                    b.rsqrt(t2[:, 0:n], t2[:, 0:n])
                    b.tt("dve", kkn[:, 0:n], kkn[:, 0:n], t2[:, 0:n], ALU.mult)
                    b.ts("dve", t1[:, 0:n], alr[:, 0:n], -1.0, ALU.add, ka_c[:, hp:hp + 1], ALU.mult)
                    b.stt(Kt[:, 0:n], t1[:, 0:n], 1.0, K0[:, 0:n], ALU.add, ALU.mult)
                    b.stt(t1[:, 0:n], Rt[:, 0:n], rk_c[:, hp:hp + 1], Kt[:, 0:n], ALU.mult, ALU.mult)
                    b.mm(pa[:, 0:n], blk[:], t1[:, 0:n])
                    yield
                    b.tt("dve", bonus[:, 0:n], pa[:, 0:n], Vt[:, 0:n], ALU.mult)
                    b.tt("dve", t2[:, 0:n], kkn[:, 0:n], alr[:, 0:n], ALU.mult)
                    if tl[0] == "s":
                        def stash(nm, ap_):
                            b.copy("dve", XA[nm][:].rearrange("p (b h) -> p b h", h=8)[:, :, hp], ap_)
                        b.act(t1[:, 0:n], sig[:, 0:n], AF.Exp, scale=-DEC_K)
                        stash("w", t1[:, 0:n])
                        stash("r", Rt[:, 0:n])
                        stash("k", Kt[:, 0:n])
                        stash("v", Vt[:, 0:n])
                        b.ts("dve", t1[:, 0:n], kkn[:, 0:n], -1.0, ALU.mult)
                        stash("a", t1[:, 0:n])
                        stash("b", t2[:, 0:n])
                        b.copy("dve", bonus_s[:, hp, :], bonus[:, 0:n])
                        b.copy("dve", gate_s[:, hp, :], gate[:, 0:n])
                        return
                    nch = n // CH
                    S.op("dve", lambda e, n=n: e.tensor_tensor_scan(out=csig[:, 0:n], data0=onesf[:, 0:n], data1=sig[:, 0:n],
                                                                    initial=0.0, op0=ALU.mult, op1=ALU.add),
                         ["onesf", "sig"], ["csig"])
                    b.memset("dve", gb[:, 0:1], 0.0)
                    c3 = csig[:, 0:n].rearrange("p (c t) -> p c t", t=CH)
                    if nch > 1:
                        b.copy("dve", gb[:, 1:nch], c3[:, 0:nch - 1, CH - 1])
                    b.tt("dve", c3, c3, gb[:, 0:nch].unsqueeze(2).to_broadcast([128, nch, CH]), ALU.subtract)
                    b.copy("dve", gend[:, 0:nch], c3[:, :, CH - 1])
                    b.act(ECt[:, 0:nch], gend[:, 0:nch], AF.Exp, scale=-DEC_K)
                    b.act(ex[:, 0:n], csig[:, 0:n], AF.Exp, scale=-DEC_K)
                    b.tt("dve", rtb[:, 0:n], Rt[:, 0:n], ex[:, 0:n], ALU.mult)
                    b.act(ex[:, 0:n], csig[:, 0:n], AF.Exp, scale=DEC_K)
                    b.tt("dve", ktb[:, 0:n], Kt[:, 0:n], ex[:, 0:n], ALU.mult)
                    b.tt("dve", btb[:, 0:n], t2[:, 0:n], ex[:, 0:n], ALU.mult)
                    b.tt("dve", t1[:, 0:n], csig[:, 0:n], sig[:, 0:n], ALU.subtract)
                    b.act(ex[:, 0:n], t1[:, 0:n], AF.Exp, scale=-DEC_K)
                    b.stt(atb[:, 0:n], kkn[:, 0:n], -1.0, ex[:, 0:n], ALU.mult, ALU.mult)
                    t13 = t1[:, 0:n].rearrange("p (c t) -> p c t", t=CH)
                    b.tt("dve", t13, gend[:, 0:nch].unsqueeze(2).to_broadcast([128, nch, CH]), c3, ALU.subtract)
                    b.act(ex[:, 0:n], t1[:, 0:n], AF.Exp, scale=-DEC_K)
                    b.tt("dve", Khb[:, 0:n], Kt[:, 0:n], ex[:, 0:n], ALU.mult)
                    b.tt("dve", Bhb[:, 0:n], t2[:, 0:n], ex[:, 0:n], ALU.mult)
                    b.copy("act", Vb[:, 0:n], Vt[:, 0:n])
                    yield

                    def stageA(sl, c):
                        cs = slice(c * CH, (c + 1) * CH)
                        AMs, LVs, TMs, WTs = sl["AM"], sl["LV"], sl["TM"], sl["WT"]
                        pX, pY2 = sl["pX"], sl["pY2"]
                        pA = [pbank[0], pbank[1]]
                        pZb = pbf(7)
                        for h in range(2):
                            ph = slice(64 * h, 64 * h + 64)
                            b.mm(pA[h][:, 0:128], btb[ph, cs], atb[ph, cs])
                            b.mm(pA[h][:, 128:256], btb[ph, cs], rtb[ph, cs])
                            b.mm(pA[h][:, 256:384], ktb[ph, cs], atb[ph, cs])
                            b.mm(pA[h][:, 384:512], ktb[ph, cs], rtb[ph, cs])
                            b.mm(pX[:, h * 128:(h + 1) * 128], atb[ph, cs], btb[ph, cs])
                            for q, srcT in enumerate((atb, Vb, Khb, Bhb)):
                                b.tr(pZb[:, h * 256 + q * 64:h * 256 + (q + 1) * 64], srcT[ph, cs], identb[ph, ph],
                                     inc=(q == 3))
                        for h in range(2):
                            b.tt("dve", AMs[h][:], pA[h][:], mask4[:], ALU.mult)
                        b.tt("dve", LVs[0][:, :, 128:256], pX[:, 0:256].rearrange("p (h j) -> p h j", h=2),
                             maskl[:].rearrange("p (h j) -> p h j", h=2), ALU.mult)
                        for h in range(2):
                            b.copy("act", TMs[:, h, :, 64 * h:64 * h + 64],
                                   pZb[:, h * 256:(h + 1) * 256].rearrange("p (q d) -> p q d", q=4))
                        yield
                        for h in range(2):
                            b.copy("act", LVs[0][:, h, 256:384], AMs[h][:, 0:128])
                            b.copy("dve", LVs[0][:, h, 64 * h:64 * h + 64], TMs[:, h, 0, 64 * h:64 * h + 64])
                            b.mm(pY2[:, h * 64:(h + 1) * 64], AMs[h][:, 256:384], TMs[:, h, 1, 64 * h:64 * h + 64])
                        yield
                        for h in range(2):
                            o = 64 * (1 - h)
                            b.copy("dve", LVs[0][:, h, o:o + 64], pY2[:, h * 64:(h + 1) * 64])
                        yield
                        cur = 0
                        for lvl in range(7):
                            nxt = 1 - cur
                            for h in range(2):
                                b.mm(pX[:, h * 128:(h + 1) * 128], LVs[cur][:, h, 256:384], LVs[cur][:, h, 0:128])
                            if lvl < 6:
                                for h in range(2):
                                    b.mm(pY2[:, h * 256:h * 256 + 128], LVs[cur][:, h, 256:384], LVs[cur][:, h, 128:256])
                                    b.mm(pY2[:, h * 256 + 128:h * 256 + 256], LVs[cur][:, h, 128:256], LVs[cur][:, h, 256:384])
                            yield
                            b.tt("dve", LVs[nxt][:, :, 0:128], pX[:, 0:256].rearrange("p (h j) -> p h j", h=2),
                                 LVs[cur][:, :, 0:128], ALU.add)
                            if lvl < 6:
                                b.copy("act", LVs[nxt][:, :, 128:384], pY2[:, 0:512].rearrange("p (h j) -> p h j", h=2))
                            yield
                            cur = nxt
                        sl["Zf"] = LVs[cur]
                        for h in range(2):
                            b.tr(pZb[:, h * 128:(h + 1) * 128], LVs[cur][:, h, 0:128], identb[:])
                        for h in range(2):
                            ph = slice(64 * h, 64 * h + 64)
                            b.copy("act", WTs[ph, :], pZb[ph, h * 128:(h + 1) * 128])
                        yield

                    def stageB(sl, c):
                        cs = slice(c * CH, (c + 1) * CH)
                        AMs, TMs, WTs, Ubs, Zf = sl["AM"], sl["TM"], sl["WT"], sl["Ub"], sl["Zf"]
                        pY2 = sl["pY2"]
                        pA = [pbank[0], pbank[1]]
                        for h in range(2):
                            ph = slice(64 * h, 64 * h + 64)
                            b.mm(pA[h][:, 0:64], WTs[ph, :], STb[hp][ph, 64 * h:64 * h + 64])
                        for h in range(2):
                            o = 64 * (1 - h)
                            b.tt("dve", Ubs[:, h, 64 * h:64 * h + 64], pA[h][:, 0:64], Zf[:, h, o:o + 64], ALU.add)
                        yield
                        if is_own:
                            b.mm(pY2[:, 0:128], STb[hp][:], rtb[:, cs], start=True, stop=False)
                            for h in range(2):
                                b.mm(pY2[:, 0:128], Ubs[:, h, :], AMs[h][:, 128:256], start=False, stop=False)
                                b.mm(pY2[:, 0:128], TMs[:, h, 1, :], AMs[h][:, 384:512], start=False, stop=(h == 1))
                        for h in range(2):
                            hs = slice(64 * h, 64 * h + 64)
                            b.mm(pY2[:, 128:192], TMs[:, h, 3, :], Ubs[:, h, hs], start=(h == 0), stop=False)
                            b.mm(pY2[:, 128:192], TMs[:, h, 2, :], TMs[:, h, 1, hs], start=False, stop=(h == 1))
                        yield
                        if is_own:
                            b.copy("act", Yt[:, cs], pY2[:, 0:128])
                        b.ts("dve", STt[hp][:], STt[hp][:], ECt[:, c:c + 1], ALU.mult)
                        b.copy("act", ex[:, 0:64], pY2[:, 128:192])
                        b.tt("dve", STt[hp][:], ex[:, 0:64], STt[hp][:], ALU.add)
                        b.copy("act", STb[hp][0:64, 0:64], STt[hp][0:64, :])
                        b.copy("act", STb[hp][64:128, 64:128], STt[hp][64:128, :])

                    def chunk_gen(c):
                        sl = slots[c % 2]
                        for _ in stageA(sl, c):
                            yield "A"
                        yield "B?"
                        for _ in stageB(sl, c):
                            yield "B"

                    cur_g = chunk_gen(0)
                    nxt_g = chunk_gen(1) if nch > 1 else None
                    nxt_c = 1
                    nxt_waiting = False
                    while cur_g is not None:
                        try:
                            next(cur_g)
                            cur_alive = True
                        except StopIteration:
                            cur_alive = False
                        if not cur_alive:
                            cur_g, nxt_waiting = nxt_g, False
                            nxt_c += 1
                            nxt_g = chunk_gen(nxt_c) if (cur_g is not None and nxt_c < nch) else None
                            continue
                        if nxt_g is not None and not nxt_waiting:
                            if next(nxt_g) == "B?":
                                nxt_waiting = True
                    if is_own:
                        post(hp, Yt[:, 0:n], bonus[:, 0:n], gate[:, 0:n], mixT[tl[1]][:, 8 + hp, 0:n], n)

                its = [(hp, ti, tl) for hp in range(8) for ti, tl in enumerate(tiles)]
                gens = [item(*it) for it in its]

                def step(g_):
                    try:
                        next(g_)
                    except StopIteration:
                        pass

                step(gens[0])
                step(gens[0])
                for i in range(len(its)):
                    step(gens[i])
                    if i + 1 < len(its):
                        step(gens[i + 1])
                    step(gens[i])
                    if i + 1 < len(its):
                        step(gens[i + 1])
                    for _ in gens[i]:
                        pass

            def post(hp, Y, bon, gt, out_ap, n):
                pm_, pv_ = pbank[3], pbank[4]
                b.mm(pm_[:, 0:n], blk[:], Y)
                b.ts("dve", t1[:, 0:n], pm_[:, 0:n], -1.0 / 64.0, ALU.mult)
                b.tt("dve", t1[:, 0:n], t1[:, 0:n], Y, ALU.add)
                b.tt("dve", t2[:, 0:n], t1[:, 0:n], t1[:, 0:n], ALU.mult)
                b.mm(pv_[:, 0:n], blk[:], t2[:, 0:n])
                b.ts("dve", t2[:, 0:n], pv_[:, 0:n], 1.0 / 64.0, ALU.mult, 64e-5, ALU.add)
                b.rsqrt(t2[:, 0:n], t2[:, 0:n])
                b.tt("dve", t1[:, 0:n], t1[:, 0:n], t2[:, 0:n], ALU.mult)
                b.ts("dve", t1[:, 0:n], t1[:, 0:n], gnw_c[:, hp:hp + 1], ALU.mult, gnb_c[:, hp:hp + 1], ALU.add)
                b.tt("dve", t1[:, 0:n], t1[:, 0:n], bon, ALU.add)
                b.tt("dve", out_ap, t1[:, 0:n], gt, ALU.mult)

            with ExitStack() as tsc:
                shtm = tsc.enter_context(nc.sbuf_tensor("shtm", [NS, SHW], F32))
                b.dma(shtm[:], st_shift[:, :])
                for ct in range(27):
                    ncl = min(128, SHW - ct * 128)
                    pp = pbank[2 + ct % 2]
                    b.tr(pp[0:ncl, 0:NS], shtm[:, ct * 128:ct * 128 + ncl], ident[0:NS, 0:NS])
                    b.evac(SH[0:ncl, ct, :], pp[0:ncl, 0:NS])
                S.barrier()

            if stop == "B":
                return done()
            if run_pass(False):
                return done()
            if stop == "C":
                return done()
            for ct in range(8):
                wt = load_wcol(ct * 128, 128)
                pp = pbank[2 + ct % 2]
                proj(wt, 128, hT[1], 16, pp, off=496)
                b.evac(halo[:, ct, :], pp[:, 0:16])
            if stop == "D":
                return done()
            if run_pass(True):
                return done()
            if stop == "E":
                dump("PL", PL[:].rearrange("p c t -> p (c t)"), [128, 27 * 17])
                dump("ST0", STt[0][:], [128, 64])
                dump("ST3", STt[3][:], [128, 64])
                cpm = sbw("cpm", [128, 1024])
                b.copy("dve", cpm[:, 0:512], mixT[0][:, 8, :])
                b.copy("dve", cpm[:, 512:1024], mixT[0][:, 11, :])
                dump("mix", cpm[:], [128, 1024])
                return done()
            wfin = sbw("wfin", [64, 16, 64])
            for hp in range(8):
                pp = pbank[2 + hp % 2]
                b.tr(pp[0:64, 0:128], STt[hp][:], ident[:])
                b.evac(wfin[:, 2 * hp:2 * hp + 2, :].rearrange("p h k -> p (h k)"), pp[0:64, 0:128])
            b.dma(wkv_fin.rearrange("h v k -> v h k"), wfin[:])

            if stop == "F":
                dump("PL", PL[:].rearrange("p c t -> p (c t)"), [128, 27 * 17])
                return done()
            post_tmp = (t1, t2)
            with ExitStack() as ssc:
                def sbs(name, shape, dt=F32):
                    return ssc.enter_context(nc.sbuf_tensor(name, list(shape), dt))
                vecs = {}
                for nm in ("r", "k", "v", "a", "b", "w"):
                    pp = pbank[2 + (len(vecs) % 2)]
                    b.tr(pp[:, 0:128], XA[nm][:], ident[:])
                    vt = sbs(f"sv_{nm}", [128, 128])
                    b.evac(vt[:], pp[:, 0:128])
                    vecs[nm] = vt
                Sa = sbs("Sa", [128, 128])
                ysv = sbs("ysv", [128, 128])
                wkv_in = st_wkv.rearrange("b (hh hl) v k -> (b hh) hl (v k)", hl=2)
                wkv_o = nwkv_s.rearrange("b (hh hl) v k -> (b hh) hl (v k)", hl=2)
                VQ = 8
                Sst = sbs("Sst", [128, VQ * 64])
                Tst = sbs("Tst", [128, VQ * 64])
                S3 = Sst[:].rearrange("p (v k) -> p v k", v=VQ)
                T3 = Tst[:].rearrange("p (v k) -> p v k", v=VQ)
                for hl in range(2):
                    for vq in range(64 // VQ):
                        v0 = hl * 64 + vq * VQ

                        def kbc(vt):
                            return vt[:, hl * 64:(hl + 1) * 64].unsqueeze(1).to_broadcast([128, VQ, 64])

                        def vbc(vt):
                            return vt[:, v0:v0 + VQ].unsqueeze(2).to_broadcast([128, VQ, 64])
                        b.dma(Sst[:], wkv_in[:, hl, vq * VQ * 64:(vq + 1) * VQ * 64])
                        b.tt("dve", T3, S3, kbc(vecs["a"]), ALU.mult)
                        b.red("dve", Sa[:, v0:v0 + VQ], T3)
                        b.tt("dve", S3, S3, kbc(vecs["w"]), ALU.mult)
                        b.tt("dve", T3, vbc(Sa), kbc(vecs["b"]), ALU.mult)
                        b.tt("dve", S3, S3, T3, ALU.add)
                        b.tt("dve", T3, vbc(vecs["v"]), kbc(vecs["k"]), ALU.mult)
                        b.tt("dve", S3, S3, T3, ALU.add)
                        b.dma(wkv_o[:, hl, vq * VQ * 64:(vq + 1) * VQ * 64], Sst[:])
                        b.tt("dve", T3, S3, kbc(vecs["r"]), ALU.mult)
                        b.red("dve", ysv[:, v0:v0 + VQ], T3)
                pp = pbank[2]
                b.tr(pp[:, 0:128], ysv[:], ident[:])
                b.evac(Ysf[:], pp[:, 0:128])
                Yc = sbs("Yc", [128, NS])
                for hp in range(8):
                    b.copy("dve", Yc[:], Ysf[:].rearrange("p (b h) -> p b h", h=8)[:, :, hp])
                    post(hp, Yc[:], bonus_s[:, hp, :], gate_s[:, hp, :], mixTs[:, 8 + hp, :], NS)
                S.barrier()
            hpstack.close()
            if stop == "G":
                dump("PL", PL[:].rearrange("p c t -> p (c t)"), [128, 27 * 17])
                return done()

            with ExitStack() as psc:
                def sbp(name, shape, dt=F32):
                    return psc.enter_context(nc.sbuf_tensor(name, list(shape), dt))
                L = 16 + TOWN
                posb = sbp("posb", [128, TOWN])
                b.dma(posb[:], pos.to_broadcast([128, TOWN]))
                inv = sbp("inv", [128, 4, TOWN])
                for g, W in enumerate((2, 4, 8, 16)):
                    b.ts("dve", inv[:, g, :], posb[:], 1.0, ALU.add, float(W), ALU.min)
                    S.op("dve", lambda e, g=g: e.reciprocal(out=inv[:, g, :], in_=inv[:, g, :]), ["inv"], ["inv"])
                Ubuf = sbp("Ubuf", [128, L])
                s_a = sbp("s_a", [128, L])
                s_b = sbp("s_b", [128, L])
                b.memset("pool", s_a[:], 0.0)
                b.memset("pool", s_b[:], 0.0)
                s_tmp = sbp("s_tmp", [128, TOWN])
                pooledT = [sbp(f"pooled{i}", [128, TOWN], BF16) for i in range(2)]
                pooledS = [sbp(f"pooledS{i}", [128, NS], BF16) for i in range(2)]
                stp = [sbp(f"stp{i}", [120, PW]) for i in range(2)]
                b.dma(stp[0][:], st_pool[0:120, :])
                b.dma(stp[1][:], st_pool[120:240, :])
                UbS = sbp("UbS", [128, NS, 16])
                swS = sbp("swS", [128, NS])
                for ct in range(8):
                    g = ct // 2
                    W = 2 ** (g + 1)
                    wt = load_wcol(ct * 128, 128)
                    b.copy("dve", Ubuf[:, 0:16], halo[:, ct, :])
                    for ti in range(2):
                        pp = pbank[2 + ti]
                        proj(wt, 128, hT[ti], 512, pp)
                        b.evac(Ubuf[:, 16 + ti * 512:16 + (ti + 1) * 512], pp[:, 0:512])
                    pp = pbank[4]
                    proj(wt, 128, hTs, NS, pp)
                    b.evac(UL[:, ct, 0:NS], pp[:, 0:NS])
                    b.copy("dve", UL[:, ct, 16:31], Ubuf[:, L - 15:L])
                    cur, sh, bi = Ubuf, 1, 0
                    bufs = [s_a, s_b]
                    while sh < W:
                        nxt = bufs[bi]
                        bi ^= 1
                        b.tt("dve", nxt[:, sh:L], cur[:, sh:L], cur[:, 0:L - sh], ALU.add)
                        cur = nxt
                        sh *= 2
                    b.tt("dve", s_tmp[:], cur[:, 16:L], inv[:, g, :], ALU.mult)
                    b.tt("dve", pooledT[ct % 2][:], s_tmp[:], Ubuf[:, 16:L], ALU.subtract)
                    for hf in range(2):
                        pp = pbank[5 + hf]
                        b.tr(pp[:, 0:120], stp[hf][:, ct * 128:(ct + 1) * 128], ident[0:120, 0:120])
                        b.evac(UbS[:, 8 * hf:8 * hf + 8, 0:15], pp[:, 0:120].rearrange("p (b r) -> p b r", r=15))
                    b.copy("dve", UbS[:, :, 15], UL[:, ct, 0:NS])
                    b.red("dve", swS[:], UbS[:, :, 16 - W:16])
                    b.stt(pooledS[ct % 2][:], swS[:], 1.0 / W, UL[:, ct, 0:NS], ALU.mult, ALU.subtract)
                    if ct % 2 == 1:
                        for dt_ in range(2):
                            mt = 2 * g + dt_
                            for ti in range(3):
                                n = 512 if ti < 2 else NS
                                pp = pbank[2 + ti]
                                for cc in range(2):
                                    rhs = pooledT[cc][:, ti * 512:(ti + 1) * 512] if ti < 2 else pooledS[cc][:]
                                    b.mm(pp[:, 0:n], wpb[:, 2 * g + cc, dt_ * 128:(dt_ + 1) * 128], rhs,
                                         start=(cc == 0), stop=(cc == 1))
                                dst = mixT[ti][:, mt, :] if ti < 2 else mixTs[:, mt, :]
                                b.act(dst, pp[:, 0:n], AF.Copy, scale=pscale[:, mt:mt + 1])
                if stop == "DBG":
                    dump("PL", PL[:].rearrange("p c t -> p (c t)"), [128, 27 * 17])
                    dump("SH", SH[:].rearrange("p c t -> p (c t)"), [128, 27 * NS])
                pltm = sbp("pltm", [17, SHW])
                for ct in range(27):
                    ncl = min(128, SHW - ct * 128)
                    pp = pbank[5 + ct % 2]
                    b.tr(pp[0:17, 0:ncl], PL[0:ncl, ct, :], ident[0:ncl, 0:ncl])
                    b.evac(pltm[:, ct * 128:ct * 128 + ncl], pp[0:17, 0:ncl])
                b.dma(nshift_s[:, :], pltm[0:16, :])
                b.dma(shift_last[:, :], pltm[16:17, :])
                ultm = sbp("ultm", [31, PW])
                for ct in range(8):
                    pp = pbank[5 + ct % 2]
                    b.tr(pp[0:31, 0:128], UL[:, ct, :], ident[:])
                    b.evac(ultm[:, ct * 128:(ct + 1) * 128], pp[0:31, 0:128])
                b.dma(npool_s[:, 14, :], ultm[0:16, :])
                b.dma(pool_last[:, :], ultm[16:31, :])
                b.dma(npool_s[:, 0:14, :], st_pool3[:, 1:15, :])
                S.barrier()

        if stop == "H":
            return done()
        x2 = [sb(f"x2_{i}", [128, D]) for i in range(9)]
        tokM = [128] * 8 + [NS]

        def at_slice(tk, cc):
            if tk < 8:
                return mixT[tk // 4][:, cc, (tk % 4) * 128:(tk % 4 + 1) * 128]
            return mixTs[:, cc, :]

        with ExitStack() as sc2:
            wob = sc2.enter_context(nc.sbuf_tensor("wob", [128, 16, D], BF16))
            xr = sc2.enter_context(nc.sbuf_tensor("xr", [128, D], F32))
            for cc in range(16):
                b.dma(wob[:, cc, :], w_out[cc * 128:(cc + 1) * 128, :], eng="pool")
            for tk in range(9):
                M = tokM[tk]
                b.dma(xr[0:M, :], xown[tk * 128:(tk + 1) * 128, :] if tk < 8 else xs[:, :])
                for db in range(4):
                    pp = pbank[db]
                    for cc in range(16):
                        b.mm(pp[0:M, :], at_slice(tk, cc), wob[:, cc, db * 512:(db + 1) * 512],
                             start=(cc == 0), stop=(cc == 15))
                    b.tt("dve", x2[tk][0:M, db * 512:(db + 1) * 512], pp[0:M, :], xr[0:M, db * 512:(db + 1) * 512], ALU.add)
            S.barrier()

        if stop == "I":
            return done()
        with ExitStack() as sc3:
            xsq3 = None
            xbf3 = sc3.enter_context(nc.sbuf_tensor("xbf3", [128, D], BF16))
            ssq3 = sc3.enter_context(nc.sbuf_tensor("ssq3", [128, 2], F32))
            rstd3 = sc3.enter_context(nc.sbuf_tensor("rstd3", [128, 2], F32))
            nb3 = (None, xsq3, [xbf3, xbf3], ssq3, rstd3)
            for tk in range(9):
                M = tokM[tk]
                if tk < 8:
                    norm_to_T_g(nb3, x2[tk][0:M, :], M, g_ffn, mixT[tk // 4], (tk % 4) * 128, 0, src_is_sbuf=True)
                else:
                    norm_to_T_g(nb3, x2[tk][0:M, :], M, g_ffn, mixTs, 0, 0, src_is_sbuf=True)
            S.barrier()

        if stop == "J":
            return done()
        with ExitStack() as sc4:
            def sb4(name, shape, dt=F32):
                return sc4.enter_context(nc.sbuf_tensor(name, list(shape), dt))
            GF = 11
            uT = sb4("uT", [128, GF, TOWN + NS], BF16)
            wg = [sb4(f"wg{i}", [128, 16, 128], BF16) for i in range(3)]
            wu = [sb4(f"wu{i}", [128, 16, 128], BF16) for i in range(3)]
            wdn = [sb4(f"wdn{i}", [128, GF, 512], BF16) for i in range(2)]
            sgt = sb4("sgt", [128, 512])
            wg_v = w_gate.rearrange("(kc p) n -> p kc n", p=128)
            wu_v = w_up.rearrange("(kc p) n -> p kc n", p=128)
            wd_v = w_down.rearrange("(ft p) d -> p ft d", p=128)
            ftiles = [(mixT[0], 512, 0), (mixT[1], 512, 512), (mixTs, NS, 1024)]
            pctr = 0
            dctr = 0
            for grp in range(NFT // GF):
                for fl in range(GF):
                    ft = grp * GF + fl
                    wgt, wut = wg[ft % 3], wu[ft % 3]
                    b.dma(wgt[:], wg_v[:, :, ft * 128:(ft + 1) * 128], eng="pool")
                    b.dma(wut[:], wu_v[:, :, ft * 128:(ft + 1) * 128], eng="pool")
                    for (rt_, n, off) in ftiles:
                        pg = pbank[(2 * pctr) % 8]
                        pu = pbank[(2 * pctr + 1) % 8]
                        pctr += 1
                        for dk in range(16):
                            b.mm(pg[:, 0:n], wgt[:, dk, :], rt_[:, dk, 0:n], start=(dk == 0), stop=(dk == 15))
                        for dk in range(16):
                            b.mm(pu[:, 0:n], wut[:, dk, :], rt_[:, dk, 0:n], start=(dk == 0), stop=(dk == 15))
                        b.act(sgt[:, 0:n], pg[:, 0:n], AF.Silu)
                        b.tt("dve", uT[:, fl, off:off + n], sgt[:, 0:n], pu[:, 0:n], ALU.mult)
                for db in range(4):
                    wdt = wdn[dctr % 2]
                    dctr += 1
                    b.dma(wdt[:], wd_v[:, grp * GF:(grp + 1) * GF, db * 512:(db + 1) * 512], eng="pool")
                    for tk in range(9):
                        M = tokM[tk]
                        pp = pbank[pctr % 8]
                        pctr += 1
                        for fl in range(GF):
                            b.mm(pp[0:M, :], uT[:, fl, tk * 128:tk * 128 + M], wdt[:, fl, :], start=(fl == 0), stop=(fl == GF - 1))
                        b.tt("dve", x2[tk][0:M, db * 512:(db + 1) * 512], pp[0:M, :], x2[tk][0:M, db * 512:(db + 1) * 512], ALU.add)
            S.barrier()

        if stop == "K":
            return done()
        with ExitStack() as sc5:
            gfin = sc5.enter_context(nc.sbuf_tensor("gfin", [128, D], F32))
            xsq5 = sc5.enter_context(nc.sbuf_tensor("xsq5", [128, D], F32))
            ssq5 = sc5.enter_context(nc.sbuf_tensor("ssq5", [128, 1], F32))
            rstd5 = sc5.enter_context(nc.sbuf_tensor("rstd5", [128, 1], F32))
            yout = [sc5.enter_context(nc.sbuf_tensor(f"yout{i}", [128, D], F32)) for i in range(2)]
            b.dma(gfin[:], norm_final.to_broadcast([128, D]))
            for tk in range(9):
                M = tokM[tk]
                xi = x2[tk][0:M, :]
                b.stt(xsq5[0:M, :], xi, 1.0, xi, ALU.mult, ALU.mult, accum=ssq5[0:M, 0:1])
                b.ts("dve", rstd5[0:M, :], ssq5[0:M, :], 1.0 / D, ALU.mult, eps_rms, ALU.add)
                b.rsqrt(rstd5[0:M, :], rstd5[0:M, :])
                yo = yout[tk % 2]
                b.stt(yo[0:M, :], xi, rstd5[0:M, 0:1], gfin[0:M, :], ALU.mult, ALU.mult)
                b.dma(y_own[tk * 128:(tk + 1) * 128, :] if tk < 8 else y_s[:, :], yo[0:M, :])

        return done()


_NC_CACHE = {}


def kernel(**inp):
    f = lambda k: np.ascontiguousarray(np.asarray(inp[k], dtype=np.float32))
    xp = f("x_prompt")
    xsmp = f("x_sample")
    sp, ss, sw = f("state_pool"), f("state_shift"), f("state_wkv")
    shared = {
        "norm_mix": f("norm_mix").reshape(D), "w_in": f("w_in").reshape(D, PROJ),
        "w_pool": f("w_pool").reshape(4, 256, 256), "pool_scale": f("pool_scale").reshape(PW),
        "mu_shift": f("mu_shift").reshape(SHW), "w0": f("w0").reshape(CW), "w2": f("w2").reshape(64, CW),
        "a0": f("a0").reshape(CW), "a2": f("a2").reshape(64, CW), "g2": f("g2").reshape(160, CW),
        "k_k": f("k_k").reshape(CW), "k_a": f("k_a").reshape(CW), "r_k": f("r_k").reshape(CW),
        "gn_w": f("gn_w").reshape(CW), "gn_b": f("gn_b").reshape(CW), "w_out": f("w_out").reshape(D, D),
        "norm_ffn": f("norm_ffn").reshape(D), "w_gate": f("w_gate").reshape(D, DFF),
        "w_up": f("w_up").reshape(D, DFF), "w_down": f("w_down").reshape(DFF, D),
        "norm_final": f("norm_final").reshape(1, D),
    }
    in_maps = []
    for c in range(NCORE):
        bq, half = c // 2, c % 2
        m = dict(shared)
        m["xown"] = np.ascontiguousarray(xp[bq, half * TOWN:(half + 1) * TOWN])
        m["xprev"] = np.ascontiguousarray(xp[bq, 0:TOWN]) if half == 1 else np.zeros((TOWN, D), np.float32)
        m["xs"] = np.ascontiguousarray(xsmp[c * NS:(c + 1) * NS, 0])
        m["st_pool"] = np.ascontiguousarray(sp[0, c * NS:(c + 1) * NS]).reshape(NS * 15, PW)
        m["st_shift"] = np.ascontiguousarray(ss[0, c * NS:(c + 1) * NS])
        m["st_wkv"] = np.ascontiguousarray(sw[0, c * NS:(c + 1) * NS])
        m["pos"] = (half * TOWN + np.arange(TOWN, dtype=np.float32)).reshape(1, TOWN)
        in_maps.append(m)
    if "nc" not in _NC_CACHE:
        _NC_CACHE["nc"] = build_program()
    res = run_bass_kernel_spmd(_NC_CACHE["nc"], in_maps, core_ids=list(range(NCORE)))
    R = res.results
    y_prompt = np.zeros((4, 2048, D), np.float32)
    for c in range(NCORE):
        y_prompt[c // 2, (c % 2) * TOWN:(c % 2 + 1) * TOWN] = R[c]["y_own"]
    y_sample = np.concatenate([R[c]["y_s"] for c in range(NCORE)], 0).reshape(128, 1, D)
    npp = np.stack([R[2 * q + 1]["pool_last"] for q in range(4)], 0)[None]
    nsp = np.stack([R[2 * q + 1]["shift_last"].reshape(SHW) for q in range(4)], 0)[None]
    nwp = np.stack([R[2 * q + 1]["wkv_fin"] for q in range(4)], 0)[None]
    nps = np.concatenate([R[c]["npool_s"] for c in range(NCORE)], 0)[None]
    nss = np.concatenate([R[c]["nshift_s"] for c in range(NCORE)], 0)[None]
    nws = np.concatenate([R[c]["nwkv_s"] for c in range(NCORE)], 0)[None]
    return (y_prompt, y_sample, npp.astype(np.float32), nsp.astype(np.float32), nwp.astype(np.float32),
            nps.astype(np.float32), nss.astype(np.float32), nws.astype(np.float32))
```

```python
import numpy as np
from contextlib import ExitStack
import concourse.bass as bass
import concourse.mybir as mybir
from concourse.bass_utils import run_bass_kernel_spmd

F32 = mybir.dt.float32
BF16 = mybir.dt.bfloat16
AF = mybir.ActivationFunctionType
ALU = mybir.AluOpType
AX = mybir.AxisListType

D = 2048
NCORE = 8
TOWN = 1024
NS = 16
PW = 1024
CW = 1024
SHW = 3360
PROJ = 4384
DFF = 5632
NFT = 44
CH = 128
DEC_K = 0.6065306597126334


class Sched:
    ENGS = ("sync", "act", "dve", "pool", "pe")

    def __init__(self, nc, sems, dma_sems):
        self.nc = nc
        self.sem = dict(zip(self.ENGS, sems))
        self.cnt = {e: 0 for e in self.ENGS}
        self.known = {e: {} for e in self.ENGS}
        self.prog = {e: [] for e in self.ENGS}
        self.lastw = {}
        self.readers = {}
        self.dma_sems = list(dma_sems)
        self.dma_next = 0
        self.dma_cnt = [0] * len(self.dma_sems)
        self.dma_pending = [None] * len(self.dma_sems)

    def _need(self, eng, dep):
        if dep is None:
            return
        if dep[0] == "e":
            _, f, n = dep
            if f == eng and eng in ("pe", "sync"):
                return
            key = ("e", f)
            sem = self.sem[f]
        else:
            _, i, n = dep
            key = ("d", i)
            sem = self.dma_sems[i]
        if self.known[eng].get(key, 0) >= n:
            return
        self.known[eng][key] = n
        self.prog[eng].append(lambda e, sem=sem, n=n: e.wait_ge(sem, n))

    def _deps(self, eng, reads, writes):
        for k in reads:
            self._need(eng, self.lastw.get(k))
        for k in writes:
            self._need(eng, self.lastw.get(k))
            for d in self.readers.get(k, ()):
                self._need(eng, d)

    def _commit(self, dep, reads, writes):
        for k in reads:
            self.readers.setdefault(k, []).append(dep)
        for k in writes:
            self.lastw[k] = dep
            self.readers[k] = []

    def op(self, eng, fn, reads=(), writes=(), inc=True):
        self._deps(eng, reads, writes)
        if inc:
            self.cnt[eng] += 1
            n = self.cnt[eng]
            sem = self.sem[eng]
            self.prog[eng].append(lambda e, fn=fn, sem=sem: fn(e).then_inc(sem, 1))
            dep = ("e", eng, n)
        else:
            self.prog[eng].append(lambda e, fn=fn: fn(e))
            dep = ("e", eng, self.cnt[eng] + 1)
        self._commit(dep, reads, writes)

    def dma(self, fn, reads=(), writes=(), eng="sync"):
        i = self.dma_next
        self.dma_next = (self.dma_next + 1) % len(self.dma_sems)
        if self.dma_pending[i] is not None:
            self._need(eng, self.dma_pending[i])
        self._deps(eng, reads, writes)
        self.dma_cnt[i] += 16
        v = self.dma_cnt[i]
        sem = self.dma_sems[i]
        self.prog[eng].append(lambda e, fn=fn, sem=sem: fn(e).then_inc(sem, 16))
        dep = ("d", i, v)
        self.dma_pending[i] = dep
        self._commit(dep, reads, writes)

    def barrier(self):
        snap = dict(self.cnt)
        pend = [d for d in self.dma_pending if d is not None]
        for e in self.ENGS:
            for d in pend:
                self._need(e, d)
            for f in self.ENGS:
                if f != e and snap[f] > 0:
                    self._need(e, ("e", f, snap[f]))

    def finish(self, eng="sync"):
        for d in self.dma_pending:
            self._need(eng, d)
        for f in self.ENGS:
            if f != eng and self.cnt[f] > 0:
                self._need(eng, ("e", f, self.cnt[f]))

    def emit(self, block):
        progs = self.prog

        @block.sync
        def _(e):
            for f in progs["sync"]:
                f(e)

        @block.scalar
        def _(e):
            for f in progs["act"]:
                f(e)

        @block.vector
        def _(e):
            for f in progs["dve"]:
                f(e)

        @block.gpsimd
        def _(e):
            for f in progs["pool"]:
                f(e)

        @block.tensor
        def _(e):
            for f in progs["pe"]:
                f(e)


def _nm(ap):
    return ap.tensor.name


class B:
    def __init__(self, S):
        self.S = S
        self.rr = 0

    def _k(self, aps):
        return [_nm(a) for a in aps if hasattr(a, "tensor")]

    def tt(self, eng, out, a, b, op):
        self.S.op(eng, lambda e: e.tensor_tensor(out=out, in0=a, in1=b, op=op), self._k([a, b]), self._k([out]))

    def ts(self, eng, out, a, s1, op0, s2=None, op1=None):
        if op1 is None:
            self.S.op(eng, lambda e: e.tensor_scalar(out=out, in0=a, scalar1=s1, scalar2=None, op0=op0),
                      self._k([a, s1]), self._k([out]))
        else:
            self.S.op(eng, lambda e: e.tensor_scalar(out=out, in0=a, scalar1=s1, scalar2=s2, op0=op0, op1=op1),
                      self._k([a, s1, s2]), self._k([out]))

    def stt(self, out, a, s, b, op0, op1, accum=None):
        if accum is None:
            self.S.op("dve", lambda e: e.scalar_tensor_tensor(out=out, in0=a, scalar=s, in1=b, op0=op0, op1=op1),
                      self._k([a, s, b]), self._k([out]))
        else:
            self.S.op("dve", lambda e: e.scalar_tensor_tensor(out=out, in0=a, scalar=s, in1=b, op0=op0, op1=op1,
                                                              accum_out=accum),
                      self._k([a, s, b]), self._k([out, accum]))

    def act(self, out, a, func, scale=None, bias=None):
        kw = {}
        if scale is not None:
            kw["scale"] = scale
        if bias is not None:
            kw["bias"] = bias
        self.S.op("act", lambda e: e.activation(out=out, in_=a, func=func, **kw),
                  self._k([a, scale, bias]), self._k([out]))

    def copy(self, eng, out, a):
        if eng == "act":
            self.act(out, a, AF.Copy)
        else:
            self.S.op(eng, lambda e: e.tensor_copy(out=out, in_=a), self._k([a]), self._k([out]))

    def rsqrt(self, out, a):
        self.act(out, a, AF.Sqrt)
        self.S.op("dve", lambda e: e.reciprocal(out=out, in_=out), self._k([out]), self._k([out]))

    def evac(self, out, a):
        self.rr ^= 1
        self.copy("act" if self.rr else "dve", out, a)

    def red(self, eng, out, a, op=ALU.add):
        self.S.op(eng, lambda e: e.tensor_reduce(out=out, in_=a, axis=AX.X, op=op), self._k([a]), self._k([out]))

    def memset(self, eng, ap, v):
        self.S.op(eng, lambda e: e.memset(ap, v), [], self._k([ap]))

    def mm(self, out, lhsT, rhs, start=True, stop=True, inc=None, tp=None):
        if inc is None:
            inc = stop
        kw = {}
        if tp is not None:
            kw["tile_position"] = tp
        self.S.op("pe", lambda e: e.matmul(out, lhsT=lhsT, rhs=rhs, start=start, stop=stop, **kw),
                  self._k([lhsT, rhs]), self._k([out]), inc=inc)

    def tr(self, out, a, ident, inc=True):
        self.S.op("pe", lambda e: e.transpose(out, a, ident), self._k([a, ident]), self._k([out]), inc=inc)

    def dma(self, out, a, eng="sync", slow=False):
        kw = {"allow_slow_non_contiguous": True} if slow else {}
        self.S.dma(lambda e: e.dma_start(out=out, in_=a, **kw), self._k([a]), self._k([out]), eng=eng)


def build_program(stop=None):
    nc = bass.Bass("TRN2", target_bir_lowering=False)

    def din(name, shape):
        return nc.dram_tensor(name, list(shape), F32, kind="ExternalInput").ap()

    def dout(name, shape):
        return nc.dram_tensor(name, list(shape), F32, kind="ExternalOutput").ap()

    xprev = din("xprev", [TOWN, D])
    xown = din("xown", [TOWN, D])
    xs = din("xs", [NS, D])
    st_pool = din("st_pool", [NS * 15, PW])
    st_pool3 = st_pool.rearrange("(b r) c -> b r c", r=15)
    st_shift = din("st_shift", [NS, SHW])
    st_wkv = din("st_wkv", [NS, 16, 64, 64])
    pos = din("pos", [1, TOWN])
    norm_mix = din("norm_mix", [D])
    w_in = din("w_in", [D, PROJ])
    w_pool = din("w_pool", [4, 256, 256])
    pool_scale = din("pool_scale", [PW])
    mu_shift = din("mu_shift", [SHW])
    w0 = din("w0", [CW])
    w2 = din("w2", [64, CW])
    a0 = din("a0", [CW])
    a2 = din("a2", [64, CW])
    g2 = din("g2", [160, CW])
    k_k = din("k_k", [CW])
    k_a = din("k_a", [CW])
    r_k = din("r_k", [CW])
    gn_w = din("gn_w", [CW])
    gn_b = din("gn_b", [CW])
    w_out = din("w_out", [D, D])
    norm_ffn = din("norm_ffn", [D])
    w_gate = din("w_gate", [D, DFF])
    w_up = din("w_up", [D, DFF])
    w_down = din("w_down", [DFF, D])
    norm_final = din("norm_final", [1, D])

    y_own = dout("y_own", [TOWN, D])
    y_s = dout("y_s", [NS, D])
    pool_last = dout("pool_last", [15, PW])
    shift_last = dout("shift_last", [1, SHW])
    wkv_fin = dout("wkv_fin", [16, 64, 64])
    npool_s = dout("npool_s", [NS, 15, PW])
    nshift_s = dout("nshift_s", [NS, SHW])
    nwkv_s = dout("nwkv_s", [NS, 16, 64, 64])

    es = ExitStack()
    with es:
        def sb(name, shape, dt=F32):
            return es.enter_context(nc.sbuf_tensor(name, list(shape), dt))

        sems = [es.enter_context(nc.semaphore(f"se{i}")) for i in range(5)]
        dsems = [es.enter_context(nc.semaphore(f"sd{i}")) for i in range(24)]
        S = Sched(nc, sems, dsems)
        b = B(S)

        dbg_n = [0]

        def dump(name, ap, shape):
            o = nc.dram_tensor("dbg_" + name, list(shape), F32, kind="ExternalOutput").ap()
            b.dma(o, ap)

        def done():
            S.finish()
            with nc.Block() as block:
                S.emit(block)
            return nc
        pbank = [es.enter_context(nc.psum_tensor(f"pb{i}", [128, 512], F32)) for i in range(8)]

        def pbf(i):
            return pbank[i][:].bitcast(BF16)

        ident = sb("ident", [128, 128])
        identb = sb("identb", [128, 128], BF16)
        blk = sb("blk", [128, 128])
        mask4 = sb("mask4", [128, 512])
        maskl = sb("maskl", [128, 256])
        b.memset("pool", ident[:], 1.0)
        S.op("pool", lambda e: e.affine_select(out=ident[:], in_=ident[:], pattern=[[-1, 128]], compare_op=ALU.is_equal,
                                               fill=0.0, base=0, channel_multiplier=1), ["ident"], ["ident"])
        b.copy("dve", identb[:], ident[:])
        b.memset("pool", blk[:], 0.0)
        b.memset("pool", blk[0:64, 0:64], 1.0)
        b.memset("pool", blk[64:128, 64:128], 1.0)
        b.memset("pool", mask4[:], 1.0)
        for q in range(4):
            cmp = ALU.is_gt if q % 2 == 0 else ALU.is_ge
            S.op("pool", lambda e, q=q, cmp=cmp: e.affine_select(
                out=mask4[:, q * 128:(q + 1) * 128], in_=mask4[:, q * 128:(q + 1) * 128], pattern=[[1, 128]],
                compare_op=cmp, fill=0.0, base=0, channel_multiplier=-1), ["mask4"], ["mask4"])
        b.memset("pool", maskl[:], 1.0)
        for q in range(2):
            S.op("pool", lambda e, q=q: e.affine_select(
                out=maskl[:, q * 128:(q + 1) * 128], in_=maskl[:, q * 128:(q + 1) * 128], pattern=[[-1, 128]],
                compare_op=ALU.is_gt, fill=0.0, base=0, channel_multiplier=1), ["maskl"], ["maskl"])

        def colparam(name, src, n):
            nt = (n + 127) // 128
            t = sb(name, [128, nt])
            nfull = n // 128
            if nfull:
                b.dma(t[:, 0:nfull], src[0:nfull * 128].rearrange("(t p) -> p t", p=128), slow=True)
            rem = n - nfull * 128
            if rem:
                b.dma(t[0:rem, nfull:nfull + 1], src[nfull * 128:n].rearrange("(t p) -> p t", p=rem), slow=True)
            return t

        g_mix = colparam("g_mix", norm_mix, D)
        g_ffn = colparam("g_ffn", norm_ffn, D)
        pscale = colparam("pscale", pool_scale, PW)
        mu_c = colparam("mu_c", mu_shift, SHW)
        w0_c = colparam("w0_c", w0, CW)
        a0_c = colparam("a0_c", a0, CW)
        kk_c = colparam("kk_c", k_k, CW)
        ka_c = colparam("ka_c", k_a, CW)
        rk_c = colparam("rk_c", r_k, CW)
        gnw_c = colparam("gnw_c", gn_w, CW)
        gnb_c = colparam("gnb_c", gn_b, CW)
        w2b = sb("w2b", [128, CW], BF16)
        g2b = sb("g2b", [128, CW], BF16)
        g2b2 = sb("g2b2", [32, CW], BF16)
        b.dma(w2b[0:64, :], w2[:, :], eng="pool")
        b.dma(w2b[64:128, :], a2[:, :], eng="pool")
        b.dma(g2b[:], g2[0:128, :], eng="pool")
        b.dma(g2b2[:], g2[128:160, :], eng="pool")
        wpb = sb("wpb", [128, 8, 256], BF16)
        b.dma(wpb[:], w_pool.rearrange("g (cc p) d -> p (g cc) d", p=128), eng="pool")

        eps_rms = 1e-6

        mixT = [sb(f"AT{i}", [128, 16, 512], BF16) for i in range(2)]
        mixTs = sb("ATs", [128, 16, NS], BF16)
        STt = [sb(f"ST{hp}", [128, 64]) for hp in range(8)]
        STb = [sb(f"STb{hp}", [128, 128], BF16) for hp in range(8)]
        carry = sb("carry", [128, 27])
        PL = sb("PL", [128, 27, 17])
        UL = sb("UL", [128, 8, 31])
        halo = sb("halo", [128, 8, 16])
        for hp in range(8):
            b.memset("pool", STt[hp][:], 0.0)
            b.memset("pool", STb[hp][:], 0.0)
        b.memset("pool", carry[:], 0.0)

        def norm_to_T_g(nbufs, src_ap, nrow, g_c, dst, dst_off, slot, src_is_sbuf=False):
            xin, xsq, xbf, ssq, rstd = nbufs
            if src_is_sbuf:
                xi = src_ap
            else:
                xi = xin[slot][0:nrow, :]
                b.dma(xi, src_ap)
            b.stt(xbf[slot][0:nrow, :], xi, 1.0, xi, ALU.mult, ALU.mult, accum=ssq[0:nrow, slot:slot + 1])
            b.ts("dve", rstd[0:nrow, slot:slot + 1], ssq[0:nrow, slot:slot + 1], 1.0 / D, ALU.mult, eps_rms, ALU.add)
            b.rsqrt(rstd[0:nrow, slot:slot + 1], rstd[0:nrow, slot:slot + 1])
            b.act(xbf[slot][0:nrow, :], xi, AF.Copy, scale=rstd[0:nrow, slot:slot + 1])
            for q in range(4):
                pb = pbf(q % 2)
                for j in range(4):
                    dk = q * 4 + j
                    b.tr(pb[:, j * 128:j * 128 + nrow], xbf[slot][0:nrow, dk * 128:(dk + 1) * 128],
                         identb[0:nrow, 0:nrow], inc=(j == 3))
                for j in range(4):
                    dk = q * 4 + j
                    if j % 2 == 0:
                        b.act(dst[:, dk, dst_off:dst_off + nrow], pb[:, j * 128:j * 128 + nrow], AF.Copy,
                              scale=g_c[:, dk:dk + 1])
                    else:
                        b.ts("dve", dst[:, dk, dst_off:dst_off + nrow], pb[:, j * 128:j * 128 + nrow],
                             g_c[:, dk:dk + 1], ALU.mult)

        if stop == "A":
            return done()
        mixer_scope = ExitStack()
        with mixer_scope:
            def sbm(name, shape, dt=F32):
                return mixer_scope.enter_context(nc.sbuf_tensor(name, list(shape), dt))

            hT = [sbm(f"hT{i}", [128, 16, 512], BF16) for i in range(2)]
            hTs = sbm("hTs", [128, 16, NS], BF16)
            xin = [sbm("xin0", [128, D]), sbm("xin1", [128, D])]
            xsq = None
            xbf = [sbm("xbf0", [128, D], BF16)] * 2
            ssq = sbm("ssq", [128, 2])
            rstd = sbm("rstd", [128, 2])
            nbufs = (xin, xsq, xbf, ssq, rstd)

            def norm_to_T(*a, **k):
                return norm_to_T_g(nbufs, *a, **k)

            def _unused(src_ap, nrow, g_c, dst, dst_off, slot, src_eng="sync", src_is_sbuf=False):
                if src_is_sbuf:
                    xi = src_ap
                else:
                    xi = xin[slot][0:nrow, :]
                    b.dma(xi, src_ap)
                b.tt("dve", xsq[0:nrow, :], xi, xi, ALU.mult)
                b.red("dve", ssq[0:nrow, slot:slot + 1], xsq[0:nrow, :])
                b.ts("dve", rstd[0:nrow, slot:slot + 1], ssq[0:nrow, slot:slot + 1], 1.0 / D, ALU.mult, eps_rms, ALU.add)
                b.rsqrt(rstd[0:nrow, slot:slot + 1], rstd[0:nrow, slot:slot + 1])
                b.act(xbf[slot][0:nrow, :], xi, AF.Copy, scale=rstd[0:nrow, slot:slot + 1])
                for q in range(4):
                    pb = pbf(q % 2)
                    for j in range(4):
                        dk = q * 4 + j
                        b.tr(pb[:, j * 128:j * 128 + nrow], xbf[slot][0:nrow, dk * 128:(dk + 1) * 128],
                             identb[0:nrow, 0:nrow], inc=(j == 3))
                    for j in range(4):
                        dk = q * 4 + j
                        if j % 2 == 0:
                            b.act(dst[:, dk, dst_off:dst_off + nrow], pb[:, j * 128:j * 128 + nrow], AF.Copy,
                                  scale=g_c[:, dk:dk + 1])
                        else:
                            b.ts("dve", dst[:, dk, dst_off:dst_off + nrow], pb[:, j * 128:j * 128 + nrow],
                                 g_c[:, dk:dk + 1], ALU.mult)

            wbuf = [sbm(f"wbuf{i}", [128, 16, 128], BF16) for i in range(4)]
            w_in_v = w_in.rearrange("(kc p) n -> p kc n", p=128)
            wctr = [0]

            def load_wcol(c0, ncol):
                t = wbuf[wctr[0] % 4]
                wctr[0] += 1
                b.dma(t[:, :, 0:ncol], w_in_v[:, :, c0:c0 + ncol], eng="pool")
                return t

            def proj(wt, ncol, rhs_tile, n, pout, off=0):
                for dk in range(16):
                    b.mm(pout[0:ncol, 0:n], wt[:, dk, 0:ncol], rhs_tile[:, dk, off:off + n], start=(dk == 0), stop=(dk == 15))

            XA = {nm: sbm(f"XA_{nm}", [128, 128]) for nm in ("r", "k", "v", "a", "b", "w")}
            bonus_s = sbm("bonus_s", [128, 8, NS])
            gate_s = sbm("gate_s", [128, 8, NS])
            SH = sbm("SH", [128, 27, NS])
            Ysf = sbm("Ysf", [128, 128])
            hpstack = mixer_scope.enter_context(ExitStack())

            def sbw(name, shape, dt=F32):
                return hpstack.enter_context(nc.sbuf_tensor(name, list(shape), dt))

            NT = 512
            Pb = [sbw(f"Pb{i}", [128, NT + 1]) for i in range(3)]
            dtmp = sbw("dtmp", [128, NT])
            Rt = sbw("Rt", [128, NT])
            K0 = sbw("K0", [128, NT])
            Vt = sbw("Vt", [128, NT])
            Kt = sbw("Kt", [128, NT])
            sig = sbw("sig", [128, NT])
            alr = sbw("alr", [128, NT])
            kkn = sbw("kkn", [128, NT])
            t1 = sbw("t1", [128, NT])
            t2 = sbw("t2", [128, NT])
            csig = sbw("csig", [128, NT])
            onesf = sbw("onesf", [128, NT])
            b.memset("pool", onesf[:], 1.0)
            gb = sbw("gb", [128, 4])
            gend = sbw("gend", [128, 4])
            ECt = sbw("ECt", [128, 4])
            ex = sbw("ex", [128, NT])
            rtb = sbw("rtb", [128, NT], BF16)
            atb = sbw("atb", [128, NT], BF16)
            ktb = sbw("ktb", [128, NT], BF16)
            btb = sbw("btb", [128, NT], BF16)
            Khb = sbw("Khb", [128, NT], BF16)
            Bhb = sbw("Bhb", [128, NT], BF16)
            Vb = sbw("Vb", [128, NT], BF16)
            bonus = sbw("bonus", [128, NT])
            gate = sbw("gate", [128, NT])
            Yt = dtmp
            wab = [sbw(f"wab{i}", [128, NT if i < 2 else NS], BF16) for i in range(3)]
            sgb = [sbw(f"sgb{i}", [128, NT if i < 2 else NS], BF16) for i in range(3)]
            sgb2 = [sbw(f"sgb2{i}", [32, NT if i < 2 else NS], BF16) for i in range(3)]
            AM = [sbw(f"AM{h}", [128, 512], BF16) for h in range(2)]
            LV = [sbw(f"LV{i}", [128, 2, 384], BF16) for i in range(2)]
            TM = sbw("TM", [128, 2, 4, 128], BF16)
            WT = sbw("WT", [128, 128], BF16)
            Ub = sbw("Ub", [128, 2, 128], BF16)
            b.memset("pool", TM[:].rearrange("p h q d -> p (h q d)"), 0.0)
            b.memset("pool", Ub[:].rearrange("p h d -> p (h d)"), 0.0)
            AM_b = [sbw(f"AMb{h}", [128, 512], BF16) for h in range(2)]
            LV_b = [sbw(f"LVb{i}", [128, 2, 384], BF16) for i in range(2)]
            TM_b = sbw("TMb", [128, 2, 4, 128], BF16)
            WT_b = sbw("WTb", [128, 128], BF16)
            Ub_b = sbw("Ubb", [128, 2, 128], BF16)
            b.memset("pool", TM_b[:].rearrange("p h q d -> p (h q d)"), 0.0)
            b.memset("pool", Ub_b[:].rearrange("p h d -> p (h d)"), 0.0)
            slots = [dict(AM=AM, LV=LV, TM=TM, WT=WT, Ub=Ub, pX=pbank[5], pY2=pbank[6]),
                     dict(AM=AM_b, LV=LV_b, TM=TM_b, WT=WT_b, Ub=Ub_b, pX=pbank[3], pY2=pbank[4])]

            def shift_mix(pb_t, ncol, n, mu_col, out_ap):
                b.tt("dve", dtmp[0:ncol, 0:n], pb_t[0:ncol, 0:n], pb_t[0:ncol, 1:n + 1], ALU.subtract)
                b.stt(out_ap, dtmp[0:ncol, 0:n], mu_col, pb_t[0:ncol, 1:n + 1], ALU.mult, ALU.add)

            def run_pass(is_own):
                tiles = [("p", 0, 512), ("p", 1, 512)]
                if is_own:
                    tiles.append(("s", 0, NS))
                src = xown if is_own else xprev
                for tt_i in range(2):
                    for blk_i in range(4):
                        r0 = tt_i * 512 + blk_i * 128
                        norm_to_T(src[r0:r0 + 128, :], 128, g_mix, hT[tt_i], blk_i * 128, blk_i % 2)
                if is_own:
                    norm_to_T(xs[:, :], NS, g_mix, hTs, 0, 0)

                if stop == "P0":
                    return True

                def rhs_of(tl):
                    return hTs if tl[0] == "s" else hT[tl[1]]

                for li, (ctile, ncol) in enumerate(((24, 128), (25, 128), (26, 32))):
                    wt = load_wcol(PW + ctile * 128, ncol)
                    for ti, tl in enumerate(tiles):
                        n = tl[2]
                        pp = pbank[2 + (ti % 2)]
                        proj(wt, ncol, rhs_of(tl), n, pp)
                        mu_col = mu_c[0:ncol, ctile:ctile + 1]
                        if tl[0] == "p":
                            pbt = Pb[li % 3]
                            b.copy("dve", pbt[0:ncol, 0:1], carry[0:ncol, ctile:ctile + 1])
                            b.evac(pbt[0:ncol, 1:n + 1], pp[0:ncol, 0:n])
                            b.copy("dve", carry[0:ncol, ctile:ctile + 1], pbt[0:ncol, n:n + 1])
                            if is_own and tl[1] == 1:
                                b.copy("dve", PL[0:ncol, ctile, 16:17], pbt[0:ncol, n:n + 1])
                            shift_mix(pbt, ncol, n, mu_col, t1[0:ncol, 0:n])
                        else:
                            b.evac(PL[0:ncol, ctile, 0:NS], pp[0:ncol, 0:n])
                            b.tt("dve", dtmp[0:ncol, 0:n], SH[0:ncol, ctile, :], PL[0:ncol, ctile, 0:NS], ALU.subtract)
                            b.stt(t1[0:ncol, 0:n], dtmp[0:ncol, 0:n], mu_col, PL[0:ncol, ctile, 0:NS], ALU.mult, ALU.add)
                        if li == 0:
                            b.act(wab[ti][0:64, 0:n], t1[0:64, 0:n], AF.Tanh)
                            b.copy("act", wab[ti][64:128, 0:n], t1[64:128, 0:n])
                        elif li == 1:
                            b.act(sgb[ti][:, 0:n], t1[:, 0:n], AF.Sigmoid)
                        else:
                            b.act(sgb2[ti][:, 0:n], t1[0:32, 0:n], AF.Sigmoid)

                if stop == "P1":
                    return True
                wts_of = {}

                def item(hp, ti, tl):
                    n = tl[2]
                    rhs_t = rhs_of(tl)
                    outs = [Rt, K0, Vt]
                    cs_ = slice(hp * 128, (hp + 1) * 128)
                    if ti == 0:
                        wts_of[hp] = [load_wcol(PW + q * CW + hp * 128, 128) for q in range(3)]
                    wts = wts_of[hp]
                    for q in range(3):
                        proj(wts[q], 128, rhs_t, n, pbank[2 + q])
                    yield
                    for q in range(3):
                        ctile = q * 8 + hp
                        pp = pbank[2 + q]
                        if tl[0] == "p":
                            pbt = Pb[q]
                            b.copy("dve", pbt[:, 0:1], carry[:, ctile:ctile + 1])
                            b.evac(pbt[:, 1:n + 1], pp[:, 0:n])
                            b.copy("dve", carry[:, ctile:ctile + 1], pbt[:, n:n + 1])
                            if is_own and tl[1] == 1:
                                b.copy("dve", PL[:, ctile, 16:17], pbt[:, n:n + 1])
                        else:
                            b.evac(PL[:, ctile, 0:NS], pp[:, 0:n])
                    yield
                    for q in range(3):
                        ctile = q * 8 + hp
                        if tl[0] == "p":
                            shift_mix(Pb[q], 128, n, mu_c[:, ctile:ctile + 1], outs[q][:, 0:n])
                        else:
                            b.tt("dve", dtmp[:, 0:n], SH[:, ctile, :], PL[:, ctile, 0:NS], ALU.subtract)
                            b.stt(outs[q][:, 0:n], dtmp[:, 0:n], mu_c[:, ctile:ctile + 1], PL[:, ctile, 0:NS],
                                  ALU.mult, ALU.add)
                    pw, pa, pg = pbank[5], pbank[6], pbank[7]
                    b.mm(pw[:, 0:n], w2b[0:64, cs_], wab[ti][0:64, 0:n])
                    b.mm(pa[:, 0:n], w2b[64:128, cs_], wab[ti][64:128, 0:n])
                    b.mm(pg[:, 0:n], g2b[:, cs_], sgb[ti][:, 0:n], start=True, stop=False)
                    b.mm(pg[:, 0:n], g2b2[:, cs_], sgb2[ti][:, 0:n], start=False, stop=True)
                    b.act(sig[:, 0:n], pw[:, 0:n], AF.Sigmoid, bias=w0_c[:, hp:hp + 1])
                    b.act(alr[:, 0:n], pa[:, 0:n], AF.Sigmoid, bias=a0_c[:, hp:hp + 1])
                    b.copy("act", gate[:, 0:n], pg[:, 0:n])
                    b.ts("dve", kkn[:, 0:n], K0[:, 0:n], kk_c[:, hp:hp + 1], ALU.mult)
                    b.tt("dve", t1[:, 0:n], kkn[:, 0:n], kkn[:, 0:n], ALU.mult)
                    b.mm(pw[:, 0:n], blk[:], t1[:, 0:n])
                    b.ts("dve", t2[:, 0:n], pw[:, 0:n], 1e-24, ALU.max)
                    b.rsqrt(t2[:, 0:n], t2[:, 0:n])
                    b.tt("dve", kkn[:, 0:n], kkn[:, 0:n], t2[:, 0:n], ALU.mult)
                    b.ts("dve", t1[:, 0:n], alr[:, 0:n], -1.0, ALU.add, ka_c[:, hp:hp + 1], ALU.mult)
                    b.stt(Kt[:, 0:n], t1[:, 0:n], 1.0, K0[:, 0:n], ALU.add, ALU.mult)
                    b.stt(t1[:, 0:n], Rt[:, 0:n], rk_c[:, hp:hp + 1], Kt[:, 0:n], ALU.mult, ALU.mult)
                    b.mm(pa[:, 0:n], blk[:], t1[:, 0:n])
                    yield
                    b.tt("dve", bonus[:, 0:n], pa[:, 0:n], Vt[:, 0:n], ALU.mult)
                    b.tt("dve", t2[:, 0:n], kkn[:, 0:n], alr[:, 0:n], ALU.mult)
                    if tl[0] == "s":
                        def stash(nm, ap_):
                            b.copy("dve", XA[nm][:].rearrange("p (b h) -> p b h", h=8)[:, :, hp], ap_)
                        b.act(t1[:, 0:n], sig[:, 0:n], AF.Exp, scale=-DEC_K)
                        stash("w", t1[:, 0:n])
                        stash("r", Rt[:, 0:n])
                        stash("k", Kt[:, 0:n])
                        stash("v", Vt[:, 0:n])
                        b.ts("dve", t1[:, 0:n], kkn[:, 0:n], -1.0, ALU.mult)
                        stash("a", t1[:, 0:n])
                        stash("b", t2[:, 0:n])
                        b.copy("dve", bonus_s[:, hp, :], bonus[:, 0:n])
                        b.copy("dve", gate_s[:, hp, :], gate[:, 0:n])
                        return
                    nch = n // CH
                    S.op("dve", lambda e, n=n: e.tensor_tensor_scan(out=csig[:, 0:n], data0=onesf[:, 0:n], data1=sig[:, 0:n],
                                                                    initial=0.0, op0=ALU.mult, op1=ALU.add),
                         ["onesf", "sig"], ["csig"])
                    b.memset("dve", gb[:, 0:1], 0.0)
                    c3 = csig[:, 0:n].rearrange("p (c t) -> p c t", t=CH)
                    if nch > 1:
                        b.copy("dve", gb[:, 1:nch], c3[:, 0:nch - 1, CH - 1])
                    b.tt("dve", c3, c3, gb[:, 0:nch].unsqueeze(2).to_broadcast([128, nch, CH]), ALU.subtract)
                    b.copy("dve", gend[:, 0:nch], c3[:, :, CH - 1])
                    b.act(ECt[:, 0:nch], gend[:, 0:nch], AF.Exp, scale=-DEC_K)
                    b.act(ex[:, 0:n], csig[:, 0:n], AF.Exp, scale=-DEC_K)
                    b.tt("dve", rtb[:, 0:n], Rt[:, 0:n], ex[:, 0:n], ALU.mult)
                    b.act(ex[:, 0:n], csig[:, 0:n], AF.Exp, scale=DEC_K)
                    b.tt("dve", ktb[:, 0:n], Kt[:, 0:n], ex[:, 0:n], ALU.mult)
                    b.tt("dve", btb[:, 0:n], t2[:, 0:n], ex[:, 0:n], ALU.mult)
                    b.tt("dve", t1[:, 0:n], csig[:, 0:n], sig[:, 0:n], ALU.subtract)
                    b.act(ex[:, 0:n], t1[:, 0:n], AF.Exp, scale=-DEC_K)
                    b.stt(atb[:, 0:n], kkn[:, 0:n], -1.0, ex[:, 0:n], ALU.mult, ALU.mult)
                    t13 = t1[:, 0:n].rearrange("p (c t) -> p c t", t=CH)
                    b.tt("dve", t13, gend[:, 0:nch].unsqueeze(2).to_broadcast([128, nch, CH]), c3, ALU.subtract)
                    b.act(ex[:, 0:n], t1[:, 0:n], AF.Exp, scale=-DEC_K)
                    b.tt("dve", Khb[:, 0:n], Kt[:, 0:n], ex[:, 0:n], ALU.mult)
                    b.tt("dve", Bhb[:, 0:n], t2[:, 0:n], ex[:, 0:n], ALU.mult)
                    b.copy("act", Vb[:, 0:n], Vt[:, 0:n])
                    yield

                    def stageA(sl, c):
                        cs = slice(c * CH, (c + 1) * CH)
                        AMs, LVs, TMs, WTs = sl["AM"], sl["LV"], sl["TM"], sl["WT"]
                        pX, pY2 = sl["pX"], sl["pY2"]
                        pA = [pbank[0], pbank[1]]
                        pZb = pbf(7)
                        for h in range(2):
                            ph = slice(64 * h, 64 * h + 64)
                            b.mm(pA[h][:, 0:128], btb[ph, cs], atb[ph, cs])
                            b.mm(pA[h][:, 128:256], btb[ph, cs], rtb[ph, cs])
                            b.mm(pA[h][:, 256:384], ktb[ph, cs], atb[ph, cs])
                            b.mm(pA[h][:, 384:512], ktb[ph, cs], rtb[ph, cs])
                            b.mm(pX[:, h * 128:(h + 1) * 128], atb[ph, cs], btb[ph, cs])
                            for q, srcT in enumerate((atb, Vb, Khb, Bhb)):
                                b.tr(pZb[:, h * 256 + q * 64:h * 256 + (q + 1) * 64], srcT[ph, cs], identb[ph, ph],
                                     inc=(q == 3))
                        for h in range(2):
                            b.tt("dve", AMs[h][:], pA[h][:], mask4[:], ALU.mult)
                        b.tt("dve", LVs[0][:, :, 128:256], pX[:, 0:256].rearrange("p (h j) -> p h j", h=2),
                             maskl[:].rearrange("p (h j) -> p h j", h=2), ALU.mult)
                        for h in range(2):
                            b.copy("act", TMs[:, h, :, 64 * h:64 * h + 64],
                                   pZb[:, h * 256:(h + 1) * 256].rearrange("p (q d) -> p q d", q=4))
                        yield
                        for h in range(2):
                            b.copy("act", LVs[0][:, h, 256:384], AMs[h][:, 0:128])
                            b.copy("dve", LVs[0][:, h, 64 * h:64 * h + 64], TMs[:, h, 0, 64 * h:64 * h + 64])
                            b.mm(pY2[:, h * 64:(h + 1) * 64], AMs[h][:, 256:384], TMs[:, h, 1, 64 * h:64 * h + 64])
                        yield
                        for h in range(2):
                            o = 64 * (1 - h)
                            b.copy("dve", LVs[0][:, h, o:o + 64], pY2[:, h * 64:(h + 1) * 64])
                        yield
                        cur = 0
                        for lvl in range(7):
                            nxt = 1 - cur
                            for h in range(2):
                                b.mm(pX[:, h * 128:(h + 1) * 128], LVs[cur][:, h, 256:384], LVs[cur][:, h, 0:128])
                            if lvl < 6:
                                for h in range(2):
                                    b.mm(pY2[:, h * 256:h * 256 + 128], LVs[cur][:, h, 256:384], LVs[cur][:, h, 128:256])
                                    b.mm(pY2[:, h * 256 + 128:h * 256 + 256], LVs[cur][:, h, 128:256], LVs[cur][:, h, 256:384])
                            yield
                            b.tt("dve", LVs[nxt][:, :, 0:128], pX[:, 0:256].rearrange("p (h j) -> p h j", h=2),
                                 LVs[cur][:, :, 0:128], ALU.add)
                            if lvl < 6:
                                b.copy("act", LVs[nxt][:, :, 128:384], pY2[:, 0:512].rearrange("p (h j) -> p h j", h=2))
                            yield
                            cur = nxt
                        sl["Zf"] = LVs[cur]
                        for h in range(2):
                            b.tr(pZb[:, h * 128:(h + 1) * 128], LVs[cur][:, h, 0:128], identb[:])
                        for h in range(2):
                            ph = slice(64 * h, 64 * h + 64)
                            b.copy("act", WTs[ph, :], pZb[ph, h * 128:(h + 1) * 128])
                        yield

                    def stageB(sl, c):
                        cs = slice(c * CH, (c + 1) * CH)
                        AMs, TMs, WTs, Ubs, Zf = sl["AM"], sl["TM"], sl["WT"], sl["Ub"], sl["Zf"]
                        pY2 = sl["pY2"]
                        pA = [pbank[0], pbank[1]]
                        for h in range(2):
                            ph = slice(64 * h, 64 * h + 64)
                            b.mm(pA[h][:, 0:64], WTs[ph, :], STb[hp][ph, 64 * h:64 * h + 64])
                        for h in range(2):
                            o = 64 * (1 - h)
                            b.tt("dve", Ubs[:, h, 64 * h:64 * h + 64], pA[h][:, 0:64], Zf[:, h, o:o + 64], ALU.add)
                        yield
                        if is_own:
                            b.mm(pY2[:, 0:128], STb[hp][:], rtb[:, cs], start=True, stop=False)
                            for h in range(2):
                                b.mm(pY2[:, 0:128], Ubs[:, h, :], AMs[h][:, 128:256], start=False, stop=False)
                                b.mm(pY2[:, 0:128], TMs[:, h, 1, :], AMs[h][:, 384:512], start=False, stop=(h == 1))
                        for h in range(2):
                            hs = slice(64 * h, 64 * h + 64)
                            b.mm(pY2[:, 128:192], TMs[:, h, 3, :], Ubs[:, h, hs], start=(h == 0), stop=False)
                            b.mm(pY2[:, 128:192], TMs[:, h, 2, :], TMs[:, h, 1, hs], start=False, stop=(h == 1))
                        yield
                        if is_own:
                            b.copy("act", Yt[:, cs], pY2[:, 0:128])
                        b.ts("dve", STt[hp][:], STt[hp][:], ECt[:, c:c + 1], ALU.mult)
                        b.copy("act", ex[:, 0:64], pY2[:, 128:192])
                        b.tt("dve", STt[hp][:], ex[:, 0:64], STt[hp][:], ALU.add)
                        b.copy("act", STb[hp][0:64, 0:64], STt[hp][0:64, :])
                        b.copy("act", STb[hp][64:128, 64:128], STt[hp][64:128, :])

                    def chunk_gen(c):
                        sl = slots[c % 2]
                        for _ in stageA(sl, c):
                            yield "A"
                        yield "B?"
                        for _ in stageB(sl, c):
                            yield "B"

                    cur_g = chunk_gen(0)
                    nxt_g = chunk_gen(1) if nch > 1 else None
                    nxt_c = 1
                    nxt_waiting = False
                    while cur_g is not None:
                        try:
                            next(cur_g)
                            cur_alive = True
                        except StopIteration:
                            cur_alive = False
                        if not cur_alive:
                            cur_g, nxt_waiting = nxt_g, False
                            nxt_c += 1
                            nxt_g = chunk_gen(nxt_c) if (cur_g is not None and nxt_c < nch) else None
                            continue
                        if nxt_g is not None and not nxt_waiting:
                            if next(nxt_g) == "B?":
                                nxt_waiting = True
                    if is_own:
                        post(hp, Yt[:, 0:n], bonus[:, 0:n], gate[:, 0:n], mixT[tl[1]][:, 8 + hp, 0:n], n)

                its = [(hp, ti, tl) for hp in range(8) for ti, tl in enumerate(tiles)]
                gens = [item(*it) for it in its]

                def step(g_):
                    try:
                        next(g_)
                    except StopIteration:
                        pass

                step(gens[0])
                step(gens[0])
                for i in range(len(its)):
                    step(gens[i])
                    if i + 1 < len(its):
                        step(gens[i + 1])
                    step(gens[i])
                    if i + 1 < len(its):
                        step(gens[i + 1])
                    for _ in gens[i]:
                        pass

            def post(hp, Y, bon, gt, out_ap, n):
                pm_, pv_ = pbank[3], pbank[4]
                b.mm(pm_[:, 0:n], blk[:], Y)
                b.ts("dve", t1[:, 0:n], pm_[:, 0:n], -1.0 / 64.0, ALU.mult)
                b.tt("dve", t1[:, 0:n], t1[:, 0:n], Y, ALU.add)
                b.tt("dve", t2[:, 0:n], t1[:, 0:n], t1[:, 0:n], ALU.mult)
                b.mm(pv_[:, 0:n], blk[:], t2[:, 0:n])
                b.ts("dve", t2[:, 0:n], pv_[:, 0:n], 1.0 / 64.0, ALU.mult, 64e-5, ALU.add)
                b.rsqrt(t2[:, 0:n], t2[:, 0:n])
                b.tt("dve", t1[:, 0:n], t1[:, 0:n], t2[:, 0:n], ALU.mult)
                b.ts("dve", t1[:, 0:n], t1[:, 0:n], gnw_c[:, hp:hp + 1], ALU.mult, gnb_c[:, hp:hp + 1], ALU.add)
                b.tt("dve", t1[:, 0:n], t1[:, 0:n], bon, ALU.add)
                b.tt("dve", out_ap, t1[:, 0:n], gt, ALU.mult)

            with ExitStack() as tsc:
                shtm = tsc.enter_context(nc.sbuf_tensor("shtm", [NS, SHW], F32))
                b.dma(shtm[:], st_shift[:, :])
                for ct in range(27):
                    ncl = min(128, SHW - ct * 128)
                    pp = pbank[2 + ct % 2]
                    b.tr(pp[0:ncl, 0:NS], shtm[:, ct * 128:ct * 128 + ncl], ident[0:NS, 0:NS])
                    b.evac(SH[0:ncl, ct, :], pp[0:ncl, 0:NS])
                S.barrier()

            if stop == "B":
                return done()
            if run_pass(False):
                return done()
            if stop == "C":
                return done()
            for ct in range(8):
                wt = load_wcol(ct * 128, 128)
                pp = pbank[2 + ct % 2]
                proj(wt, 128, hT[1], 16, pp, off=496)
                b.evac(halo[:, ct, :], pp[:, 0:16])
            if stop == "D":
                return done()
            if run_pass(True):
                return done()
            if stop == "E":
                dump("PL", PL[:].rearrange("p c t -> p (c t)"), [128, 27 * 17])
                dump("ST0", STt[0][:], [128, 64])
                dump("ST3", STt[3][:], [128, 64])
                cpm = sbw("cpm", [128, 1024])
                b.copy("dve", cpm[:, 0:512], mixT[0][:, 8, :])
                b.copy("dve", cpm[:, 512:1024], mixT[0][:, 11, :])
                dump("mix", cpm[:], [128, 1024])
                return done()
            wfin = sbw("wfin", [64, 16, 64])
            for hp in range(8):
                pp = pbank[2 + hp % 2]
                b.tr(pp[0:64, 0:128], STt[hp][:], ident[:])
                b.evac(wfin[:, 2 * hp:2 * hp + 2, :].rearrange("p h k -> p (h k)"), pp[0:64, 0:128])
            b.dma(wkv_fin.rearrange("h v k -> v h k"), wfin[:])

            if stop == "F":
                dump("PL", PL[:].rearrange("p c t -> p (c t)"), [128, 27 * 17])
                return done()
            post_tmp = (t1, t2)
            with ExitStack() as ssc:
                def sbs(name, shape, dt=F32):
                    return ssc.enter_context(nc.sbuf_tensor(name, list(shape), dt))
                vecs = {}
                for nm in ("r", "k", "v", "a", "b", "w"):
                    pp = pbank[2 + (len(vecs) % 2)]
                    b.tr(pp[:, 0:128], XA[nm][:], ident[:])
                    vt = sbs(f"sv_{nm}", [128, 128])
                    b.evac(vt[:], pp[:, 0:128])
                    vecs[nm] = vt
                Sa = sbs("Sa", [128, 128])
                ysv = sbs("ysv", [128, 128])
                wkv_in = st_wkv.rearrange("b (hh hl) v k -> (b hh) hl (v k)", hl=2)
                wkv_o = nwkv_s.rearrange("b (hh hl) v k -> (b hh) hl (v k)", hl=2)
                VQ = 8
                Sst = sbs("Sst", [128, VQ * 64])
                Tst = sbs("Tst", [128, VQ * 64])
                S3 = Sst[:].rearrange("p (v k) -> p v k", v=VQ)
                T3 = Tst[:].rearrange("p (v k) -> p v k", v=VQ)
                for hl in range(2):
                    for vq in range(64 // VQ):
                        v0 = hl * 64 + vq * VQ

                        def kbc(vt):
                            return vt[:, hl * 64:(hl + 1) * 64].unsqueeze(1).to_broadcast([128, VQ, 64])

                        def vbc(vt):
                            return vt[:, v0:v0 + VQ].unsqueeze(2).to_broadcast([128, VQ, 64])
                        b.dma(Sst[:], wkv_in[:, hl, vq * VQ * 64:(vq + 1) * VQ * 64])
                        b.tt("dve", T3, S3, kbc(vecs["a"]), ALU.mult)
                        b.red("dve", Sa[:, v0:v0 + VQ], T3)
                        b.tt("dve", S3, S3, kbc(vecs["w"]), ALU.mult)
                        b.tt("dve", T3, vbc(Sa), kbc(vecs["b"]), ALU.mult)
                        b.tt("dve", S3, S3, T3, ALU.add)
                        b.tt("dve", T3, vbc(vecs["v"]), kbc(vecs["k"]), ALU.mult)
                        b.tt("dve", S3, S3, T3, ALU.add)
                        b.dma(wkv_o[:, hl, vq * VQ * 64:(vq + 1) * VQ * 64], Sst[:])
                        b.tt("dve", T3, S3, kbc(vecs["r"]), ALU.mult)
                        b.red("dve", ysv[:, v0:v0 + VQ], T3)
                pp = pbank[2]
                b.tr(pp[:, 0:128], ysv[:], ident[:])
                b.evac(Ysf[:], pp[:, 0:128])
                Yc = sbs("Yc", [128, NS])
                for hp in range(8):
                    b.copy("dve", Yc[:], Ysf[:].rearrange("p (b h) -> p b h", h=8)[:, :, hp])
                    post(hp, Yc[:], bonus_s[:, hp, :], gate_s[:, hp, :], mixTs[:, 8 + hp, :], NS)
                S.barrier()
            hpstack.close()
            if stop == "G":
                dump("PL", PL[:].rearrange("p c t -> p (c t)"), [128, 27 * 17])
                return done()

            with ExitStack() as psc:
                def sbp(name, shape, dt=F32):
                    return psc.enter_context(nc.sbuf_tensor(name, list(shape), dt))
                L = 16 + TOWN
                posb = sbp("posb", [128, TOWN])
                b.dma(posb[:], pos.to_broadcast([128, TOWN]))
                inv = sbp("inv", [128, 4, TOWN])
                for g, W in enumerate((2, 4, 8, 16)):
                    b.ts("dve", inv[:, g, :], posb[:], 1.0, ALU.add, float(W), ALU.min)
                    S.op("dve", lambda e, g=g: e.reciprocal(out=inv[:, g, :], in_=inv[:, g, :]), ["inv"], ["inv"])
                Ubuf = sbp("Ubuf", [128, L])
                s_a = sbp("s_a", [128, L])
                s_b = sbp("s_b", [128, L])
                b.memset("pool", s_a[:], 0.0)
                b.memset("pool", s_b[:], 0.0)
                s_tmp = sbp("s_tmp", [128, TOWN])
                pooledT = [sbp(f"pooled{i}", [128, TOWN], BF16) for i in range(2)]
                pooledS = [sbp(f"pooledS{i}", [128, NS], BF16) for i in range(2)]
                stp = [sbp(f"stp{i}", [120, PW]) for i in range(2)]
                b.dma(stp[0][:], st_pool[0:120, :])
                b.dma(stp[1][:], st_pool[120:240, :])
                UbS = sbp("UbS", [128, NS, 16])
                swS = sbp("swS", [128, NS])
                for ct in range(8):
                    g = ct // 2
                    W = 2 ** (g + 1)
                    wt = load_wcol(ct * 128, 128)
                    b.copy("dve", Ubuf[:, 0:16], halo[:, ct, :])
                    for ti in range(2):
                        pp = pbank[2 + ti]
                        proj(wt, 128, hT[ti], 512, pp)
                        b.evac(Ubuf[:, 16 + ti * 512:16 + (ti + 1) * 512], pp[:, 0:512])
                    pp = pbank[4]
                    proj(wt, 128, hTs, NS, pp)
                    b.evac(UL[:, ct, 0:NS], pp[:, 0:NS])
                    b.copy("dve", UL[:, ct, 16:31], Ubuf[:, L - 15:L])
                    cur, sh, bi = Ubuf, 1, 0
                    bufs = [s_a, s_b]
                    while sh < W:
                        nxt = bufs[bi]
                        bi ^= 1
                        b.tt("dve", nxt[:, sh:L], cur[:, sh:L], cur[:, 0:L - sh], ALU.add)
                        cur = nxt
                        sh *= 2
                    b.tt("dve", s_tmp[:], cur[:, 16:L], inv[:, g, :], ALU.mult)
                    b.tt("dve", pooledT[ct % 2][:], s_tmp[:], Ubuf[:, 16:L], ALU.subtract)
                    for hf in range(2):
                        pp = pbank[5 + hf]
                        b.tr(pp[:, 0:120], stp[hf][:, ct * 128:(ct + 1) * 128], ident[0:120, 0:120])
                        b.evac(UbS[:, 8 * hf:8 * hf + 8, 0:15], pp[:, 0:120].rearrange("p (b r) -> p b r", r=15))
                    b.copy("dve", UbS[:, :, 15], UL[:, ct, 0:NS])
                    b.red("dve", swS[:], UbS[:, :, 16 - W:16])
                    b.stt(pooledS[ct % 2][:], swS[:], 1.0 / W, UL[:, ct, 0:NS], ALU.mult, ALU.subtract)
                    if ct % 2 == 1:
                        for dt_ in range(2):
                            mt = 2 * g + dt_
                            for ti in range(3):
                                n = 512 if ti < 2 else NS
                                pp = pbank[2 + ti]
                                for cc in range(2):
                                    rhs = pooledT[cc][:, ti * 512:(ti + 1) * 512] if ti < 2 else pooledS[cc][:]
                                    b.mm(pp[:, 0:n], wpb[:, 2 * g + cc, dt_ * 128:(dt_ + 1) * 128], rhs,
                                         start=(cc == 0), stop=(cc == 1))
                                dst = mixT[ti][:, mt, :] if ti < 2 else mixTs[:, mt, :]
                                b.act(dst, pp[:, 0:n], AF.Copy, scale=pscale[:, mt:mt + 1])
                if stop == "DBG":
                    dump("PL", PL[:].rearrange("p c t -> p (c t)"), [128, 27 * 17])
                    dump("SH", SH[:].rearrange("p c t -> p (c t)"), [128, 27 * NS])
                pltm = sbp("pltm", [17, SHW])
                for ct in range(27):
                    ncl = min(128, SHW - ct * 128)
                    pp = pbank[5 + ct % 2]
                    b.tr(pp[0:17, 0:ncl], PL[0:ncl, ct, :], ident[0:ncl, 0:ncl])
                    b.evac(pltm[:, ct * 128:ct * 128 + ncl], pp[0:17, 0:ncl])
                b.dma(nshift_s[:, :], pltm[0:16, :])
                b.dma(shift_last[:, :], pltm[16:17, :])
                ultm = sbp("ultm", [31, PW])
                for ct in range(8):
                    pp = pbank[5 + ct % 2]
                    b.tr(pp[0:31, 0:128], UL[:, ct, :], ident[:])
                    b.evac(ultm[:, ct * 128:(ct + 1) * 128], pp[0:31, 0:128])
                b.dma(npool_s[:, 14, :], ultm[0:16, :])
                b.dma(pool_last[:, :], ultm[16:31, :])
                b.dma(npool_s[:, 0:14, :], st_pool3[:, 1:15, :])
                S.barrier()

        if stop == "H":
            return done()
        x2 = [sb(f"x2_{i}", [128, D]) for i in range(9)]
        tokM = [128] * 8 + [NS]

        def at_slice(tk, cc):
            if tk < 8:
                return mixT[tk // 4][:, cc, (tk % 4) * 128:(tk % 4 + 1) * 128]
            return mixTs[:, cc, :]

        with ExitStack() as sc2:
            wob = sc2.enter_context(nc.sbuf_tensor("wob", [128, 16, D], BF16))
            xr = sc2.enter_context(nc.sbuf_tensor("xr", [128, D], F32))
            for cc in range(16):
                b.dma(wob[:, cc, :], w_out[cc * 128:(cc + 1) * 128, :], eng="pool")
            for tk in range(9):
                M = tokM[tk]
                b.dma(xr[0:M, :], xown[tk * 128:(tk + 1) * 128, :] if tk < 8 else xs[:, :])
                for db in range(4):
                    pp = pbank[db]
                    for cc in range(16):
                        b.mm(pp[0:M, :], at_slice(tk, cc), wob[:, cc, db * 512:(db + 1) * 512],
                             start=(cc == 0), stop=(cc == 15))
                    b.tt("dve", x2[tk][0:M, db * 512:(db + 1) * 512], pp[0:M, :], xr[0:M, db * 512:(db + 1) * 512], ALU.add)
            S.barrier()

        if stop == "I":
            return done()
        with ExitStack() as sc3:
            xsq3 = None
            xbf3 = sc3.enter_context(nc.sbuf_tensor("xbf3", [128, D], BF16))
            ssq3 = sc3.enter_context(nc.sbuf_tensor("ssq3", [128, 2], F32))
            rstd3 = sc3.enter_context(nc.sbuf_tensor("rstd3", [128, 2], F32))
            xbf3b = sc3.enter_context(nc.sbuf_tensor("xbf3b", [128, D], BF16))
            nb3 = (None, xsq3, [xbf3, xbf3b], ssq3, rstd3)
            for tk in range(9):
                M = tokM[tk]
                if tk < 8:
                    norm_to_T_g(nb3, x2[tk][0:M, :], M, g_ffn, mixT[tk // 4], (tk % 4) * 128, tk % 2, src_is_sbuf=True)
                else:
                    norm_to_T_g(nb3, x2[tk][0:M, :], M, g_ffn, mixTs, 0, tk % 2, src_is_sbuf=True)
            S.barrier()

        if stop == "J":
            return done()
        with ExitStack() as sc4:
            def sb4(name, shape, dt=F32):
                return sc4.enter_context(nc.sbuf_tensor(name, list(shape), dt))
            GF = 11
            uT = sb4("uT", [128, GF, TOWN + NS], BF16)
            wg = [sb4(f"wg{i}", [128, 16, 128], BF16) for i in range(3)]
            wu = [sb4(f"wu{i}", [128, 16, 128], BF16) for i in range(3)]
            wdn = [sb4(f"wdn{i}", [128, GF, 512], BF16) for i in range(2)]
            sgt = sb4("sgt", [128, 512])
            wg_v = w_gate.rearrange("(kc p) n -> p kc n", p=128)
            wu_v = w_up.rearrange("(kc p) n -> p kc n", p=128)
            wd_v = w_down.rearrange("(ft p) d -> p ft d", p=128)
            ftiles = [(mixT[0], 512, 0), (mixT[1], 512, 512), (mixTs, NS, 1024)]
            pctr = 0
            dctr = 0
            for grp in range(NFT // GF):
                for fl in range(GF):
                    ft = grp * GF + fl
                    wgt, wut = wg[ft % 3], wu[ft % 3]
                    b.dma(wgt[:], wg_v[:, :, ft * 128:(ft + 1) * 128], eng="pool")
                    b.dma(wut[:], wu_v[:, :, ft * 128:(ft + 1) * 128], eng="pool")
                    for (rt_, n, off) in ftiles:
                        pg = pbank[(2 * pctr) % 8]
                        pu = pbank[(2 * pctr + 1) % 8]
                        pctr += 1
                        for dk in range(16):
                            b.mm(pg[:, 0:n], wgt[:, dk, :], rt_[:, dk, 0:n], start=(dk == 0), stop=(dk == 15))
                        for dk in range(16):
                            b.mm(pu[:, 0:n], wut[:, dk, :], rt_[:, dk, 0:n], start=(dk == 0), stop=(dk == 15))
                        b.act(sgt[:, 0:n], pg[:, 0:n], AF.Silu)
                        b.tt("dve", uT[:, fl, off:off + n], sgt[:, 0:n], pu[:, 0:n], ALU.mult)
                for db in range(4):
                    wdt = wdn[dctr % 2]
                    dctr += 1
                    b.dma(wdt[:], wd_v[:, grp * GF:(grp + 1) * GF, db * 512:(db + 1) * 512], eng="pool")
                    for tk in range(9):
                        M = tokM[tk]
                        pp = pbank[pctr % 8]
                        pctr += 1
                        for fl in range(GF):
                            b.mm(pp[0:M, :], uT[:, fl, tk * 128:tk * 128 + M], wdt[:, fl, :], start=(fl == 0), stop=(fl == GF - 1))
                        b.tt("dve", x2[tk][0:M, db * 512:(db + 1) * 512], pp[0:M, :], x2[tk][0:M, db * 512:(db + 1) * 512], ALU.add)
            S.barrier()

        if stop == "K":
            return done()
        with ExitStack() as sc5:
            gfin = sc5.enter_context(nc.sbuf_tensor("gfin", [128, D], F32))
            xsq5 = sc5.enter_context(nc.sbuf_tensor("xsq5", [128, D], F32))
            ssq5 = sc5.enter_context(nc.sbuf_tensor("ssq5", [128, 1], F32))
            rstd5 = sc5.enter_context(nc.sbuf_tensor("rstd5", [128, 1], F32))
            yout = [sc5.enter_context(nc.sbuf_tensor(f"yout{i}", [128, D], F32)) for i in range(2)]
            b.dma(gfin[:], norm_final.to_broadcast([128, D]))
            for tk in range(9):
                M = tokM[tk]
                xi = x2[tk][0:M, :]
                b.stt(xsq5[0:M, :], xi, 1.0, xi, ALU.mult, ALU.mult, accum=ssq5[0:M, 0:1])
                b.ts("dve", rstd5[0:M, :], ssq5[0:M, :], 1.0 / D, ALU.mult, eps_rms, ALU.add)
                b.rsqrt(rstd5[0:M, :], rstd5[0:M, :])
                yo = yout[tk % 2]
                b.stt(yo[0:M, :], xi, rstd5[0:M, 0:1], gfin[0:M, :], ALU.mult, ALU.mult)
                b.dma(y_own[tk * 128:(tk + 1) * 128, :] if tk < 8 else y_s[:, :], yo[0:M, :])

        return done()


_NC_CACHE = {}


def kernel(**inp):
    f = lambda k: np.ascontiguousarray(np.asarray(inp[k], dtype=np.float32))
    xp = f("x_prompt")
    xsmp = f("x_sample")
    sp, ss, sw = f("state_pool"), f("state_shift"), f("state_wkv")
    shared = {
        "norm_mix": f("norm_mix").reshape(D), "w_in": f("w_in").reshape(D, PROJ),
        "w_pool": f("w_pool").reshape(4, 256, 256), "pool_scale": f("pool_scale").reshape(PW),
        "mu_shift": f("mu_shift").reshape(SHW), "w0": f("w0").reshape(CW), "w2": f("w2").reshape(64, CW),
        "a0": f("a0").reshape(CW), "a2": f("a2").reshape(64, CW), "g2": f("g2").reshape(160, CW),
        "k_k": f("k_k").reshape(CW), "k_a": f("k_a").reshape(CW), "r_k": f("r_k").reshape(CW),
        "gn_w": f("gn_w").reshape(CW), "gn_b": f("gn_b").reshape(CW), "w_out": f("w_out").reshape(D, D),
        "norm_ffn": f("norm_ffn").reshape(D), "w_gate": f("w_gate").reshape(D, DFF),
        "w_up": f("w_up").reshape(D, DFF), "w_down": f("w_down").reshape(DFF, D),
        "norm_final": f("norm_final").reshape(1, D),
    }
    in_maps = []
    for c in range(NCORE):
        bq, half = c // 2, c % 2
        m = dict(shared)
        m["xown"] = np.ascontiguousarray(xp[bq, half * TOWN:(half + 1) * TOWN])
        m["xprev"] = np.ascontiguousarray(xp[bq, 0:TOWN]) if half == 1 else np.zeros((TOWN, D), np.float32)
        m["xs"] = np.ascontiguousarray(xsmp[c * NS:(c + 1) * NS, 0])
        m["st_pool"] = np.ascontiguousarray(sp[0, c * NS:(c + 1) * NS]).reshape(NS * 15, PW)
        m["st_shift"] = np.ascontiguousarray(ss[0, c * NS:(c + 1) * NS])
        m["st_wkv"] = np.ascontiguousarray(sw[0, c * NS:(c + 1) * NS])
        m["pos"] = (half * TOWN + np.arange(TOWN, dtype=np.float32)).reshape(1, TOWN)
        in_maps.append(m)
    if "nc" not in _NC_CACHE:
        _NC_CACHE["nc"] = build_program()
    res = run_bass_kernel_spmd(_NC_CACHE["nc"], in_maps, core_ids=list(range(NCORE)))
    R = res.results
    y_prompt = np.zeros((4, 2048, D), np.float32)
    for c in range(NCORE):
        y_prompt[c // 2, (c % 2) * TOWN:(c % 2 + 1) * TOWN] = R[c]["y_own"]
    y_sample = np.concatenate([R[c]["y_s"] for c in range(NCORE)], 0).reshape(128, 1, D)
    npp = np.stack([R[2 * q + 1]["pool_last"] for q in range(4)], 0)[None]
    nsp = np.stack([R[2 * q + 1]["shift_last"].reshape(SHW) for q in range(4)], 0)[None]
    nwp = np.stack([R[2 * q + 1]["wkv_fin"] for q in range(4)], 0)[None]
    nps = np.concatenate([R[c]["npool_s"] for c in range(NCORE)], 0)[None]
    nss = np.concatenate([R[c]["nshift_s"] for c in range(NCORE)], 0)[None]
    nws = np.concatenate([R[c]["nwkv_s"] for c in range(NCORE)], 0)[None]
    return (y_prompt, y_sample, npp.astype(np.float32), nsp.astype(np.float32), nwp.astype(np.float32),
            nps.astype(np.float32), nss.astype(np.float32), nws.astype(np.float32))
```
